# Optimizing a Trainium2 kernel written in Bass

```python
import jax, jax.numpy as jnp
from jax import lax
import numpy as np

D_MODEL = 2048
BATCH = 4
SEQ = 2048
DEPTH = 2

GRID_W = 64
CTX_LEN = 256
D_FF = 5632
D_A = 1024
GROUPS_A = 8
CHUNK_A = 128
D_B = 1024
HEADS_B = 8
HEAD_DK = D_B // HEADS_B
HEAD_DV = D_B // HEADS_B
CHUNK_B = 64
N_MOD = 9
N_NORM = 6
MACARON = 0.5
EPS = 1e-6
SPLIT_SIZES = (D_A, D_A, D_B, D_B, D_B, D_B, D_B, D_MODEL, D_MODEL)
IN_COLS = 2 * D_A + 5 * D_B + 2 * D_MODEL
OFF_F = 2 * D_A + D_B
OFF_G = 2 * D_A + 4 * D_B

kernel_name = 'hybrid_gmlp_hgrn2_dit'


def _rms(x, g):
    xf = x.astype(jnp.float32)
    y = xf * lax.rsqrt(jnp.mean(xf * xf, axis=-1, keepdims=True) + EPS)
    return (y * g.astype(jnp.float32)).astype(x.dtype)


def _layernorm(x, g):
    xf = x.astype(jnp.float32)
    xc = xf - jnp.mean(xf, axis=-1, keepdims=True)
    y = xc * lax.rsqrt(jnp.mean(xc * xc, axis=-1, keepdims=True) + EPS)
    return (y * g.astype(jnp.float32)).astype(x.dtype)


def _modulate(z, g_pre, shift, scale):
    return _rms(z, g_pre) * (1 + scale) + shift


def _residual(z, y, g_post, gate, weight):
    return z + weight * gate * _rms(y, g_post)


def _swiglu(h, w_gu, w_down):
    a, b = jnp.split(h @ w_gu, 2, axis=-1)
    return (jax.nn.silu(a) * b) @ w_down


def _split_cols(p):
    idx = [int(i) for i in np.cumsum(SPLIT_SIZES)[:-1]]
    return jnp.split(p, idx, axis=-1)


def _chunk_mlp(u, v, n_chunks, g_norm, w_s, b_s):
    b, n, _ = u.shape
    v = _layernorm(v, g_norm).reshape(b, n_chunks, CHUNK_A, GROUPS_A, D_A // GROUPS_A)
    sv = jnp.einsum('gts,bcsgd->bctgd', w_s, v) + b_s.T[None, None, :, :, None]
    return u * sv.reshape(b, n, D_A)


def _heads(t):
    b, n, _ = t.shape
    return jnp.transpose(t.reshape(b, n, HEADS_B, -1), (0, 2, 1, 3)).astype(jnp.float32)


def _decay(f_logit, lb):
    f = lb + (1.0 - lb) * jax.nn.sigmoid(f_logit.astype(jnp.float32))
    f = jnp.maximum(f, 1e-30)
    return _heads(1.0 - f), _heads(jnp.log(f))


def _hgrn_scan(q, k, v, logf, s0):
    b, h, n, _ = q.shape
    nc = n // CHUNK_B

    def to_chunks(t):
        return jnp.moveaxis(t.reshape(b, h, nc, CHUNK_B, t.shape[-1]), 2, 0)

    tri = jnp.tril(jnp.ones((CHUNK_B, CHUNK_B), dtype=bool))[:, :, None]

    def step(S, inp):
        qc, kc, vc, gc = inp
        cum = jnp.cumsum(gc, axis=2)
        inter = jnp.einsum('bhtd,bhde->bhte', qc * jnp.exp(cum), S)
        diff = cum[:, :, :, None, :] - cum[:, :, None, :, :]
        dec = jnp.where(tri, jnp.exp(jnp.where(tri, diff, 0.0)), 0.0)
        scores = jnp.einsum('bhtd,bhsd,bhtsd->bhts', qc, kc, dec)
        intra = jnp.einsum('bhts,bhse->bhte', scores, vc)
        last = cum[:, :, -1:, :]
        S_new = jnp.exp(last[:, :, 0, :])[..., None] * S + jnp.einsum('bhsd,bhse->bhde', kc * jnp.exp(last - cum), vc)
        return S_new, inter + intra

    s_fin, o = lax.scan(step, s0, (to_chunks(q), to_chunks(k), to_chunks(v), to_chunks(logf)))
    o = jnp.moveaxis(o, 0, 2).reshape(b, h, n, -1)
    return o, s_fin


def _final_state(k, v, logf):
    cum = jnp.cumsum(logf, axis=2)
    return jnp.einsum('bhsd,bhse->bhde', k * jnp.exp(cum[:, :, -1:, :] - cum), v)


def _readout(o, g_gate, gain):
    b, h, n, dv = o.shape
    o = jnp.transpose(o, (0, 2, 1, 3))
    o = o * lax.rsqrt(jnp.mean(o * o, axis=-1, keepdims=True) + EPS)
    o = o.reshape(b, n, h * dv) * gain.astype(jnp.float32)
    return o.astype(g_gate.dtype) * jax.nn.silu(g_gate)


def _merge(y_a, y_b, ga, gb, w_up_a, w_up_b, w_out):
    m = jax.nn.sigmoid(ga) * (y_a @ w_up_a) + jax.nn.sigmoid(gb) * (y_b @ w_up_b)
    return m @ w_out


def _mixer(h, hc, n_chunks, w_in, chunk_g, w_s, b_s, lb_f, lb_b, hgrn_g, w_up_a, w_up_b, w_out, need_ctx):
    b = h.shape[0]
    flip = lambda t: jnp.flip(t, axis=2)
    u, v, q, f_f, f_b, i, g, ga, gb = _split_cols(h @ w_in)
    if need_ctx:
        uc, vc, qc, f_fc, f_bc, ic, gc, gac, gbc = _split_cols(hc @ w_in)
    else:
        f_fc, f_bc, ic = jnp.split(hc @ w_in[:, OFF_F:OFF_G], 3, axis=-1)
    k_fc, lf_c = _decay(f_fc, lb_f)
    k_bc, lbw_c = _decay(f_bc, lb_b)
    v_c = _heads(ic)
    if need_ctx:
        q_c = _heads(qc)
        s0 = jnp.zeros((b, HEADS_B, HEAD_DK, HEAD_DV), jnp.float32)
        o_cf, s_cf = _hgrn_scan(q_c, k_fc, v_c, lf_c, s0)
        o_cb, s_cb = _hgrn_scan(flip(q_c), flip(k_bc), flip(v_c), flip(lbw_c), s0)
        y_bc = _readout(o_cf + flip(o_cb), gc, hgrn_g)
        y_ac = _chunk_mlp(jax.nn.gelu(uc), jax.nn.gelu(vc), hc.shape[1] // CHUNK_A, chunk_g, w_s, b_s)
        out_c = _merge(y_ac, y_bc, gac, gbc, w_up_a, w_up_b, w_out)
    else:
        s_cf = _final_state(k_fc, v_c, lf_c)
        s_cb = _final_state(flip(k_bc), flip(v_c), flip(lbw_c))
        out_c = None
    q_l = _heads(q)
    v_l = _heads(i)
    k_f, lf = _decay(f_f, lb_f)
    k_b, lbw = _decay(f_b, lb_b)
    o_f, _ = _hgrn_scan(q_l, k_f, v_l, lf, s_cf)
    o_b, _ = _hgrn_scan(flip(q_l), flip(k_b), flip(v_l), flip(lbw), s_cb)
    y_b = _readout(o_f + flip(o_b), g, hgrn_g)
    y_a = _chunk_mlp(jax.nn.gelu(u), jax.nn.gelu(v), n_chunks, chunk_g, w_s, b_s)
    out = _merge(y_a, y_b, ga, gb, w_up_a, w_up_b, w_out)
    return out, out_c


def setup_inputs(seed: int = 0) -> dict:
    key = jax.random.key(seed)
    ks = jax.random.split(key, 20)
    D = D_MODEL

    def nrm(k, shape, fan_in):
        return jax.random.normal(k, shape, jnp.float32) * (fan_in ** -0.5)

    def near_one(k, shape, s):
        return 1.0 + s * jax.random.normal(k, shape, jnp.float32)

    return {
        'x': jax.random.normal(ks[0], (BATCH, SEQ, D), jnp.float32),
        'c': jax.random.normal(ks[1], (BATCH, D), jnp.float32),
        'ctx': jax.random.normal(ks[2], (BATCH, CTX_LEN, D), jnp.float32),
        'c_ctx': jax.random.normal(ks[3], (D,), jnp.float32),
        'w_mod': nrm(ks[4], (DEPTH, D, N_MOD * D), D),
        'b_mod': 0.01 * jax.random.normal(ks[5], (DEPTH, N_MOD * D), jnp.float32),
        'norm_g': near_one(ks[6], (DEPTH, N_NORM, D), 0.05),
        'ffn1_w_gu': nrm(ks[7], (DEPTH, D, 2 * D_FF), D),
        'ffn1_w_down': nrm(ks[8], (DEPTH, D_FF, D), D_FF),
        'ffn2_w_gu': nrm(ks[9], (DEPTH, D, 2 * D_FF), D),
        'ffn2_w_down': nrm(ks[10], (DEPTH, D_FF, D), D_FF),
        'w_in': nrm(ks[11], (DEPTH, D, IN_COLS), D),
        'chunk_norm_g': near_one(ks[12], (DEPTH, D_A), 0.05),
        'w_spatial': nrm(ks[13], (DEPTH, GROUPS_A, CHUNK_A, CHUNK_A), CHUNK_A),
        'b_spatial': near_one(ks[14], (DEPTH, GROUPS_A, CHUNK_A), 0.02),
        'lb_logits': jax.random.normal(ks[15], (DEPTH, 2, D_B), jnp.float32),
        'hgrn_norm_g': near_one(ks[16], (DEPTH, D_B), 0.05),
        'w_up_a': nrm(ks[17], (DEPTH, D_A, D), D_A),
        'w_up_b': nrm(ks[18], (DEPTH, D_B, D), D_B),
        'w_out': nrm(ks[19], (DEPTH, D, D), D),
    }


def reference(x, c, ctx, c_ctx, w_mod, b_mod, norm_g, ffn1_w_gu, ffn1_w_down, ffn2_w_gu, ffn2_w_down,
              w_in, chunk_norm_g, w_spatial, b_spatial, lb_logits, hgrn_norm_g, w_up_a, w_up_b, w_out):
    n = x.shape[1]
    rows = n // GRID_W
    n_chunks = rows // (CHUNK_A // GRID_W)
    lb_all = jnp.cumsum(jax.nn.softmax(lb_logits.astype(jnp.float32), axis=0), axis=0)
    lb_all = lb_all - lb_all[0:1]
    sc = jax.nn.silu(c)
    scc = jax.nn.silu(c_ctx)
    xc = ctx
    for l in range(DEPTH):
        last = l == DEPTH - 1
        mod = (sc @ w_mod[l] + b_mod[l]).reshape(x.shape[0], 1, N_MOD, D_MODEL)
        modc = (scc @ w_mod[l] + b_mod[l]).reshape(1, 1, N_MOD, D_MODEL)
        g = norm_g[l]
        h = _modulate(x, g[0], mod[:, :, 0], mod[:, :, 1])
        x = _residual(x, _swiglu(h, ffn1_w_gu[l], ffn1_w_down[l]), g[1], mod[:, :, 2], MACARON)
        hc = _modulate(xc, g[0], modc[:, :, 0], modc[:, :, 1])
        xc = _residual(xc, _swiglu(hc, ffn1_w_gu[l], ffn1_w_down[l]), g[1], modc[:, :, 2], MACARON)
        h = _modulate(x, g[2], mod[:, :, 3], mod[:, :, 4])
        hc = _modulate(xc, g[2], modc[:, :, 3], modc[:, :, 4])
        y, yc = _mixer(h, hc, n_chunks, w_in[l], chunk_norm_g[l], w_spatial[l], b_spatial[l],
                       lb_all[l, 0], lb_all[l, 1], hgrn_norm_g[l], w_up_a[l], w_up_b[l], w_out[l],
                       not last)
        x = _residual(x, y, g[3], mod[:, :, 5], 1.0)
        h = _modulate(x, g[4], mod[:, :, 6], mod[:, :, 7])
        x = _residual(x, _swiglu(h, ffn2_w_gu[l], ffn2_w_down[l]), g[5], mod[:, :, 8], MACARON)
        if not last:
            xc = _residual(xc, yc, g[3], modc[:, :, 5], 1.0)
            hc = _modulate(xc, g[4], modc[:, :, 6], modc[:, :, 7])
            xc = _residual(xc, _swiglu(hc, ffn2_w_gu[l], ffn2_w_down[l]), g[5], modc[:, :, 8], MACARON)
    return x
```

```python
import numpy as np
from contextlib import ExitStack
import concourse.bass as bass
import concourse.mybir as mybir
from concourse.bass_utils import run_bass_kernel_spmd

F32 = mybir.dt.float32
BF16 = mybir.dt.bfloat16
AF = mybir.ActivationFunctionType
ALU = mybir.AluOpType
AX = mybir.AxisListType

NCORES = 8
D = 2048
FC = 16
NL = 1024
NCX = 128
NT = NL + NCX
DFF = 5632
NJ = DFF // 128
DEPTH = 2
EPS = 1e-6
TBS = [(0, 384), (384, 768), (768, 1152)]
RNG = [(0, NL), (NL, NT)]
NCH = NT // 32
OFF_U, OFF_V, OFF_Q, OFF_FF, OFF_FB, OFF_I, OFF_G, OFF_GA, OFF_GB = 0, 1024, 2048, 3072, 4096, 5120, 6144, 7168, 9216
V_BMOD, V_NG, V_CG, V_HG, V_LB, V_C, V_ROWS = 0, 288, 480, 496, 512, 544, 640


class Sched:
    ENG = ("pe", "act", "dve", "pool", "sp")

    def __init__(self, nc, sems, dma_sems):
        self.nc = nc
        self.sem = dict(zip(self.ENG, sems))
        self.cnt = {e: 0 for e in self.ENG}
        self.prog = {e: [] for e in self.ENG}
        self.waited = {e: {} for e in self.ENG}
        self.dma_sems = list(dma_sems)
        self.dma_val = [0] * len(self.dma_sems)
        self.dma_rr = 0
        self.last_w = {}
        self.readers = {}
        self.n_ins = 0

    def _need(self, eng, tok, waits):
        if tok is None:
            return
        if tok[0] == "e":
            if tok[1] == eng and eng == "pe":
                return
            key = ("e", tok[1])
        else:
            key = ("d", tok[1])
        val = tok[2]
        if self.waited[eng].get(key, 0) >= val:
            return
        waits[key] = max(waits.get(key, 0), val)

    def _emit_waits(self, eng, waits):
        for key, val in waits.items():
            self.waited[eng][key] = val
            s = self.sem[key[1]] if key[0] == "e" else self.dma_sems[key[1]]
            self.prog[eng].append(("w", s, val))

    def _deps(self, eng, reads, writes, waits):
        for b in reads:
            self._need(eng, self.last_w.get(b), waits)
        for b in writes:
            self._need(eng, self.last_w.get(b), waits)
            for t in self.readers.get(b, ()):
                self._need(eng, t, waits)

    def _record(self, tok, reads, writes):
        for b in reads:
            self.readers.setdefault(b, []).append(tok)
        for b in writes:
            self.last_w[b] = tok
            self.readers[b] = []
        self.n_ins += 1

    def op(self, eng, fn, reads=(), writes=(), signal=True):
        waits = {}
        self._deps(eng, reads, writes, waits)
        self._emit_waits(eng, waits)
        if signal:
            self.cnt[eng] += 1
            tok = ("e", eng, self.cnt[eng])
            self.prog[eng].append(("i", fn, self.sem[eng], 1))
        else:
            tok = ("e", eng, self.cnt[eng] + 1)
            self.prog[eng].append(("i", fn, None, 0))
        self._record(tok, reads, writes)
        return tok

    def dma(self, eng, fn, reads=(), writes=()):
        idx = self.dma_rr
        self.dma_rr = (self.dma_rr + 1) % len(self.dma_sems)
        waits = {}
        if self.dma_val[idx] > 0:
            self._need(eng, ("d", idx, self.dma_val[idx]), waits)
        self._deps(eng, reads, writes, waits)
        self._emit_waits(eng, waits)
        self.dma_val[idx] += 16
        tok = ("d", idx, self.dma_val[idx])
        self.prog[eng].append(("i", fn, self.dma_sems[idx], 16))
        self._record(tok, reads, writes)
        return tok

    def retire(self, old_keys, new_keys):
        toks = []
        for k in old_keys:
            if self.last_w.get(k) is not None:
                toks.append(self.last_w[k])
            toks.extend(self.readers.get(k, ()))
            self.last_w.pop(k, None)
            self.readers.pop(k, None)
        best = {}
        for t in toks:
            key = (t[0], t[1])
            if key not in best or best[key][2] < t[2]:
                best[key] = t
        toks = list(best.values())
        for k in new_keys:
            self.last_w[k] = None
            self.readers[k] = list(toks) + self.readers.get(k, [])

    def final_wait(self, eng):
        waits = {}
        for e in self.ENG:
            if e != eng and self.cnt[e] > 0:
                self._need(eng, ("e", e, self.cnt[e]), waits)
        for i, v in enumerate(self.dma_val):
            if v > 0:
                self._need(eng, ("d", i, v), waits)
        self._emit_waits(eng, waits)

    def flush(self, block):
        engobj = {"pe": "tensor", "act": "scalar", "dve": "vector", "pool": "gpsimd", "sp": "sync"}

        def run(e):
            def body(engine):
                for item in self.prog[e]:
                    if item[0] == "w":
                        engine.wait_ge(item[1], item[2])
                    else:
                        ins = item[1](engine)
                        if item[2] is not None:
                            ins.then_inc(item[2], item[3])
            return body

        for e in self.ENG:
            getattr(block, engobj[e])(run(e))


class Builder:
    def __init__(self, stages):
        self.stages = stages
        self.nc = bass.Bass("TRN2", target_bir_lowering=False)

    def build(self):
        nc = self.nc
        dt = nc.dram_tensor
        self.x_lat = dt("x_lat", [NL, D], F32, kind="ExternalInput").ap()
        self.x_ctx = dt("x_ctx", [NCX, D], F32, kind="ExternalInput").ap()
        self.vecs = dt("vecs", [V_ROWS, 128], F32, kind="ExternalInput").ap()
        self.consts = dt("consts", [128, 192], F32, kind="ExternalInput").ap()
        self.flags = dt("flags", [128, 2], F32, kind="ExternalInput").ap()
        self.w_mod = dt("w_mod", [DEPTH, D, 9 * D], F32, kind="ExternalInput").ap()
        self.w_gu = [dt(f"ffn{i}_w_gu", [DEPTH, D, 2 * DFF], F32, kind="ExternalInput").ap() for i in (1, 2)]
        self.w_dn = [dt(f"ffn{i}_w_down", [DEPTH, DFF, D], F32, kind="ExternalInput").ap() for i in (1, 2)]
        self.w_in = dt("w_in", [DEPTH, D, 11264], F32, kind="ExternalInput").ap()
        self.w_sp = dt("w_spatial", [DEPTH, 8, 128, 128], F32, kind="ExternalInput").ap()
        self.b_sp = dt("b_spatial", [DEPTH, 1, 1024], F32, kind="ExternalInput").ap()
        self.w_ua = dt("w_up_a", [DEPTH, 1024, D], F32, kind="ExternalInput").ap()
        self.w_ub = dt("w_up_b", [DEPTH, 1024, D], F32, kind="ExternalInput").ap()
        self.w_out = dt("w_out", [DEPTH, D, D], F32, kind="ExternalInput").ap()
        self.out = dt("out", [NL, D], F32, kind="ExternalOutput").ap()
        self.xs = dt("xs", [FC, 128, NT], F32).ap()
        self.cc_in = [dt(f"cc_in{g}", [128, 1040], F32).ap() for g in range(4)]
        self.cc_out = [dt(f"cc_out{g}", [256, 1040], F32).ap() for g in range(4)]

        with ExitStack() as st:
            E = st.enter_context
            sb = lambda n, s, d: E(nc.sbuf_tensor(n, s, d))
            self.G = sb("G", [128, 50688], BF16)
            self.H = sb("H", [128, FC * NT], BF16)
            self.NS, self.NB = 2, 4
            self.wst = [sb(f"wst{i}", [128, 2048], F32) for i in range(self.NS)]
            self.wbf = [sb(f"wbf{i}", [128, 2048], BF16) for i in range(self.NB)]
            self.scr = [sb(f"scr{i}", [128, NT], F32) for i in range(4)]
            self.sqb = [sb(f"sqb{i}", [128, NT], BF16) for i in range(2)]
            self.rstd = sb("rstd", [128, NT], F32)
            self.cst = sb("cst", [128, 192], F32)
            self.identb = sb("identb", [128, 128], BF16)
            self.onesb = sb("onesb", [128, 128], BF16)
            self.mkb = sb("mkb", [128, 64], BF16)
            self.m32 = sb("m32", [128, NT], BF16)
            self.mpc = sb("mpc", [128, NT], BF16)
            self.vecT = sb("vecT", [128, V_ROWS], F32)
            self.modT = sb("modT", [128, 144, 2], F32)
            self.cols = sb("cols", [128, 8, FC], F32)
            self.lbc = sb("lbc", [128, 64], F32)
            self.scb = sb("scb", [128, 16, 2], BF16)
            self.flg = sb("flg", [128, 2], F32)
            self.dloc = sb("dloc", [128, 32], F32)
            self.edge = sb("edge", [128, 36], F32)
            self.dcol = sb("dcol", [128, 2, 36], F32)
            self.ps = [E(nc.psum_tensor(f"ps{i}", [128, 512], F32)) for i in range(8)]
            sems = [E(nc.semaphore(f"e{i}")) for i in range(5)]
            dsems = [E(nc.semaphore(f"d{i}")) for i in range(24)]
            self.ccsem = E(nc.semaphore("cc"))
            self.ccval = 0
            block = E(nc.Block())
            self.S = Sched(nc, sems, dsems)
            self.ucount = 0
            self.bank = 0
            self.X = self.G[:, 0:2 * FC * NT].bitcast(F32).rearrange("p (c t) -> p c t", t=NT)
            self.gT = self.G[:, 0:NJ * NT].rearrange("p (c t) -> p c t", t=NT)
            self.h = self.H[:, :].rearrange("p (c t) -> p c t", t=NT)
            self.program()
            self.S.final_wait("sp")
            self.S.flush(block)
        return nc

    def banks(self, n):
        r = [(self.bank + i) % 8 for i in range(n)]
        self.bank = (self.bank + n) % 8
        return r

    def load_unit(self, wap, k0, kc, c0, ncol=128):
        S = self.S
        u = self.ucount
        self.ucount += 1
        si, bi = u % self.NS, u % self.NB
        stg = self.wst[si][:, 0:kc * ncol].rearrange("p (k c) -> p k c", c=ncol)
        wb = self.wbf[bi][:, 0:kc * ncol].rearrange("p (k c) -> p k c", c=ncol)
        src = wap[k0 * 128:(k0 + kc) * 128, c0:c0 + ncol].rearrange("(k p) c -> p k c", p=128)
        S.dma("sp", lambda e: e.dma_start(out=stg, in_=src), writes=[("ws", si)])
        flat_s = self.wst[si][:, 0:kc * ncol]
        flat_b = self.wbf[bi][:, 0:kc * ncol]
        if u % 3 == 2:
            S.op("act", lambda e: e.activation(out=flat_b, in_=flat_s, func=AF.Copy), reads=[("ws", si)], writes=[("wb", bi)])
        else:
            S.op("pool", lambda e: e.tensor_copy(out=flat_b, in_=flat_s), reads=[("ws", si)], writes=[("wb", bi)])
        return wb, ("wb", bi)

    def project(self, units, rhs_fn, rhs_keys, evac, tbs=TBS):
        S = self.S
        bks = self.banks(len(tbs))
        ktot = sum(u[2] for u in units)
        kg = 0
        for (wap, k0, kc, c0) in units:
            wb, wkey = self.load_unit(wap, k0, kc, c0)
            for k in range(kc):
                r = rhs_fn(kg)
                for ti, (t0, t1) in enumerate(tbs):
                    last = (k == kc - 1 and ti == len(tbs) - 1)
                    o = self.ps[bks[ti]][:, 0:t1 - t0]
                    S.op("pe", (lambda o=o, l=wb[:, k, :], rr=r[:, t0:t1], st=(kg == 0), sp=(kg == ktot - 1):
                                lambda e: e.matmul(o, lhsT=l, rhs=rr, start=st, stop=sp))(),
                         reads=[wkey] + list(rhs_keys(kg)), writes=[("ps", bks[ti])], signal=last)
                kg += 1
        evac([(self.ps[bks[ti]][:, 0:t1 - t0], t0, t1, ("ps", bks[ti])) for ti, (t0, t1) in enumerate(tbs)])

    def colstats(self, src_fn, src_keys, nfc, n, out, outkey, eps=EPS):
        S = self.S
        bks = self.banks(3)
        for fc in range(nfc):
            sq = self.sqb[fc % 2]
            sk = ("sqb", fc % 2)
            S.op("act", (lambda s=src_fn(fc), sq=sq: lambda e: e.activation(out=sq[:, :], in_=s, func=AF.Square))(),
                 reads=list(src_keys(fc)), writes=[sk])
            for ti, (t0, t1) in enumerate(TBS):
                S.op("pe", (lambda o=self.ps[bks[ti]][:, 0:t1 - t0], rr=sq[:, t0:t1], st=(fc == 0), sp=(fc == nfc - 1):
                            lambda e: e.matmul(o, lhsT=self.onesb[:, :], rhs=rr, start=st, stop=sp))(),
                     reads=[sk, "onesb"], writes=[("ps", bks[ti])], signal=(ti == 2))
        for ti, (t0, t1) in enumerate(TBS):
            S.op("act", (lambda o=out[:, t0:t1], i=self.ps[bks[ti]][:, 0:t1 - t0]:
                         lambda e: e.activation(out=o, in_=i, func=AF.Sqrt, scale=1.0 / n, bias=eps))(),
                 reads=[("ps", bks[ti])], writes=[outkey])
        S.op("dve", lambda e: e.reciprocal(out=out[:, :], in_=out[:, :]), reads=[outkey], writes=[outkey])

    def mod_cols(self, l, kA, kB, gi, dst, mode):
        S = self.S
        g = self.vecT[:, V_NG + l * 96 + gi * 16: V_NG + l * 96 + gi * 16 + 16]
        for j in range(2):
            m = self.modT[:, kA * 16:(kA + 1) * 16, j]
            o = self.cols[:, dst + j, :]
            if mode == "pre":
                S.op("dve", (lambda o=o, m=m: lambda e: e.scalar_tensor_tensor(out=o, in0=m, scalar=1.0, in1=g, op0=ALU.add, op1=ALU.mult))(),
                     reads=["modT", "vecT"], writes=["cols"])
            else:
                S.op("dve", (lambda o=o, m=m: lambda e: e.scalar_tensor_tensor(out=o, in0=m, scalar=float(kB), in1=g, op0=ALU.mult, op1=ALU.mult))(),
                     reads=["modT", "vecT"], writes=["cols"])

    def prologue(self, l, k_shift, k_scale, gi):
        S = self.S
        self.colstats(lambda fc: self.X[:, fc, :], lambda fc: [("x", fc)], FC, D, self.rstd, "rstd")
        self.mod_cols(l, k_scale, None, gi, 0, "pre")
        for fc in range(FC):
            tmp = self.scr[fc % 2]
            tk = ("scr", fc % 2)
            for j, (t0, t1) in enumerate(RNG):
                S.op("dve", (lambda o=tmp[:, t0:t1], i=self.X[:, fc, t0:t1], a=self.cols[:, j, fc:fc + 1], r=self.rstd[:, t0:t1]:
                             lambda e: e.scalar_tensor_tensor(out=o, in0=i, scalar=a, in1=r, op0=ALU.mult, op1=ALU.mult))(),
                     reads=[("x", fc), "cols", "rstd"], writes=[tk])
                S.op("act", (lambda o=self.h[:, fc, t0:t1], i=tmp[:, t0:t1], b=self.modT[:, k_shift * 16 + fc, j:j + 1]:
                             lambda e: e.activation(out=o, in_=i, func=AF.Identity, bias=b, scale=1.0))(),
                     reads=[tk, "modT"], writes=[("h", fc)])

    def epilogue(self, l, k_gate, gi, weight):
        S = self.S
        self.colstats(lambda fc: self.h[:, fc, :], lambda fc: [("y", fc)], FC, D, self.rstd, "rstd")
        self.mod_cols(l, k_gate, weight, gi, 2, "post")
        for fc in range(FC):
            S.dma("sp", (lambda fc=fc: lambda e: e.dma_start(out=self.X[:, fc, :], in_=self.xs[fc]))(),
                  reads=[("xs", fc)], writes=[("x", fc)])
        for fc in range(FC):
            tmp = self.scr[fc % 2]
            tk = ("scr", fc % 2)
            for j, (t0, t1) in enumerate(RNG):
                S.op("dve", (lambda o=tmp[:, t0:t1], i=self.h[:, fc, t0:t1], a=self.cols[:, 2 + j, fc:fc + 1], r=self.rstd[:, t0:t1]:
                             lambda e: e.scalar_tensor_tensor(out=o, in0=i, scalar=a, in1=r, op0=ALU.mult, op1=ALU.mult))(),
                     reads=[("y", fc), "cols", "rstd"], writes=[tk])
            S.op("pool", (lambda o=self.X[:, fc, :], t=tmp[:, :]: lambda e: e.tensor_tensor(out=o, in0=o, in1=t, op=ALU.add))(),
                 reads=[tk, ("x", fc)], writes=[("x", fc)])
            S.dma("sp", (lambda fc=fc: lambda e: e.dma_start(out=self.xs[fc], in_=self.X[:, fc, :]))(),
                  reads=[("x", fc)], writes=[("xs", fc)])

    def setup(self):
        S = self.S
        S.dma("sp", lambda e: e.dma_start(out=self.cst[:, :], in_=self.consts), writes=["cst"])
        S.dma("sp", lambda e: e.dma_start(out=self.flg[:, :], in_=self.flags), writes=["flg"])
        S.op("dve", lambda e: e.tensor_copy(out=self.identb[:, :], in_=self.cst[:, 0:128]), reads=["cst"], writes=["identb"])
        S.op("dve", lambda e: e.memset(self.onesb[:, :], 1.0), writes=["onesb"])
        S.op("dve", lambda e: e.tensor_copy(out=self.mkb[0:32, :], in_=self.cst[0:32, 128:192]), reads=["cst"], writes=["mkb"])
        for m, step in ((self.m32, 32), (self.mpc, 1024)):
            key = "m32" if step == 32 else "mpc"
            S.op("pool", (lambda m=m: lambda e: e.memset(m[:, :], 1.0))(), writes=[key])
            if step == 32:
                S.op("pool", lambda e: e.memset(self.m32[:, :].rearrange("p (c t) -> p c t", t=32)[:, :, 0:1], 0.0), reads=[key], writes=[key])
            else:
                S.op("pool", lambda e: e.memset(self.mpc[:, 0:1], 0.0), reads=[key], writes=[key])
                S.op("pool", lambda e: e.memset(self.mpc[:, NL:NL + 1], 0.0), reads=[key], writes=[key])
        stg = self.wst[0]
        for i in range(V_ROWS // 128):
            S.dma("sp", (lambda i=i: lambda e: e.dma_start(out=stg[:, i * 128:(i + 1) * 128], in_=self.vecs[i * 128:(i + 1) * 128, :]))(),
                  writes=[("ws", 0)])
        b = self.banks(2)
        for i in range(V_ROWS // 128):
            bk = b[i // 4]
            S.op("pe", (lambda i=i, bk=bk: lambda e: e.transpose(out=self.ps[bk][:, (i % 4) * 128:(i % 4 + 1) * 128], in_=stg[:, i * 128:(i + 1) * 128], identity=self.cst[:, 0:128]))(),
                 reads=[("ws", 0), "cst"], writes=[("ps", bk)])
        S.op("dve", lambda e: e.tensor_copy(out=self.vecT[:, 0:512], in_=self.ps[b[0]][:, 0:512]), reads=[("ps", b[0])], writes=["vecT"])
        S.op("dve", lambda e: e.tensor_copy(out=self.vecT[:, 512:640], in_=self.ps[b[1]][:, 0:128]), reads=[("ps", b[1])], writes=["vecT"])
        S.op("act", lambda e: e.activation(out=self.scb[:, :, :], in_=self.vecT[:, V_C:V_C + 32].rearrange("p (j k) -> p k j", j=2), func=AF.Silu),
             reads=["vecT"], writes=["scb"])
        S.op("dve", lambda e: e.memset(self.lbc[:, 0:16], 0.0), writes=["lbc"])
        S.op("dve", lambda e: e.tensor_tensor(out=self.lbc[:, 16:32], in0=self.vecT[:, V_LB + 16:V_LB + 32], in1=self.vecT[:, V_LB:V_LB + 16], op=ALU.subtract),
             reads=["vecT", "lbc"], writes=["lbc"])
        S.op("act", lambda e: e.activation(out=self.lbc[:, 16:32], in_=self.lbc[:, 16:32], func=AF.Sigmoid), reads=["lbc"], writes=["lbc"])
        S.op("dve", lambda e: e.tensor_scalar(out=self.lbc[:, 32:64], in0=self.lbc[:, 0:32], scalar1=-1.0, scalar2=1.0, op0=ALU.mult, op1=ALU.add),
             reads=["lbc"], writes=["lbc"])

    def load_x(self):
        S = self.S
        for tt in range(NT // 128):
            stg = self.wst[tt % self.NS]
            sk = ("ws", tt % self.NS)
            src = self.x_lat[tt * 128:(tt + 1) * 128, :] if tt < 8 else self.x_ctx
            S.dma("sp", (lambda stg=stg, src=src: lambda e: e.dma_start(out=stg[:, :], in_=src))(), writes=[sk])
            for q in range(4):
                bk = self.banks(1)[0]
                for i in range(4):
                    fc = q * 4 + i
                    S.op("pe", (lambda bk=bk, i=i, fc=fc, stg=stg: lambda e: e.transpose(out=self.ps[bk][:, i * 128:(i + 1) * 128], in_=stg[:, fc * 128:(fc + 1) * 128], identity=self.cst[:, 0:128]))(),
                         reads=[sk, "cst"], writes=[("ps", bk)], signal=(i == 3))
                eng = "dve" if q % 2 == 0 else "act"
                o = self.X[:, q * 4:(q + 1) * 4, tt * 128:(tt + 1) * 128]
                i_ = self.ps[bk][:, :].rearrange("p (c t) -> p c t", t=128)
                if eng == "dve":
                    S.op("dve", (lambda o=o, i_=i_: lambda e: e.tensor_copy(out=o, in_=i_))(), reads=[("ps", bk)], writes=[("x", q * 4 + i) for i in range(4)])
                else:
                    S.op("act", (lambda o=o, i_=i_: lambda e: e.activation(out=o, in_=i_, func=AF.Copy))(), reads=[("ps", bk)], writes=[("x", q * 4 + i) for i in range(4)])
        for fc in range(FC):
            S.dma("sp", (lambda fc=fc: lambda e: e.dma_start(out=self.xs[fc], in_=self.X[:, fc, :]))(), reads=[("x", fc)], writes=[("xs", fc)])

    def store_x(self):
        S = self.S
        for tt in range(NL // 128):
            stg = self.wst[tt % self.NS]
            sk = ("ws", tt % self.NS)
            for q in range(4):
                bk = self.banks(1)[0]
                for i in range(4):
                    fc = q * 4 + i
                    S.op("pe", (lambda bk=bk, i=i, fc=fc, tt=tt: lambda e: e.transpose(out=self.ps[bk][:, i * 128:(i + 1) * 128], in_=self.X[:, fc, tt * 128:(tt + 1) * 128], identity=self.cst[:, 0:128]))(),
                         reads=[("x", fc), "cst"], writes=[("ps", bk)], signal=(i == 3))
                o = stg[:, q * 512:(q + 1) * 512]
                if q % 2 == 0:
                    S.op("dve", (lambda o=o, bk=bk: lambda e: e.tensor_copy(out=o, in_=self.ps[bk][:, :]))(), reads=[("ps", bk)], writes=[sk])
                else:
                    S.op("act", (lambda o=o, bk=bk: lambda e: e.activation(out=o, in_=self.ps[bk][:, :], func=AF.Copy))(), reads=[("ps", bk)], writes=[sk])
            S.dma("sp", (lambda tt=tt, stg=stg: lambda e: e.dma_start(out=self.out[tt * 128:(tt + 1) * 128, :], in_=stg[:, :]))(), reads=[sk], writes=[("out", tt)])

    def compute_mod(self, l):
        S = self.S
        bk = self.banks(1)[0]
        for jb in range(144):
            wb, wkey = self.load_unit(self.w_mod[l], 0, 16, jb * 128)
            for k in range(16):
                S.op("pe", (lambda k=k, wb=wb, jb=jb: lambda e: e.matmul(self.ps[bk][:, jb * 2:jb * 2 + 2], lhsT=wb[:, k, :], rhs=self.scb[:, k, :], start=(k == 0), stop=(k == 15)))(),
                     reads=[wkey, "scb"], writes=[("ps", bk)], signal=(k == 15))
        bm = self.vecT[:, V_BMOD + l * 144: V_BMOD + (l + 1) * 144]
        S.op("dve", lambda e: e.tensor_tensor(out=self.modT[:, :, :], in0=self.ps[bk][:, 0:288].rearrange("p (b j) -> p b j", j=2),
                                              in1=bm.unsqueeze(2).broadcast_to([128, 144, 2]), op=ALU.add),
             reads=[("ps", bk), "vecT"], writes=["modT"])

    def ffn(self, l, which, k0, gi0):
        S = self.S
        wgu = self.w_gu[which][l]
        wdn = self.w_dn[which][l]
        self.prologue(l, k0, k0 + 1, gi0)
        S.retire([("x", fc) for fc in range(FC)], [("g", j) for j in range(NJ)])
        hk = lambda k: [("h", k)]
        hf = lambda k: self.h[:, k, :]
        for j in range(NJ):
            s = self.scr[2 + j % 2]
            skey = ("scr", 2 + j % 2)

            def ev_a(bl, s=s, skey=skey):
                for (p, t0, t1, bkey) in bl:
                    S.op("act", (lambda p=p, o=s[:, t0:t1]: lambda e: e.activation(out=o, in_=p, func=AF.Silu))(), reads=[bkey], writes=[skey])

            def ev_b(bl, s=s, skey=skey, j=j):
                for (p, t0, t1, bkey) in bl:
                    S.op("dve", (lambda p=p, o=self.gT[:, j, t0:t1], i=s[:, t0:t1]: lambda e: e.tensor_tensor(out=o, in0=i, in1=p, op=ALU.mult))(),
                         reads=[bkey, skey], writes=[("g", j)])
            self.project([(wgu, 0, 16, j * 128)], hf, hk, ev_a)
            self.project([(wgu, 0, 16, DFF + j * 128)], hf, hk, ev_b)
        S.retire([("h", fc) for fc in range(FC)], [("y", fc) for fc in range(FC)])
        gk = lambda k: [("g", k)]
        gf = lambda k: self.gT[:, k, :]
        for jo in range(FC):
            def ev_y(bl, jo=jo):
                for n, (p, t0, t1, bkey) in enumerate(bl):
                    o = self.h[:, jo, t0:t1]
                    if n % 2 == 0:
                        S.op("act", (lambda p=p, o=o: lambda e: e.activation(out=o, in_=p, func=AF.Copy))(), reads=[bkey], writes=[("y", jo)])
                    else:
                        S.op("dve", (lambda p=p, o=o: lambda e: e.tensor_copy(out=o, in_=p))(), reads=[bkey], writes=[("y", jo)])
            self.project([(wdn, kq * 11, 11, jo * 128) for kq in range(4)], gf, gk, ev_y)
        S.retire([("g", j) for j in range(NJ)], [("x", fc) for fc in range(FC)])
        self.epilogue(l, k0 + 2, gi0 + 1, 0.5)
        S.retire([("y", fc) for fc in range(FC)], [("h", fc) for fc in range(FC)])

    def program(self):
        st = self.stages
        self.setup()
        self.load_x()
        for l in range(DEPTH):
            if l >= st.get("layers", DEPTH):
                break
            self.compute_mod(l)
            self.ffn(l, 0, 0, 0)
            if st.get("mixer", True):
                self.mixer(l)
            if st.get("ffn2", True):
                self.ffn(l, 1, 6, 4)
        self.store_x()


    def gbuf(self, off, n, f32=False):
        ap = self.G[:, off:off + n]
        return (ap.bitcast(F32) if f32 else ap), ("G", off)

    def decay_prep(self, l, d, h, zbanks):
        S = self.S
        idx = l * 16 + d * 8 + h
        s0, s1 = self.scr[0], self.scr[1]
        for (p, t0, t1, bkey) in zbanks:
            S.op("act", (lambda p=p, o=s0[:, t0:t1]: lambda e: e.activation(out=o, in_=p, func=AF.Sigmoid))(), reads=[bkey], writes=[("scr", 0)])
        S.op("dve", lambda e: e.tensor_scalar(out=s0[:, :], in0=s0[:, :], scalar1=self.lbc[:, 32 + idx:33 + idx], scalar2=self.lbc[:, idx:idx + 1], op0=ALU.mult, op1=ALU.add),
             reads=[("scr", 0), "lbc"], writes=[("scr", 0)])
        S.op("dve", lambda e: e.tensor_scalar_max(out=s0[:, :], in0=s0[:, :], scalar1=1e-30), reads=[("scr", 0)], writes=[("scr", 0)])
        S.op("act", lambda e: e.activation(out=s1[:, :], in_=s0[:, :], func=AF.Ln), reads=[("scr", 0)], writes=[("scr", 1)])
        S.op("dve", lambda e: e.tensor_scalar(out=s0[:, :], in0=s0[:, :], scalar1=-1.0, scalar2=1.0, op0=ALU.mult, op1=ALU.add),
             reads=[("scr", 0)], writes=[("scr", 0)])

    def transpose_tiles(self, src, skey, dst, dkey, ncols, npart_out):
        S = self.S
        n = NT // ncols
        per = 8
        for b0 in range(0, n, per):
            nb = min(per, n - b0)
            bk = self.banks(1)[0]
            psb = self.ps[bk][:, :].bitcast(BF16)
            for j in range(nb):
                c = b0 + j
                S.op("pe", (lambda j=j, c=c, psb=psb: lambda e: e.transpose(out=psb[0:npart_out, j * 128:(j + 1) * 128], in_=src[:, c * ncols:(c + 1) * ncols], identity=self.identb[:, :]))(),
                     reads=[skey, "identb"], writes=[("ps", bk)], signal=(j == nb - 1))
            o = dst[0:npart_out, b0:b0 + nb, :]
            i_ = psb[0:npart_out, 0:nb * 128].rearrange("p (n d) -> p n d", d=128)
            if (b0 // per) % 2 == 0:
                S.op("dve", (lambda o=o, i_=i_: lambda e: e.tensor_copy(out=o, in_=i_))(), reads=[("ps", bk)], writes=[dkey])
            else:
                S.op("act", (lambda o=o, i_=i_: lambda e: e.activation(out=o, in_=i_, func=AF.Copy))(), reads=[("ps", bk)], writes=[dkey])

    def hgrn(self, l, xkeys, yb):
        S = self.S
        w_in = self.w_in[l]
        hk = lambda k: [("h", k)]
        hf = lambda k: self.h[:, k, :]
        s0, s1, s2, s3 = self.scr
        K = lambda i: ("scr", i)
        ybk = [("yb", i) for i in range(8)]
        Eloc, kE = self.gbuf(18432, 8192, True)
        Ev = Eloc.rearrange("p (s e) -> p s e", e=128)
        kg = [self.gbuf(26624, NT), self.gbuf(27776, NT)]
        vT, kvT = self.gbuf(28928, NT)
        ktok = [self.gbuf(30080, NT), self.gbuf(31232, NT)]
        vtok, kvtok = self.gbuf(32384, NT)
        p1keys = [kE, kg[0][1], kg[1][1], kvT, ktok[0][1], ktok[1][1], kvtok]
        S.retire(xkeys, p1keys + ybk)
        lastc = [NL - 1, NT - 1]
        for h in range(8):
            for d in range(2):
                got = []
                self.project([(w_in, 0, 16, (OFF_FF if d == 0 else OFF_FB) + h * 128)], hf, hk, lambda bl: got.extend(bl))
                self.decay_prep(l, d, h, got)
                S.op("dve", lambda e: e.tensor_tensor_scan(out=s2[:, :], data0=self.mpc[:, :], data1=s1[:, :], initial=0.0, op0=ALU.mult, op1=ALU.add),
                     reads=[K(1), "mpc"], writes=[K(2)])
                for pc, (t0, t1) in enumerate(RNG):
                    slot = (d * 2 + pc) * 8 + h
                    lc = lastc[pc]
                    S.op("act", (lambda slot=slot, lc=lc: lambda e: e.activation(out=self.dloc[:, slot:slot + 1], in_=s2[:, lc:lc + 1], func=AF.Exp))(),
                         reads=[K(2)], writes=["dloc"])
                if d == 0:
                    for pc, (t0, t1) in enumerate(RNG):
                        lc = lastc[pc]
                        S.op("act", (lambda t0=t0, t1=t1, lc=lc: lambda e: e.activation(out=s3[:, t0:t1], in_=s2[:, t0:t1], func=AF.Exp, scale=-1.0, bias=s2[:, lc:lc + 1]))(),
                             reads=[K(2)], writes=[K(3)])
                else:
                    S.op("dve", lambda e: e.tensor_tensor(out=s3[:, :], in0=s2[:, :], in1=s1[:, :], op=ALU.subtract), reads=[K(2), K(1)], writes=[K(3)])
                    S.op("act", lambda e: e.activation(out=s3[:, :], in_=s3[:, :], func=AF.Exp), reads=[K(3)], writes=[K(3)])
                S.op("dve", (lambda d=d: lambda e: e.tensor_tensor(out=kg[d][0], in0=s0[:, :], in1=s3[:, :], op=ALU.mult))(), reads=[K(0), K(3)], writes=[kg[d][1]])
            got = []
            self.project([(w_in, 0, 16, OFF_I + h * 128)], hf, hk, lambda bl: got.extend(bl))
            for (p, t0, t1, bkey) in got:
                S.op("act", (lambda p=p, o=vT[:, t0:t1]: lambda e: e.activation(out=o, in_=p, func=AF.Copy))(), reads=[bkey], writes=[kvT])
            v3 = lambda ap: ap.rearrange("p (n d) -> p n d", d=128)
            self.transpose_tiles(vT, kvT, v3(vtok), kvtok, 128, 128)
            for d in range(2):
                self.transpose_tiles(kg[d][0], kg[d][1], v3(ktok[d][0]), ktok[d][1], 128, 128)
                bk = self.banks(1)[0]
                kt, vt = v3(ktok[d][0]), v3(vtok)
                for tt in range(9):
                    col = 0 if tt < 8 else 128
                    S.op("pe", (lambda tt=tt, col=col, kt=kt, vt=vt, bk=bk: lambda e: e.matmul(self.ps[bk][:, col:col + 128], lhsT=kt[:, tt, :], rhs=vt[:, tt, :], start=(tt == 0 or tt == 8), stop=(tt == 7 or tt == 8)))(),
                         reads=[ktok[d][1], kvtok], writes=[("ps", bk)], signal=(tt >= 7))
                for pc in range(2):
                    slot = (d * 2 + pc) * 8 + h
                    S.op("dve", (lambda slot=slot, pc=pc, bk=bk: lambda e: e.tensor_copy(out=Ev[:, slot, :], in_=self.ps[bk][:, pc * 128:(pc + 1) * 128]))(),
                         reads=[("ps", bk)], writes=[kE])
        for g in range(4):
            S.dma("sp", (lambda g=g: lambda e: e.dma_start(out=self.cc_in[g][:, 0:1024], in_=Eloc[:, g * 1024:(g + 1) * 1024]))(), reads=[kE], writes=[("cc_in", g)])
            S.dma("sp", (lambda g=g: lambda e: e.dma_start(out=self.cc_in[g][:, 1024:1032], in_=self.dloc[:, g * 8:(g + 1) * 8]))(), reads=["dloc"], writes=[("cc_in", g)])
            S.op("pool", (lambda g=g: lambda e: e.collective_compute("AllGather", ALU.bypass, replica_groups=[[0, 1], [2, 3], [4, 5], [6, 7]], ins=[self.cc_in[g]], outs=[self.cc_out[g]]))(),
                 reads=[("cc_in", g)], writes=[("cc_out", g)])
        EA, kEA = self.gbuf(0, 8320, True)
        EB, kEB = self.gbuf(33536, 8320, True)
        Sin, kSin = self.gbuf(41856, 8192, True)
        S.retire(p1keys, [kEA, kEB, kSin])
        for g in range(4):
            S.dma("sp", (lambda g=g: lambda e: e.dma_start(out=EA[:, g * 1040:(g + 1) * 1040], in_=self.cc_out[g][0:128, :]))(), reads=[("cc_out", g)], writes=[kEA])
            S.dma("sp", (lambda g=g: lambda e: e.dma_start(out=EB[:, g * 1040:(g + 1) * 1040], in_=self.cc_out[g][128:256, :]))(), reads=[("cc_out", g)], writes=[kEB])
        Eg = lambda X, g: X[:, g * 1040:g * 1040 + 1024].rearrange("p (h e) -> p h e", e=128)
        Dg = lambda X, g: X[:, g * 1040 + 1024:g * 1040 + 1032].unsqueeze(2).broadcast_to([128, 8, 128])
        Sg = lambda g: Sin[:, g * 1024:(g + 1) * 1024].rearrange("p (h e) -> p h e", e=128)
        t0v = s0[:, 0:1024].rearrange("p (h e) -> p h e", e=128)
        t1v = s1[:, 0:1024].rearrange("p (h e) -> p h e", e=128)
        fA, fB = self.flg[:, 0:1], self.flg[:, 1:2]
        rk = [kEA, kEB, "flg"]

        def sin_dir(P, Q, kP, kQ, fP, fQ, g_lat, g_ctx):
            S.op("dve", lambda e: e.tensor_tensor(out=t0v, in0=Eg(P, g_ctx), in1=Dg(Q, g_ctx), op=ALU.mult), reads=rk, writes=[K(0)])
            S.op("dve", lambda e: e.tensor_tensor(out=t0v, in0=t0v, in1=Eg(Q, g_ctx), op=ALU.add), reads=rk + [K(0)], writes=[K(0)])
            S.op("dve", lambda e: e.tensor_tensor(out=t1v, in0=t0v, in1=Dg(P, g_lat), op=ALU.mult), reads=rk + [K(0)], writes=[K(1)])
            S.op("dve", lambda e: e.tensor_tensor(out=t1v, in0=t1v, in1=Eg(P, g_lat), op=ALU.add), reads=rk + [K(1)], writes=[K(1)])
            S.op("dve", lambda e: e.tensor_scalar(out=Sg(g_lat), in0=t0v, scalar1=fP, scalar2=None, op0=ALU.mult), reads=rk + [K(0)], writes=[kSin])
            S.op("dve", lambda e: e.scalar_tensor_tensor(out=Sg(g_lat), in0=t1v, scalar=fQ, in1=Sg(g_lat), op0=ALU.mult, op1=ALU.add), reads=rk + [K(1), kSin], writes=[kSin])
            S.op("dve", lambda e: e.tensor_scalar(out=Sg(g_ctx), in0=Eg(P, g_ctx), scalar1=fQ, scalar2=None, op0=ALU.mult), reads=rk, writes=[kSin])
        sin_dir(EA, EB, kEA, kEB, fA, fB, 0, 1)
        sin_dir(EB, EA, kEB, kEA, fB, fA, 2, 3)
        Sbf = [self.gbuf(18432, 4608), self.gbuf(23296, 4608)]
        qX = [self.gbuf(28160, NT), self.gbuf(29312, NT)]
        kX = [self.gbuf(30464, NT), self.gbuf(31616, NT)]
        khat, kkhat = self.gbuf(32768, NT)
        PX = [self.gbuf(33920, NT), self.gbuf(35072, NT)]
        Spp, kSpp = self.gbuf(36224, 512, True)
        khtok, kkhtok = self.gbuf(36736, 4608)
        vt32, kvt32 = self.gbuf(0, 4608)
        T4, kT4 = self.gbuf(4608, 2304, True)
        T5, kT5 = self.gbuf(6912, 2304, True)
        p2keys = [Sbf[0][1], Sbf[1][1], qX[0][1], qX[1][1], kX[0][1], kX[1][1], kkhat, PX[0][1], PX[1][1], kSpp, kkhtok, kvt32, kT4, kT5]
        S.retire([kEA, kEB], p2keys)
        c3 = lambda ap: ap.rearrange("p (c e) -> p c e", e=128)
        vT2, kvT2 = self.sqb[0], ("sqb", 0)
        Sin4 = Sin.rearrange("p (g h e) -> p g h e", h=8, e=128)
        for h in range(8):
            got = []
            self.project([(w_in, 0, 16, OFF_Q + h * 128)], hf, hk, lambda bl: got.extend(bl))
            for (p, t0, t1, bkey) in got:
                S.op("act", (lambda p=p, o=T4[:, t0:t1]: lambda e: e.activation(out=o, in_=p, func=AF.Copy))(), reads=[bkey], writes=[kT4])
            got = []
            self.project([(w_in, 0, 16, OFF_G + h * 128)], hf, hk, lambda bl: got.extend(bl))
            for (p, t0, t1, bkey) in got:
                S.op("act", (lambda p=p, o=T5[:, t0:t1]: lambda e: e.activation(out=o, in_=p, func=AF.Silu))(), reads=[bkey], writes=[kT5])
            got = []
            self.project([(w_in, 0, 16, OFF_I + h * 128)], hf, hk, lambda bl: got.extend(bl))
            for (p, t0, t1, bkey) in got:
                S.op("act", (lambda p=p, o=vT2[:, t0:t1]: lambda e: e.activation(out=o, in_=p, func=AF.Copy))(), reads=[bkey], writes=[kvT2])
            self.transpose_tiles(vT2, kvT2, c3(vt32), kvt32, 32, 32)
            for d in range(2):
                got = []
                self.project([(w_in, 0, 16, (OFF_FF if d == 0 else OFF_FB) + h * 128)], hf, hk, lambda bl: got.extend(bl))
                self.decay_prep(l, d, h, got)
                b3 = lambda ap: ap[:, :].rearrange("p (c t) -> p c t", t=32)
                S.op("dve", lambda e: e.tensor_tensor_scan(out=s2[:, :], data0=self.m32[:, :], data1=s1[:, :], initial=0.0, op0=ALU.mult, op1=ALU.add),
                     reads=[K(1), "m32"], writes=[K(2)])
                if d == 0:
                    S.op("dve", lambda e: e.tensor_copy(out=self.edge[:, :], in_=b3(s2)[:, :, 31]), reads=[K(2)], writes=["edge"])
                else:
                    S.op("dve", lambda e: e.tensor_copy(out=self.edge[:, :], in_=b3(s2)[:, :, 31]), reads=[K(2)], writes=["edge"])
                    S.op("dve", lambda e: e.tensor_tensor(out=s3[:, :], in0=s1[:, :], in1=s2[:, :], op=ALU.subtract), reads=[K(1), K(2)], writes=[K(3)])
                    S.op("dve", lambda e: e.tensor_tensor(out=b3(s2), in0=b3(s3), in1=self.edge[:, :].unsqueeze(2).broadcast_to([128, NCH, 32]), op=ALU.add),
                         reads=[K(3), "edge"], writes=[K(2)])
                S.op("act", (lambda d=d: lambda e: e.activation(out=self.dcol[:, d, :], in_=self.edge[:, :], func=AF.Exp))(), reads=["edge"], writes=[("dcol", d)])
                S.op("act", lambda e: e.activation(out=s3[:, :], in_=s2[:, :], func=AF.Exp), reads=[K(2)], writes=[K(3)])
                S.op("dve", (lambda d=d: lambda e: e.tensor_tensor(out=qX[d][0], in0=T4, in1=s3[:, :], op=ALU.mult))(), reads=[kT4, K(3)], writes=[qX[d][1]])
                S.op("dve", lambda e: e.tensor_tensor(out=b3(s3), in0=b3(s2), in1=self.edge[:, :].unsqueeze(2).broadcast_to([128, NCH, 32]), op=ALU.subtract),
                     reads=[K(2), "edge"], writes=[K(3)])
                S.op("act", lambda e: e.activation(out=s3[:, :], in_=s3[:, :], func=AF.Exp, scale=-1.0), reads=[K(3)], writes=[K(3)])
                S.op("dve", lambda e: e.tensor_tensor(out=khat, in0=s0[:, :], in1=s3[:, :], op=ALU.mult), reads=[K(0), K(3)], writes=[kkhat])
                self.transpose_tiles(khat, kkhat, c3(khtok), kkhtok, 32, 32)
                midi = 15 if d == 0 else 16
                S.op("dve", (lambda midi=midi: lambda e: e.tensor_copy(out=self.edge[:, :], in_=b3(s2)[:, :, midi]))(), reads=[K(2), ("dcol", d), kkhat], writes=["edge"])
                S.op("dve", lambda e: e.tensor_tensor(out=b3(s3), in0=b3(s2), in1=self.edge[:, :].unsqueeze(2).broadcast_to([128, NCH, 32]), op=ALU.subtract),
                     reads=[K(2), "edge"], writes=[K(3)])
                S.op("act", lambda e: e.activation(out=s2[:, :], in_=s3[:, :], func=AF.Exp), reads=[K(3)], writes=[K(2)])
                S.op("dve", lambda e: e.tensor_tensor(out=khat, in0=T4, in1=s2[:, :], op=ALU.mult), reads=[kT4, K(2)], writes=[kkhat])
                S.op("act", lambda e: e.activation(out=s2[:, :], in_=s3[:, :], func=AF.Exp, scale=-1.0), reads=[K(3)], writes=[K(2)])
                S.op("dve", (lambda d=d: lambda e: e.tensor_tensor(out=kX[d][0], in0=s0[:, :], in1=s2[:, :], op=ALU.mult))(), reads=[K(0), K(2)], writes=[kX[d][1]])
                P3 = PX[d][0].rearrange("p (c t) -> p c t", t=32)
                S.op("pool", (lambda d=d: lambda e: e.memset(PX[d][0][0:32, :], 0.0))(), writes=[PX[d][1]])
                msk = self.mkb[0:32, d * 32:(d + 1) * 32].bitcast(mybir.dt.uint16)
                for c0 in range(0, NCH, 16):
                    n = min(16, NCH - c0)
                    bk = self.banks(1)[0]
                    for j in range(n):
                        c = c0 + j
                        S.op("pe", (lambda j=j, c=c, d=d, bk=bk: lambda e: e.matmul(self.ps[bk][0:32, j * 32:(j + 1) * 32], lhsT=kX[d][0][:, c * 32:(c + 1) * 32], rhs=khat[:, c * 32:(c + 1) * 32], start=True, stop=True))(),
                             reads=[kX[d][1], kkhat], writes=[("ps", bk)], signal=(j == n - 1))
                    for j in range(n):
                        c = c0 + j
                        S.op("dve", (lambda c=c, j=j, bk=bk, P3=P3, msk=msk: lambda e: e.copy_predicated(out=P3[0:32, c, :], mask=msk, data=self.ps[bk][0:32, j * 32:(j + 1) * 32]))(),
                             reads=[("ps", bk), "mkb"], writes=[PX[d][1]])
                Sb3 = c3(Sbf[d][0])
                kh3, vt3 = c3(khtok), c3(vt32)
                for pc, (cs, ce) in enumerate([(0, 32), (32, 36)]):
                    order = list(range(cs, ce)) if d == 0 else list(range(ce - 1, cs - 1, -1))
                    state = Sin4[:, d * 2 + pc, h, :]
                    skey = kSin
                    bank_of = {}
                    for n_, c in enumerate(order):
                        if n_ % 4 == 0 and n_ + 1 < len(order):
                            bk = self.banks(1)[0]
                            grp = order[n_:n_ + 4]
                            for j, cc in enumerate(grp):
                                bank_of[cc] = (bk, j)
                                S.op("pe", (lambda j=j, cc=cc, bk=bk, kh3=kh3, vt3=vt3: lambda e: e.matmul(self.ps[bk][:, j * 128:(j + 1) * 128], lhsT=kh3[0:32, cc, :], rhs=vt3[0:32, cc, :], start=True, stop=True))(),
                                     reads=[kkhtok, kvt32], writes=[("ps", bk)], signal=(j == len(grp) - 1))
                        S.op("act", (lambda c=c, state=state, Sb3=Sb3: lambda e: e.activation(out=Sb3[:, c, :], in_=state, func=AF.Copy))(), reads=[skey], writes=[Sbf[d][1]])
                        if n_ + 1 < len(order):
                            bk, j = bank_of[c]
                            nxt = Spp[:, (n_ % 2) * 128:(n_ % 2 + 1) * 128]
                            S.op("dve", (lambda c=c, d=d, state=state, nxt=nxt, bk=bk, j=j: lambda e: e.scalar_tensor_tensor(
                                out=nxt, in0=state, scalar=self.dcol[:, d, c:c + 1], in1=self.ps[bk][:, j * 128:(j + 1) * 128], op0=ALU.mult, op1=ALU.add))(),
                                reads=[skey, ("dcol", d), ("ps", bk)], writes=[kSpp])
                            state, skey = nxt, kSpp
            bks = self.banks(3)
            vt3 = c3(vt32)
            for c in range(NCH):
                bk = bks[c // 12]
                col = (c % 12) * 32
                ops = []
                for d in range(2):
                    ops.append((vt3[0:32, c, :], PX[d][0][0:32, c * 32:(c + 1) * 32], [kvt32, PX[d][1]]))
                    ops.append((c3(Sbf[d][0])[:, c, :], qX[d][0][:, c * 32:(c + 1) * 32], [Sbf[d][1], qX[d][1]]))
                for n_, (lt, rr, rkeys) in enumerate(ops):
                    S.op("pe", (lambda lt=lt, rr=rr, bk=bk, col=col, n_=n_: lambda e: e.matmul(self.ps[bk][:, col:col + 32], lhsT=lt, rhs=rr, start=(n_ == 0), stop=(n_ == 3)))(),
                         reads=rkeys, writes=[("ps", bk)], signal=(n_ == 3 and (c % 12 == 11)))
            for ti, (t0, t1) in enumerate(TBS):
                S.op("act", (lambda ti=ti, t0=t0, t1=t1, bks=bks: lambda e: e.activation(out=s0[:, t0:t1], in_=self.ps[bks[ti]][:, 0:t1 - t0], func=AF.Copy))(),
                     reads=[("ps", bks[ti])], writes=[K(0)])
            self.colstats(lambda fc: s0[:, :], lambda fc: [K(0)], 1, 128, self.rstd, "rstd")
            S.op("dve", (lambda h=h: lambda e: e.scalar_tensor_tensor(out=s1[:, :], in0=s0[:, :], scalar=self.vecT[:, V_HG + l * 8 + h:V_HG + l * 8 + h + 1], in1=self.rstd[:, :], op0=ALU.mult, op1=ALU.mult))(),
                 reads=[K(0), "vecT", "rstd"], writes=[K(1)])
            S.op("pool", (lambda h=h: lambda e: e.tensor_tensor(out=yb[:, h, :], in0=s1[:, :], in1=T5, op=ALU.mult))(), reads=[K(1), kT5], writes=[("yb", h)])
        return p2keys + [kSin]

    def mixer(self, l):
        S = self.S
        G = self.G
        w_in = self.w_in[l]
        self.prologue(l, 3, 4, 2)
        ya = G[:, 0:9216].rearrange("p (c t) -> p c t", t=NT)
        yb = G[:, 9216:18432].rearrange("p (c t) -> p c t", t=NT)
        v32 = G[:, 18432:36864].bitcast(F32).rearrange("p (c t) -> p c t", t=NT)
        vn = G[:, 36864:46080].rearrange("p (c t) -> p c t", t=NT)
        wsT = G[:, 46080:47104].rearrange("p (g t) -> p g t", t=128)
        bsb = G[:, 47104:49152].bitcast(F32)
        vtok = G[:, 18432:27648].rearrange("p (n d) -> p n d", d=1024)
        mm = G[:, 18432:36864].rearrange("p (c t) -> p c t", t=NT)
        k8 = lambda n: [(n, i) for i in range(8)]
        xkeys = [("x", fc) for fc in range(FC)]
        hk = lambda k: [("h", k)]
        hf = lambda k: self.h[:, k, :]
        if self.stages.get("hgrn", True):
            hkeys = self.hgrn(l, xkeys, yb)
            S.retire(hkeys, k8("ya") + k8("v") + k8("vn") + ["wsT", "bsb"])
        else:
            S.retire(xkeys, k8("ya") + k8("yb") + k8("v") + k8("vn") + ["wsT", "bsb"])
            for i in range(8):
                S.op("pool", (lambda i=i: lambda e: e.memset(yb[:, i, :], 0.0))(), writes=[("yb", i)])
        for fc in range(8):
            def ev_u(bl, fc=fc):
                for (p, t0, t1, bkey) in bl:
                    S.op("act", (lambda p=p, o=ya[:, fc, t0:t1]: lambda e: e.activation(out=o, in_=p, func=AF.Gelu))(), reads=[bkey], writes=[("ya", fc)])
            self.project([(w_in, 0, 16, OFF_U + fc * 128)], hf, hk, ev_u)
        for fc in range(8):
            def ev_v(bl, fc=fc):
                for (p, t0, t1, bkey) in bl:
                    S.op("act", (lambda p=p, o=v32[:, fc, t0:t1]: lambda e: e.activation(out=o, in_=p, func=AF.Gelu))(), reads=[bkey], writes=[("v", fc)])
            self.project([(w_in, 0, 16, OFF_V + fc * 128)], hf, hk, ev_v)
        bks = self.banks(3)
        for fc in range(8):
            sq = self.sqb[fc % 2]
            sk = ("sqb", fc % 2)
            S.op("act", (lambda fc=fc, sq=sq: lambda e: e.activation(out=sq[:, :], in_=v32[:, fc, :], func=AF.Copy))(), reads=[("v", fc)], writes=[sk])
            for ti, (t0, t1) in enumerate(TBS):
                S.op("pe", (lambda o=self.ps[bks[ti]][:, 0:t1 - t0], rr=sq[:, t0:t1], st=(fc == 0), sp=(fc == 7):
                            lambda e: e.matmul(o, lhsT=self.onesb[:, :], rhs=rr, start=st, stop=sp))(),
                     reads=[sk, "onesb"], writes=[("ps", bks[ti])], signal=(ti == 2))
        mean = self.scr[0]
        for ti, (t0, t1) in enumerate(TBS):
            S.op("act", (lambda o=mean[:, t0:t1], i=self.ps[bks[ti]][:, 0:t1 - t0]: lambda e: e.activation(out=o, in_=i, func=AF.Identity, scale=1.0 / 1024, bias=0.0))(),
                 reads=[("ps", bks[ti])], writes=[("scr", 0)])
        for fc in range(8):
            S.op("dve", (lambda fc=fc: lambda e: e.tensor_tensor(out=v32[:, fc, :], in0=v32[:, fc, :], in1=mean[:, :], op=ALU.subtract))(),
                 reads=[("v", fc), ("scr", 0)], writes=[("v", fc)])
        self.colstats(lambda fc: v32[:, fc, :], lambda fc: [("v", fc)], 8, 1024, self.rstd, "rstd")
        for fc in range(8):
            S.op("dve", (lambda fc=fc: lambda e: e.scalar_tensor_tensor(out=vn[:, fc, :], in0=v32[:, fc, :], scalar=self.vecT[:, V_CG + l * 8 + fc:V_CG + l * 8 + fc + 1], in1=self.rstd[:, :], op0=ALU.mult, op1=ALU.mult))(),
                 reads=[("v", fc), "vecT", "rstd"], writes=[("vn", fc)])
        vtk = [("vtok", tt) for tt in range(9)]
        S.retire(k8("v"), vtk)
        for tt in range(9):
            bk = self.banks(1)[0]
            psb = self.ps[bk][:, :].bitcast(BF16)
            for g in range(8):
                S.op("pe", (lambda g=g, tt=tt, psb=psb: lambda e: e.transpose(out=psb[:, g * 128:(g + 1) * 128], in_=vn[:, g, tt * 128:(tt + 1) * 128], identity=self.identb[:, :]))(),
                     reads=[("vn", g), "identb"], writes=[("ps", bk)], signal=(g == 7))
            if tt % 2 == 0:
                S.op("dve", (lambda tt=tt, psb=psb: lambda e: e.tensor_copy(out=vtok[:, tt, :], in_=psb[:, 0:1024]))(), reads=[("ps", bk)], writes=[("vtok", tt)])
            else:
                S.op("act", (lambda tt=tt, psb=psb: lambda e: e.activation(out=vtok[:, tt, :], in_=psb[:, 0:1024], func=AF.Copy))(), reads=[("ps", bk)], writes=[("vtok", tt)])
        wstg = self.scr[1][:, 0:1024].rearrange("p (g s) -> p g s", s=128)
        S.dma("sp", lambda e: e.dma_start(out=wstg, in_=self.w_sp[l].rearrange("g t s -> t g s")), writes=[("scr", 1)])
        S.dma("sp", lambda e: e.dma_start(out=bsb, in_=self.b_sp[l].partition_broadcast(128)), writes=["bsb"])
        b2 = self.banks(2)
        for g in range(8):
            bk = b2[g // 4]
            S.op("pe", (lambda g=g, bk=bk: lambda e: e.transpose(out=self.ps[bk][:, (g % 4) * 128:(g % 4 + 1) * 128], in_=wstg[:, g, :], identity=self.cst[:, 0:128]))(),
                 reads=[("scr", 1), "cst"], writes=[("ps", bk)], signal=(g % 4 == 3))
        for q in range(2):
            S.op("dve", (lambda q=q: lambda e: e.tensor_copy(out=wsT[:, q * 4:(q + 1) * 4, :], in_=self.ps[b2[q]][:, :].rearrange("p (g t) -> p g t", t=128)))(),
                 reads=[("ps", b2[q])], writes=["wsT"])
        for g in range(8):
            bks = self.banks(3)
            for tt in range(9):
                S.op("pe", (lambda g=g, tt=tt, bk=bks[tt // 3]: lambda e: e.matmul(self.ps[bk][:, (tt % 3) * 128:(tt % 3 + 1) * 128], lhsT=vtok[:, tt, g * 128:(g + 1) * 128], rhs=wsT[:, g, :], start=True, stop=True))(),
                     reads=[("vtok", tt), "wsT"], writes=[("ps", bks[tt // 3])], signal=(tt % 3 == 2))
            sv = self.scr[2]
            for ti, (t0, t1) in enumerate(TBS):
                S.op("dve", (lambda g=g, bk=bks[ti], t0=t0, t1=t1: lambda e: e.tensor_tensor(
                    out=sv[:, t0:t1].rearrange("p (n t) -> p n t", t=128), in0=self.ps[bk][:, 0:384].rearrange("p (n t) -> p n t", t=128),
                    in1=bsb[:, g * 128:(g + 1) * 128].unsqueeze(1).broadcast_to([128, 3, 128]), op=ALU.add))(),
                    reads=[("ps", bks[ti]), "bsb"], writes=[("scr", 2)])
            S.op("pool", (lambda g=g: lambda e: e.tensor_tensor(out=ya[:, g, :], in0=ya[:, g, :], in1=sv[:, :], op=ALU.mult))(),
                 reads=[("scr", 2), ("ya", g)], writes=[("ya", g)])
        mk = [("mm", j) for j in range(FC)]
        S.retire(vtk + k8("vn") + ["wsT", "bsb"], mk)
        yaf, yak = (lambda k: ya[:, k, :]), (lambda k: [("ya", k)])
        ybf, ybk = (lambda k: yb[:, k, :]), (lambda k: [("yb", k)])
        for jb in range(FC):
            def ev_sig(dst, dk):
                def ev(bl):
                    for (p, t0, t1, bkey) in bl:
                        S.op("act", (lambda p=p, o=dst[:, t0:t1]: lambda e: e.activation(out=o, in_=p, func=AF.Sigmoid))(), reads=[bkey], writes=[dk])
                return ev

            def ev_mul(dst, dk, src, skk):
                def ev(bl):
                    for (p, t0, t1, bkey) in bl:
                        S.op("dve", (lambda p=p, o=dst[:, t0:t1], i=src[:, t0:t1]: lambda e: e.tensor_tensor(out=o, in0=i, in1=p, op=ALU.mult))(), reads=[bkey, skk], writes=[dk])
                return ev
            self.project([(w_in, 0, 16, OFF_GA + jb * 128)], hf, hk, ev_sig(self.scr[2], ("scr", 2)))
            self.project([(self.w_ua[l], 0, 8, jb * 128)], yaf, yak, ev_mul(self.scr[3], ("scr", 3), self.scr[2], ("scr", 2)))
            self.project([(w_in, 0, 16, OFF_GB + jb * 128)], hf, hk, ev_sig(self.scr[0], ("scr", 0)))
            self.project([(self.w_ub[l], 0, 8, jb * 128)], ybf, ybk, ev_mul(self.scr[1], ("scr", 1), self.scr[0], ("scr", 0)))
            S.op("pool", (lambda jb=jb: lambda e: e.tensor_tensor(out=mm[:, jb, :], in0=self.scr[3][:, :], in1=self.scr[1][:, :], op=ALU.add))(),
                 reads=[("scr", 3), ("scr", 1)], writes=[("mm", jb)])
        ykeys = [("y", fc) for fc in range(FC)]
        S.retire([("h", fc) for fc in range(FC)], ykeys)
        for jo in range(FC):
            def ev_y(bl, jo=jo):
                for n, (p, t0, t1, bkey) in enumerate(bl):
                    o = self.h[:, jo, t0:t1]
                    if n % 2 == 0:
                        S.op("act", (lambda p=p, o=o: lambda e: e.activation(out=o, in_=p, func=AF.Copy))(), reads=[bkey], writes=[("y", jo)])
                    else:
                        S.op("dve", (lambda p=p, o=o: lambda e: e.tensor_copy(out=o, in_=p))(), reads=[bkey], writes=[("y", jo)])
            self.project([(self.w_out[l], 0, 16, jo * 128)], lambda k: mm[:, k, :], lambda k: [("mm", k)], ev_y)
        S.retire(mk + k8("ya") + k8("yb"), xkeys)
        self.epilogue(l, 5, 3, 1.0)
        S.retire(ykeys, [("h", fc) for fc in range(FC)])


def host_prep(inputs):
    f = lambda a: np.ascontiguousarray(np.asarray(a, dtype=np.float32))
    x, c, ctx, c_ctx = f(inputs["x"]), f(inputs["c"]), f(inputs["ctx"]), f(inputs["c_ctx"])
    consts = np.zeros((128, 192), np.float32)
    consts[:, 0:128] = np.eye(128, dtype=np.float32)
    s = np.arange(32)
    consts[0:32, 128:160] = (s[:, None] <= s[None, :]).astype(np.float32)
    consts[0:32, 160:192] = (s[:, None] >= s[None, :]).astype(np.float32)
    shared = {
        "consts": consts,
        "w_mod": f(inputs["w_mod"]),
        "ffn1_w_gu": f(inputs["ffn1_w_gu"]), "ffn1_w_down": f(inputs["ffn1_w_down"]),
        "ffn2_w_gu": f(inputs["ffn2_w_gu"]), "ffn2_w_down": f(inputs["ffn2_w_down"]),
        "w_in": f(inputs["w_in"]), "w_spatial": f(inputs["w_spatial"]),
        "b_spatial": f(inputs["b_spatial"]).reshape(DEPTH, 1, 1024),
        "w_up_a": f(inputs["w_up_a"]), "w_up_b": f(inputs["w_up_b"]), "w_out": f(inputs["w_out"]),
    }
    in_maps = []
    for r in range(NCORES):
        b, hf = r // 2, r % 2
        vec = np.zeros((V_ROWS, 128), np.float32)
        vec[V_BMOD:V_BMOD + 288] = f(inputs["b_mod"]).reshape(288, 128)
        vec[V_NG:V_NG + 192] = f(inputs["norm_g"]).reshape(192, 128)
        vec[V_CG:V_CG + 16] = f(inputs["chunk_norm_g"]).reshape(16, 128)
        vec[V_HG:V_HG + 16] = f(inputs["hgrn_norm_g"]).reshape(16, 128)
        vec[V_LB:V_LB + 32] = f(inputs["lb_logits"]).reshape(32, 128)
        vec[V_C:V_C + 16] = c[b].reshape(16, 128)
        vec[V_C + 16:V_C + 32] = c_ctx.reshape(16, 128)
        flags = np.zeros((128, 2), np.float32)
        flags[:, 0] = 1.0 if hf == 0 else 0.0
        flags[:, 1] = 0.0 if hf == 0 else 1.0
        m = dict(shared)
        m["x_lat"] = np.ascontiguousarray(x[b, hf * NL:(hf + 1) * NL])
        m["x_ctx"] = np.ascontiguousarray(ctx[b, hf * NCX:(hf + 1) * NCX])
        m["vecs"] = vec
        m["flags"] = flags
        in_maps.append(m)
    return in_maps


def run(inputs, stages=None):
    bld = Builder(stages or {})
    nc = bld.build()
    in_maps = host_prep(inputs)
    res = run_bass_kernel_spmd(nc, in_maps, core_ids=list(range(NCORES)))
    out = np.zeros((4, 2048, D), np.float32)
    for r in range(NCORES):
        b, hf = r // 2, r % 2
        out[b, hf * NL:(hf + 1) * NL] = res.results[r]["out"]
    return out


def kernel(**inputs):
    return run(inputs)
```

```python
import numpy as np
from contextlib import ExitStack
import concourse.bass as bass
import concourse.mybir as mybir
from concourse.bass_utils import run_bass_kernel_spmd

F32 = mybir.dt.float32
BF16 = mybir.dt.bfloat16
AF = mybir.ActivationFunctionType
ALU = mybir.AluOpType
AX = mybir.AxisListType

NCORES = 8
D = 2048
FC = 16
NL = 1024
NCX = 128
NT = NL + NCX
DFF = 5632
NJ = DFF // 128
DEPTH = 2
EPS = 1e-6
TBS = [(0, 384), (384, 768), (768, 1152)]
RNG = [(0, NL), (NL, NT)]
NCH = NT // 32
OFF_U, OFF_V, OFF_Q, OFF_FF, OFF_FB, OFF_I, OFF_G, OFF_GA, OFF_GB = 0, 1024, 2048, 3072, 4096, 5120, 6144, 7168, 9216
V_BMOD, V_NG, V_CG, V_HG, V_LB, V_C, V_ROWS = 0, 288, 480, 496, 512, 544, 640


class Sched:
    ENG = ("pe", "act", "dve", "pool", "sp")

    def __init__(self, nc, sems, dma_sems):
        self.nc = nc
        self.sem = dict(zip(self.ENG, sems))
        self.cnt = {e: 0 for e in self.ENG}
        self.prog = {e: [] for e in self.ENG}
        self.waited = {e: {} for e in self.ENG}
        self.dma_sems = list(dma_sems)
        self.dma_val = [0] * len(self.dma_sems)
        self.dma_rr = 0
        self.last_w = {}
        self.readers = {}
        self.n_ins = 0

    def _need(self, eng, tok, waits):
        if tok is None:
            return
        if tok[0] == "e":
            if tok[1] == eng and eng == "pe":
                return
            key = ("e", tok[1])
        else:
            key = ("d", tok[1])
        val = tok[2]
        if self.waited[eng].get(key, 0) >= val:
            return
        waits[key] = max(waits.get(key, 0), val)

    def _emit_waits(self, eng, waits):
        for key, val in waits.items():
            self.waited[eng][key] = val
            s = self.sem[key[1]] if key[0] == "e" else self.dma_sems[key[1]]
            self.prog[eng].append(("w", s, val))

    def _deps(self, eng, reads, writes, waits):
        for b in reads:
            self._need(eng, self.last_w.get(b), waits)
        for b in writes:
            self._need(eng, self.last_w.get(b), waits)
            for t in self.readers.get(b, ()):
                self._need(eng, t, waits)

    def _record(self, tok, reads, writes):
        for b in reads:
            self.readers.setdefault(b, []).append(tok)
        for b in writes:
            self.last_w[b] = tok
            self.readers[b] = []
        self.n_ins += 1

    def op(self, eng, fn, reads=(), writes=(), signal=True):
        waits = {}
        self._deps(eng, reads, writes, waits)
        self._emit_waits(eng, waits)
        if signal:
            self.cnt[eng] += 1
            tok = ("e", eng, self.cnt[eng])
            self.prog[eng].append(("i", fn, self.sem[eng], 1))
        else:
            tok = ("e", eng, self.cnt[eng] + 1)
            self.prog[eng].append(("i", fn, None, 0))
        self._record(tok, reads, writes)
        return tok

    def dma(self, eng, fn, reads=(), writes=()):
        idx = self.dma_rr
        self.dma_rr = (self.dma_rr + 1) % len(self.dma_sems)
        waits = {}
        if self.dma_val[idx] > 0:
            self._need(eng, ("d", idx, self.dma_val[idx]), waits)
        self._deps(eng, reads, writes, waits)
        self._emit_waits(eng, waits)
        self.dma_val[idx] += 16
        tok = ("d", idx, self.dma_val[idx])
        self.prog[eng].append(("i", fn, self.dma_sems[idx], 16))
        self._record(tok, reads, writes)
        return tok

    def retire(self, old_keys, new_keys):
        toks = []
        for k in old_keys:
            if self.last_w.get(k) is not None:
                toks.append(self.last_w[k])
            toks.extend(self.readers.get(k, ()))
            self.last_w.pop(k, None)
            self.readers.pop(k, None)
        best = {}
        for t in toks:
            key = (t[0], t[1])
            if key not in best or best[key][2] < t[2]:
                best[key] = t
        toks = list(best.values())
        for k in new_keys:
            self.last_w[k] = None
            self.readers[k] = list(toks) + self.readers.get(k, [])

    def final_wait(self, eng):
        waits = {}
        for e in self.ENG:
            if e != eng and self.cnt[e] > 0:
                self._need(eng, ("e", e, self.cnt[e]), waits)
        for i, v in enumerate(self.dma_val):
            if v > 0:
                self._need(eng, ("d", i, v), waits)
        self._emit_waits(eng, waits)

    def flush(self, block):
        engobj = {"pe": "tensor", "act": "scalar", "dve": "vector", "pool": "gpsimd", "sp": "sync"}

        def run(e):
            def body(engine):
                for item in self.prog[e]:
                    if item[0] == "w":
                        engine.wait_ge(item[1], item[2])
                    else:
                        ins = item[1](engine)
                        if item[2] is not None:
                            ins.then_inc(item[2], item[3])
            return body

        for e in self.ENG:
            getattr(block, engobj[e])(run(e))


class Builder:
    def __init__(self, stages):
        self.stages = stages
        self.nc = bass.Bass("TRN2", target_bir_lowering=False)

    def build(self):
        nc = self.nc
        dt = nc.dram_tensor
        self.x_lat = dt("x_lat", [NL, D], F32, kind="ExternalInput").ap()
        self.x_ctx = dt("x_ctx", [NCX, D], F32, kind="ExternalInput").ap()
        self.vecs = dt("vecs", [V_ROWS, 128], F32, kind="ExternalInput").ap()
        self.consts = dt("consts", [128, 192], F32, kind="ExternalInput").ap()
        self.flags = dt("flags", [128, 2], F32, kind="ExternalInput").ap()
        self.w_mod = dt("w_mod", [DEPTH, D, 9 * D], F32, kind="ExternalInput").ap()
        self.w_gu = [dt(f"ffn{i}_w_gu", [DEPTH, D, 2 * DFF], F32, kind="ExternalInput").ap() for i in (1, 2)]
        self.w_dn = [dt(f"ffn{i}_w_down", [DEPTH, DFF, D], F32, kind="ExternalInput").ap() for i in (1, 2)]
        self.w_in = dt("w_in", [DEPTH, D, 11264], F32, kind="ExternalInput").ap()
        self.w_sp = dt("w_spatial", [DEPTH, 8, 128, 128], F32, kind="ExternalInput").ap()
        self.b_sp = dt("b_spatial", [DEPTH, 1, 1024], F32, kind="ExternalInput").ap()
        self.w_ua = dt("w_up_a", [DEPTH, 1024, D], F32, kind="ExternalInput").ap()
        self.w_ub = dt("w_up_b", [DEPTH, 1024, D], F32, kind="ExternalInput").ap()
        self.w_out = dt("w_out", [DEPTH, D, D], F32, kind="ExternalInput").ap()
        self.out = dt("out", [NL, D], F32, kind="ExternalOutput").ap()
        self.xs = dt("xs", [FC, 128, NT], F32).ap()
        self.cc_in = [dt(f"cc_in{g}", [128, 1040], F32).ap() for g in range(4)]
        self.cc_out = [dt(f"cc_out{g}", [256, 1040], F32).ap() for g in range(4)]

        with ExitStack() as st:
            E = st.enter_context
            sb = lambda n, s, d: E(nc.sbuf_tensor(n, s, d))
            self.G = sb("G", [128, 50688], BF16)
            self.H = sb("H", [128, FC * NT], BF16)
            self.NS, self.NB = 4, 7
            self.wst = [sb(f"wst{i}", [128, 1024], F32) for i in range(self.NS)]
            self.wbf = [sb(f"wbf{i}", [128, 1024], BF16) for i in range(self.NB)]
            self.scr = [sb(f"scr{i}", [128, NT], F32) for i in range(4)]
            self.sqb = [sb(f"sqb{i}", [128, NT], BF16) for i in range(2)]
            self.rstd = sb("rstd", [128, NT], F32)
            self.cst = sb("cst", [128, 192], F32)
            self.identb = sb("identb", [128, 128], BF16)
            self.onesb = sb("onesb", [128, 128], BF16)
            self.mkb = sb("mkb", [128, 64], BF16)
            self.m32 = sb("m32", [128, NT], BF16)
            self.mpc = sb("mpc", [128, NT], BF16)
            self.vecT = sb("vecT", [128, V_ROWS], F32)
            self.modT = sb("modT", [128, 144, 2], F32)
            self.cols = sb("cols", [128, 8, FC], F32)
            self.lbc = sb("lbc", [128, 64], F32)
            self.scb = sb("scb", [128, 16, 2], BF16)
            self.flg = sb("flg", [128, 2], F32)
            self.dloc = sb("dloc", [128, 32], F32)
            self.edge = sb("edge", [128, 36], F32)
            self.dcol = sb("dcol", [128, 2, 36], F32)
            self.ps = [E(nc.psum_tensor(f"ps{i}", [128, 512], F32)) for i in range(8)]
            sems = [E(nc.semaphore(f"e{i}")) for i in range(5)]
            dsems = [E(nc.semaphore(f"d{i}")) for i in range(24)]
            self.ccsem = E(nc.semaphore("cc"))
            self.ccval = 0
            block = E(nc.Block())
            self.S = Sched(nc, sems, dsems)
            self.ucount = 0
            self.bank = 0
            self.X = self.G[:, 0:2 * FC * NT].bitcast(F32).rearrange("p (c t) -> p c t", t=NT)
            self.gT = self.G[:, 0:NJ * NT].rearrange("p (c t) -> p c t", t=NT)
            self.h = self.H[:, :].rearrange("p (c t) -> p c t", t=NT)
            self.program()
            self.S.final_wait("sp")
            self.S.flush(block)
        return nc

    def banks(self, n):
        r = [(self.bank + i) % 8 for i in range(n)]
        self.bank = (self.bank + n) % 8
        return r

    def load_unit(self, wap, k0, kc, c0, ncol=128):
        S = self.S
        u = self.ucount
        self.ucount += 1
        si, bi = u % self.NS, u % self.NB
        stg = self.wst[si][:, 0:kc * ncol].rearrange("p (k c) -> p k c", c=ncol)
        wb = self.wbf[bi][:, 0:kc * ncol].rearrange("p (k c) -> p k c", c=ncol)
        src = wap[k0 * 128:(k0 + kc) * 128, c0:c0 + ncol].rearrange("(k p) c -> p k c", p=128)
        S.dma("sp", lambda e: e.dma_start(out=stg, in_=src), writes=[("ws", si)])
        flat_s = self.wst[si][:, 0:kc * ncol]
        flat_b = self.wbf[bi][:, 0:kc * ncol]
        ce = ("act", "dve", "pool", "dve", "act")[u % 5]
        if ce == "act":
            S.op("act", lambda e: e.activation(out=flat_b, in_=flat_s, func=AF.Copy), reads=[("ws", si)], writes=[("wb", bi)])
        else:
            S.op(ce, lambda e: e.tensor_copy(out=flat_b, in_=flat_s), reads=[("ws", si)], writes=[("wb", bi)])
        return wb, ("wb", bi)

    def project(self, units, rhs_fn, rhs_keys, evac, tbs=TBS):
        S = self.S
        bks = self.banks(len(tbs))
        ktot = sum(u[2] for u in units)
        kg = 0
        units = [(wap, k0 + q, min(8, kc - q), c0) for (wap, k0, kc, c0) in units for q in range(0, kc, 8)]
        for (wap, k0, kc, c0) in units:
            wb, wkey = self.load_unit(wap, k0, kc, c0)
            for k in range(kc):
                r = rhs_fn(kg)
                for ti, (t0, t1) in enumerate(tbs):
                    last = (k == kc - 1 and ti == len(tbs) - 1)
                    o = self.ps[bks[ti]][:, 0:t1 - t0]
                    S.op("pe", (lambda o=o, l=wb[:, k, :], rr=r[:, t0:t1], st=(kg == 0), sp=(kg == ktot - 1):
                                lambda e: e.matmul(o, lhsT=l, rhs=rr, start=st, stop=sp))(),
                         reads=[wkey] + list(rhs_keys(kg)), writes=[("ps", bks[ti])], signal=last)
                kg += 1
        evac([(self.ps[bks[ti]][:, 0:t1 - t0], t0, t1, ("ps", bks[ti])) for ti, (t0, t1) in enumerate(tbs)])

    def colstats(self, src_fn, src_keys, nfc, n, out, outkey, eps=EPS):
        S = self.S
        bks = self.banks(3)
        for fc in range(nfc):
            sq = self.sqb[fc % 2]
            sk = ("sqb", fc % 2)
            S.op("act", (lambda s=src_fn(fc), sq=sq: lambda e: e.activation(out=sq[:, :], in_=s, func=AF.Square))(),
                 reads=list(src_keys(fc)), writes=[sk])
            for ti, (t0, t1) in enumerate(TBS):
                S.op("pe", (lambda o=self.ps[bks[ti]][:, 0:t1 - t0], rr=sq[:, t0:t1], st=(fc == 0), sp=(fc == nfc - 1):
                            lambda e: e.matmul(o, lhsT=self.onesb[:, :], rhs=rr, start=st, stop=sp))(),
                     reads=[sk, "onesb"], writes=[("ps", bks[ti])], signal=(ti == 2))
        for ti, (t0, t1) in enumerate(TBS):
            S.op("act", (lambda o=out[:, t0:t1], i=self.ps[bks[ti]][:, 0:t1 - t0]:
                         lambda e: e.activation(out=o, in_=i, func=AF.Sqrt, scale=1.0 / n, bias=eps))(),
                 reads=[("ps", bks[ti])], writes=[outkey])
        S.op("dve", lambda e: e.reciprocal(out=out[:, :], in_=out[:, :]), reads=[outkey], writes=[outkey])

    def mod_cols(self, l, kA, kB, gi, dst, mode):
        S = self.S
        g = self.vecT[:, V_NG + l * 96 + gi * 16: V_NG + l * 96 + gi * 16 + 16]
        for j in range(2):
            m = self.modT[:, kA * 16:(kA + 1) * 16, j]
            o = self.cols[:, dst + j, :]
            if mode == "pre":
                S.op("dve", (lambda o=o, m=m: lambda e: e.scalar_tensor_tensor(out=o, in0=m, scalar=1.0, in1=g, op0=ALU.add, op1=ALU.mult))(),
                     reads=["modT", "vecT"], writes=["cols"])
            else:
                S.op("dve", (lambda o=o, m=m: lambda e: e.scalar_tensor_tensor(out=o, in0=m, scalar=float(kB), in1=g, op0=ALU.mult, op1=ALU.mult))(),
                     reads=["modT", "vecT"], writes=["cols"])

    def prologue(self, l, k_shift, k_scale, gi):
        S = self.S
        self.colstats(lambda fc: self.X[:, fc, :], lambda fc: [("x", fc)], FC, D, self.rstd, "rstd")
        self.mod_cols(l, k_scale, None, gi, 0, "pre")
        for fc in range(FC):
            tmp = self.scr[fc % 2]
            tk = ("scr", fc % 2)
            for j, (t0, t1) in enumerate(RNG):
                S.op("dve", (lambda o=tmp[:, t0:t1], i=self.X[:, fc, t0:t1], a=self.cols[:, j, fc:fc + 1], r=self.rstd[:, t0:t1]:
                             lambda e: e.scalar_tensor_tensor(out=o, in0=i, scalar=a, in1=r, op0=ALU.mult, op1=ALU.mult))(),
                     reads=[("x", fc), "cols", "rstd"], writes=[tk])
                S.op("act", (lambda o=self.h[:, fc, t0:t1], i=tmp[:, t0:t1], b=self.modT[:, k_shift * 16 + fc, j:j + 1]:
                             lambda e: e.activation(out=o, in_=i, func=AF.Identity, bias=b, scale=1.0))(),
                     reads=[tk, "modT"], writes=[("h", fc)])

    def epilogue(self, l, k_gate, gi, weight):
        S = self.S
        self.colstats(lambda fc: self.h[:, fc, :], lambda fc: [("y", fc)], FC, D, self.rstd, "rstd")
        self.mod_cols(l, k_gate, weight, gi, 2, "post")
        for fc in range(FC):
            S.dma("sp", (lambda fc=fc: lambda e: e.dma_start(out=self.X[:, fc, :], in_=self.xs[fc]))(),
                  reads=[("xs", fc)], writes=[("x", fc)])
        for fc in range(FC):
            tmp = self.scr[fc % 2]
            tk = ("scr", fc % 2)
            for j, (t0, t1) in enumerate(RNG):
                S.op("dve", (lambda o=tmp[:, t0:t1], i=self.h[:, fc, t0:t1], a=self.cols[:, 2 + j, fc:fc + 1], r=self.rstd[:, t0:t1]:
                             lambda e: e.scalar_tensor_tensor(out=o, in0=i, scalar=a, in1=r, op0=ALU.mult, op1=ALU.mult))(),
                     reads=[("y", fc), "cols", "rstd"], writes=[tk])
            S.op("pool", (lambda o=self.X[:, fc, :], t=tmp[:, :]: lambda e: e.tensor_tensor(out=o, in0=o, in1=t, op=ALU.add))(),
                 reads=[tk, ("x", fc)], writes=[("x", fc)])
            S.dma("sp", (lambda fc=fc: lambda e: e.dma_start(out=self.xs[fc], in_=self.X[:, fc, :]))(),
                  reads=[("x", fc)], writes=[("xs", fc)])

    def setup(self):
        S = self.S
        S.dma("sp", lambda e: e.dma_start(out=self.cst[:, :], in_=self.consts), writes=["cst"])
        S.dma("sp", lambda e: e.dma_start(out=self.flg[:, :], in_=self.flags), writes=["flg"])
        S.op("dve", lambda e: e.tensor_copy(out=self.identb[:, :], in_=self.cst[:, 0:128]), reads=["cst"], writes=["identb"])
        S.op("dve", lambda e: e.memset(self.onesb[:, :], 1.0), writes=["onesb"])
        S.op("dve", lambda e: e.tensor_copy(out=self.mkb[0:32, :], in_=self.cst[0:32, 128:192]), reads=["cst"], writes=["mkb"])
        for m, step in ((self.m32, 32), (self.mpc, 1024)):
            key = "m32" if step == 32 else "mpc"
            S.op("pool", (lambda m=m: lambda e: e.memset(m[:, :], 1.0))(), writes=[key])
            if step == 32:
                S.op("pool", lambda e: e.memset(self.m32[:, :].rearrange("p (c t) -> p c t", t=32)[:, :, 0:1], 0.0), reads=[key], writes=[key])
            else:
                S.op("pool", lambda e: e.memset(self.mpc[:, 0:1], 0.0), reads=[key], writes=[key])
                S.op("pool", lambda e: e.memset(self.mpc[:, NL:NL + 1], 0.0), reads=[key], writes=[key])
        stg = self.wst[0]
        for i in range(V_ROWS // 128):
            S.dma("sp", (lambda i=i: lambda e: e.dma_start(out=stg[:, i * 128:(i + 1) * 128], in_=self.vecs[i * 128:(i + 1) * 128, :]))(),
                  writes=[("ws", 0)])
        b = self.banks(2)
        for i in range(V_ROWS // 128):
            bk = b[i // 4]
            S.op("pe", (lambda i=i, bk=bk: lambda e: e.transpose(out=self.ps[bk][:, (i % 4) * 128:(i % 4 + 1) * 128], in_=stg[:, i * 128:(i + 1) * 128], identity=self.cst[:, 0:128]))(),
                 reads=[("ws", 0), "cst"], writes=[("ps", bk)])
        S.op("dve", lambda e: e.tensor_copy(out=self.vecT[:, 0:512], in_=self.ps[b[0]][:, 0:512]), reads=[("ps", b[0])], writes=["vecT"])
        S.op("dve", lambda e: e.tensor_copy(out=self.vecT[:, 512:640], in_=self.ps[b[1]][:, 0:128]), reads=[("ps", b[1])], writes=["vecT"])
        S.op("act", lambda e: e.activation(out=self.scb[:, :, :], in_=self.vecT[:, V_C:V_C + 32].rearrange("p (j k) -> p k j", j=2), func=AF.Silu),
             reads=["vecT"], writes=["scb"])
        S.op("dve", lambda e: e.memset(self.lbc[:, 0:16], 0.0), writes=["lbc"])
        S.op("dve", lambda e: e.tensor_tensor(out=self.lbc[:, 16:32], in0=self.vecT[:, V_LB + 16:V_LB + 32], in1=self.vecT[:, V_LB:V_LB + 16], op=ALU.subtract),
             reads=["vecT", "lbc"], writes=["lbc"])
        S.op("act", lambda e: e.activation(out=self.lbc[:, 16:32], in_=self.lbc[:, 16:32], func=AF.Sigmoid), reads=["lbc"], writes=["lbc"])
        S.op("dve", lambda e: e.tensor_scalar(out=self.lbc[:, 32:64], in0=self.lbc[:, 0:32], scalar1=-1.0, scalar2=1.0, op0=ALU.mult, op1=ALU.add),
             reads=["lbc"], writes=["lbc"])

    def load_x(self):
        S = self.S
        Hf = self.H[:, 0:16384].bitcast(F32).rearrange("p (s c) -> p s c", c=2048)
        for tt in range(NT // 128):
            stg = Hf[:, tt % 4, :]
            sk = ("hx", tt % 4)
            src = self.x_lat[tt * 128:(tt + 1) * 128, :] if tt < 8 else self.x_ctx
            S.dma("sp", (lambda stg=stg, src=src: lambda e: e.dma_start(out=stg[:, :], in_=src))(), writes=[sk])
            for q in range(4):
                bk = self.banks(1)[0]
                for i in range(4):
                    fc = q * 4 + i
                    S.op("pe", (lambda bk=bk, i=i, fc=fc, stg=stg: lambda e: e.transpose(out=self.ps[bk][:, i * 128:(i + 1) * 128], in_=stg[:, fc * 128:(fc + 1) * 128], identity=self.cst[:, 0:128]))(),
                         reads=[sk, "cst"], writes=[("ps", bk)], signal=(i == 3))
                eng = "dve" if q % 2 == 0 else "act"
                o = self.X[:, q * 4:(q + 1) * 4, tt * 128:(tt + 1) * 128]
                i_ = self.ps[bk][:, :].rearrange("p (c t) -> p c t", t=128)
                if eng == "dve":
                    S.op("dve", (lambda o=o, i_=i_: lambda e: e.tensor_copy(out=o, in_=i_))(), reads=[("ps", bk)], writes=[("x", q * 4 + i) for i in range(4)])
                else:
                    S.op("act", (lambda o=o, i_=i_: lambda e: e.activation(out=o, in_=i_, func=AF.Copy))(), reads=[("ps", bk)], writes=[("x", q * 4 + i) for i in range(4)])
        for fc in range(FC):
            S.dma("sp", (lambda fc=fc: lambda e: e.dma_start(out=self.xs[fc], in_=self.X[:, fc, :]))(), reads=[("x", fc)], writes=[("xs", fc)])
        S.retire([("hx", i) for i in range(4)], [("h", fc) for fc in range(FC)])

    def store_x(self):
        S = self.S
        Hf = self.H[:, 0:16384].bitcast(F32).rearrange("p (s c) -> p s c", c=2048)
        S.retire([("h", fc) for fc in range(FC)], [("hx", i) for i in range(4)])
        for tt in range(NL // 128):
            stg = Hf[:, tt % 4, :]
            sk = ("hx", tt % 4)
            for q in range(4):
                bk = self.banks(1)[0]
                for i in range(4):
                    fc = q * 4 + i
                    S.op("pe", (lambda bk=bk, i=i, fc=fc, tt=tt: lambda e: e.transpose(out=self.ps[bk][:, i * 128:(i + 1) * 128], in_=self.X[:, fc, tt * 128:(tt + 1) * 128], identity=self.cst[:, 0:128]))(),
                         reads=[("x", fc), "cst"], writes=[("ps", bk)], signal=(i == 3))
                o = stg[:, q * 512:(q + 1) * 512]
                if q % 2 == 0:
                    S.op("dve", (lambda o=o, bk=bk: lambda e: e.tensor_copy(out=o, in_=self.ps[bk][:, :]))(), reads=[("ps", bk)], writes=[sk])
                else:
                    S.op("act", (lambda o=o, bk=bk: lambda e: e.activation(out=o, in_=self.ps[bk][:, :], func=AF.Copy))(), reads=[("ps", bk)], writes=[sk])
            S.dma("sp", (lambda tt=tt, stg=stg: lambda e: e.dma_start(out=self.out[tt * 128:(tt + 1) * 128, :], in_=stg[:, :]))(), reads=[sk], writes=[("out", tt)])

    def compute_mod(self, l):
        S = self.S
        bk = self.banks(1)[0]
        for jb in range(144):
            for k0 in (0, 8):
                wb, wkey = self.load_unit(self.w_mod[l], k0, 8, jb * 128)
                for k in range(k0, k0 + 8):
                    S.op("pe", (lambda k=k, k0=k0, wb=wb, jb=jb: lambda e: e.matmul(self.ps[bk][:, jb * 2:jb * 2 + 2], lhsT=wb[:, k - k0, :], rhs=self.scb[:, k, :], start=(k == 0), stop=(k == 15)))(),
                         reads=[wkey, "scb"], writes=[("ps", bk)], signal=(k == k0 + 7))
        bm = self.vecT[:, V_BMOD + l * 144: V_BMOD + (l + 1) * 144]
        S.op("dve", lambda e: e.tensor_tensor(out=self.modT[:, :, :], in0=self.ps[bk][:, 0:288].rearrange("p (b j) -> p b j", j=2),
                                              in1=bm.unsqueeze(2).broadcast_to([128, 144, 2]), op=ALU.add),
             reads=[("ps", bk), "vecT"], writes=["modT"])

    def ffn(self, l, which, k0, gi0):
        S = self.S
        wgu = self.w_gu[which][l]
        wdn = self.w_dn[which][l]
        self.prologue(l, k0, k0 + 1, gi0)
        S.retire([("x", fc) for fc in range(FC)], [("g", j) for j in range(NJ)])
        hk = lambda k: [("h", k)]
        hf = lambda k: self.h[:, k, :]
        for j in range(NJ):
            s = self.scr[2 + j % 2]
            skey = ("scr", 2 + j % 2)

            def ev_a(bl, s=s, skey=skey):
                for (p, t0, t1, bkey) in bl:
                    S.op("act", (lambda p=p, o=s[:, t0:t1]: lambda e: e.activation(out=o, in_=p, func=AF.Silu))(), reads=[bkey], writes=[skey])

            def ev_b(bl, s=s, skey=skey, j=j):
                for (p, t0, t1, bkey) in bl:
                    S.op("dve", (lambda p=p, o=self.gT[:, j, t0:t1], i=s[:, t0:t1]: lambda e: e.tensor_tensor(out=o, in0=i, in1=p, op=ALU.mult))(),
                         reads=[bkey, skey], writes=[("g", j)])
            self.project([(wgu, 0, 16, j * 128)], hf, hk, ev_a)
            self.project([(wgu, 0, 16, DFF + j * 128)], hf, hk, ev_b)
        S.retire([("h", fc) for fc in range(FC)], [("y", fc) for fc in range(FC)])
        gk = lambda k: [("g", k)]
        gf = lambda k: self.gT[:, k, :]
        for jo in range(FC):
            def ev_y(bl, jo=jo):
                for n, (p, t0, t1, bkey) in enumerate(bl):
                    o = self.h[:, jo, t0:t1]
                    if n % 2 == 0:
                        S.op("act", (lambda p=p, o=o: lambda e: e.activation(out=o, in_=p, func=AF.Copy))(), reads=[bkey], writes=[("y", jo)])
                    else:
                        S.op("dve", (lambda p=p, o=o: lambda e: e.tensor_copy(out=o, in_=p))(), reads=[bkey], writes=[("y", jo)])
            self.project([(wdn, kq * 11, 11, jo * 128) for kq in range(4)], gf, gk, ev_y)
        S.retire([("g", j) for j in range(NJ)], [("x", fc) for fc in range(FC)])
        self.epilogue(l, k0 + 2, gi0 + 1, 0.5)
        S.retire([("y", fc) for fc in range(FC)], [("h", fc) for fc in range(FC)])

    def program(self):
        st = self.stages
        self.setup()
        self.load_x()
        for l in range(DEPTH):
            if l >= st.get("layers", DEPTH):
                break
            self.compute_mod(l)
            self.ffn(l, 0, 0, 0)
            if st.get("mixer", True):
                self.mixer(l)
            if st.get("ffn2", True):
                self.ffn(l, 1, 6, 4)
        self.store_x()


    def gbuf(self, off, n, f32=False):
        ap = self.G[:, off:off + n]
        return (ap.bitcast(F32) if f32 else ap), ("G", off)

    def decay_prep(self, l, d, h, zbanks):
        S = self.S
        idx = l * 16 + d * 8 + h
        s0, s1 = self.scr[0], self.scr[1]
        for (p, t0, t1, bkey) in zbanks:
            S.op("act", (lambda p=p, o=s0[:, t0:t1]: lambda e: e.activation(out=o, in_=p, func=AF.Sigmoid))(), reads=[bkey], writes=[("scr", 0)])
        S.op("dve", lambda e: e.tensor_scalar(out=s0[:, :], in0=s0[:, :], scalar1=self.lbc[:, 32 + idx:33 + idx], scalar2=self.lbc[:, idx:idx + 1], op0=ALU.mult, op1=ALU.add),
             reads=[("scr", 0), "lbc"], writes=[("scr", 0)])
        S.op("dve", lambda e: e.tensor_scalar_max(out=s0[:, :], in0=s0[:, :], scalar1=1e-30), reads=[("scr", 0)], writes=[("scr", 0)])
        S.op("act", lambda e: e.activation(out=s1[:, :], in_=s0[:, :], func=AF.Ln), reads=[("scr", 0)], writes=[("scr", 1)])
        S.op("dve", lambda e: e.tensor_scalar(out=s0[:, :], in0=s0[:, :], scalar1=-1.0, scalar2=1.0, op0=ALU.mult, op1=ALU.add),
             reads=[("scr", 0)], writes=[("scr", 0)])

    def transpose_tiles(self, src, skey, dst, dkey, ncols, npart_out):
        S = self.S
        n = NT // ncols
        per = 8
        for b0 in range(0, n, per):
            nb = min(per, n - b0)
            bk = self.banks(1)[0]
            psb = self.ps[bk][:, :].bitcast(BF16)
            for j in range(nb):
                c = b0 + j
                S.op("pe", (lambda j=j, c=c, psb=psb: lambda e: e.transpose(out=psb[0:npart_out, j * 128:(j + 1) * 128], in_=src[:, c * ncols:(c + 1) * ncols], identity=self.identb[:, :]))(),
                     reads=[skey, "identb"], writes=[("ps", bk)], signal=(j == nb - 1))
            o = dst[0:npart_out, b0:b0 + nb, :]
            i_ = psb[0:npart_out, 0:nb * 128].rearrange("p (n d) -> p n d", d=128)
            if (b0 // per) % 2 == 0:
                S.op("dve", (lambda o=o, i_=i_: lambda e: e.tensor_copy(out=o, in_=i_))(), reads=[("ps", bk)], writes=[dkey])
            else:
                S.op("act", (lambda o=o, i_=i_: lambda e: e.activation(out=o, in_=i_, func=AF.Copy))(), reads=[("ps", bk)], writes=[dkey])

    def hgrn(self, l, xkeys, yb):
        S = self.S
        w_in = self.w_in[l]
        hk = lambda k: [("h", k)]
        hf = lambda k: self.h[:, k, :]
        s0, s1, s2, s3 = self.scr
        K = lambda i: ("scr", i)
        ybk = [("yb", i) for i in range(8)]
        Eloc, kE = self.gbuf(18432, 8192, True)
        Ev = Eloc.rearrange("p (s e) -> p s e", e=128)
        kg = [self.gbuf(26624, NT), self.gbuf(27776, NT)]
        vT, kvT = self.gbuf(28928, NT)
        ktok = [self.gbuf(30080, NT), self.gbuf(31232, NT)]
        vtok, kvtok = self.gbuf(32384, NT)
        p1keys = [kE, kg[0][1], kg[1][1], kvT, ktok[0][1], ktok[1][1], kvtok]
        S.retire(xkeys, p1keys + ybk)
        lastc = [NL - 1, NT - 1]
        for h in range(8):
            for d in range(2):
                got = []
                self.project([(w_in, 0, 16, (OFF_FF if d == 0 else OFF_FB) + h * 128)], hf, hk, lambda bl: got.extend(bl))
                self.decay_prep(l, d, h, got)
                S.op("dve", lambda e: e.tensor_tensor_scan(out=s2[:, :], data0=self.mpc[:, :], data1=s1[:, :], initial=0.0, op0=ALU.mult, op1=ALU.add),
                     reads=[K(1), "mpc"], writes=[K(2)])
                for pc, (t0, t1) in enumerate(RNG):
                    slot = (d * 2 + pc) * 8 + h
                    lc = lastc[pc]
                    S.op("act", (lambda slot=slot, lc=lc: lambda e: e.activation(out=self.dloc[:, slot:slot + 1], in_=s2[:, lc:lc + 1], func=AF.Exp))(),
                         reads=[K(2)], writes=["dloc"])
                if d == 0:
                    for pc, (t0, t1) in enumerate(RNG):
                        lc = lastc[pc]
                        S.op("act", (lambda t0=t0, t1=t1, lc=lc: lambda e: e.activation(out=s3[:, t0:t1], in_=s2[:, t0:t1], func=AF.Exp, scale=-1.0, bias=s2[:, lc:lc + 1]))(),
                             reads=[K(2)], writes=[K(3)])
                else:
                    S.op("dve", lambda e: e.tensor_tensor(out=s3[:, :], in0=s2[:, :], in1=s1[:, :], op=ALU.subtract), reads=[K(2), K(1)], writes=[K(3)])
                    S.op("act", lambda e: e.activation(out=s3[:, :], in_=s3[:, :], func=AF.Exp), reads=[K(3)], writes=[K(3)])
                S.op("dve", (lambda d=d: lambda e: e.tensor_tensor(out=kg[d][0], in0=s0[:, :], in1=s3[:, :], op=ALU.mult))(), reads=[K(0), K(3)], writes=[kg[d][1]])
            got = []
            self.project([(w_in, 0, 16, OFF_I + h * 128)], hf, hk, lambda bl: got.extend(bl))
            for (p, t0, t1, bkey) in got:
                S.op("act", (lambda p=p, o=vT[:, t0:t1]: lambda e: e.activation(out=o, in_=p, func=AF.Copy))(), reads=[bkey], writes=[kvT])
            v3 = lambda ap: ap.rearrange("p (n d) -> p n d", d=128)
            self.transpose_tiles(vT, kvT, v3(vtok), kvtok, 128, 128)
            for d in range(2):
                self.transpose_tiles(kg[d][0], kg[d][1], v3(ktok[d][0]), ktok[d][1], 128, 128)
                bk = self.banks(1)[0]
                kt, vt = v3(ktok[d][0]), v3(vtok)
                for tt in range(9):
                    col = 0 if tt < 8 else 128
                    S.op("pe", (lambda tt=tt, col=col, kt=kt, vt=vt, bk=bk: lambda e: e.matmul(self.ps[bk][:, col:col + 128], lhsT=kt[:, tt, :], rhs=vt[:, tt, :], start=(tt == 0 or tt == 8), stop=(tt == 7 or tt == 8)))(),
                         reads=[ktok[d][1], kvtok], writes=[("ps", bk)], signal=(tt >= 7))
                for pc in range(2):
                    slot = (d * 2 + pc) * 8 + h
                    S.op("dve", (lambda slot=slot, pc=pc, bk=bk: lambda e: e.tensor_copy(out=Ev[:, slot, :], in_=self.ps[bk][:, pc * 128:(pc + 1) * 128]))(),
                         reads=[("ps", bk)], writes=[kE])
        for g in range(4):
            S.dma("sp", (lambda g=g: lambda e: e.dma_start(out=self.cc_in[g][:, 0:1024], in_=Eloc[:, g * 1024:(g + 1) * 1024]))(), reads=[kE], writes=[("cc_in", g)])
            S.dma("sp", (lambda g=g: lambda e: e.dma_start(out=self.cc_in[g][:, 1024:1032], in_=self.dloc[:, g * 8:(g + 1) * 8]))(), reads=["dloc"], writes=[("cc_in", g)])
            S.op("pool", (lambda g=g: lambda e: e.collective_compute("AllGather", ALU.bypass, replica_groups=[[0, 1], [2, 3], [4, 5], [6, 7]], ins=[self.cc_in[g]], outs=[self.cc_out[g]]))(),
                 reads=[("cc_in", g)], writes=[("cc_out", g)])
        EA, kEA = self.gbuf(0, 8320, True)
        EB, kEB = self.gbuf(33536, 8320, True)
        Sin, kSin = self.gbuf(41856, 8192, True)
        S.retire(p1keys, [kEA, kEB, kSin])
        for g in range(4):
            S.dma("sp", (lambda g=g: lambda e: e.dma_start(out=EA[:, g * 1040:(g + 1) * 1040], in_=self.cc_out[g][0:128, :]))(), reads=[("cc_out", g)], writes=[kEA])
            S.dma("sp", (lambda g=g: lambda e: e.dma_start(out=EB[:, g * 1040:(g + 1) * 1040], in_=self.cc_out[g][128:256, :]))(), reads=[("cc_out", g)], writes=[kEB])
        Eg = lambda X, g: X[:, g * 1040:g * 1040 + 1024].rearrange("p (h e) -> p h e", e=128)
        Dg = lambda X, g: X[:, g * 1040 + 1024:g * 1040 + 1032].unsqueeze(2).broadcast_to([128, 8, 128])
        Sg = lambda g: Sin[:, g * 1024:(g + 1) * 1024].rearrange("p (h e) -> p h e", e=128)
        t0v = s0[:, 0:1024].rearrange("p (h e) -> p h e", e=128)
        t1v = s1[:, 0:1024].rearrange("p (h e) -> p h e", e=128)
        fA, fB = self.flg[:, 0:1], self.flg[:, 1:2]
        rk = [kEA, kEB, "flg"]

        def sin_dir(P, Q, kP, kQ, fP, fQ, g_lat, g_ctx):
            S.op("dve", lambda e: e.tensor_tensor(out=t0v, in0=Eg(P, g_ctx), in1=Dg(Q, g_ctx), op=ALU.mult), reads=rk, writes=[K(0)])
            S.op("dve", lambda e: e.tensor_tensor(out=t0v, in0=t0v, in1=Eg(Q, g_ctx), op=ALU.add), reads=rk + [K(0)], writes=[K(0)])
            S.op("dve", lambda e: e.tensor_tensor(out=t1v, in0=t0v, in1=Dg(P, g_lat), op=ALU.mult), reads=rk + [K(0)], writes=[K(1)])
            S.op("dve", lambda e: e.tensor_tensor(out=t1v, in0=t1v, in1=Eg(P, g_lat), op=ALU.add), reads=rk + [K(1)], writes=[K(1)])
            S.op("dve", lambda e: e.tensor_scalar(out=Sg(g_lat), in0=t0v, scalar1=fP, scalar2=None, op0=ALU.mult), reads=rk + [K(0)], writes=[kSin])
            S.op("dve", lambda e: e.scalar_tensor_tensor(out=Sg(g_lat), in0=t1v, scalar=fQ, in1=Sg(g_lat), op0=ALU.mult, op1=ALU.add), reads=rk + [K(1), kSin], writes=[kSin])
            S.op("dve", lambda e: e.tensor_scalar(out=Sg(g_ctx), in0=Eg(P, g_ctx), scalar1=fQ, scalar2=None, op0=ALU.mult), reads=rk, writes=[kSin])
        sin_dir(EA, EB, kEA, kEB, fA, fB, 0, 1)
        sin_dir(EB, EA, kEB, kEA, fB, fA, 2, 3)
        Sbf = [self.gbuf(18432, 4608), self.gbuf(23296, 4608)]
        qX = [self.gbuf(28160, NT), self.gbuf(29312, NT)]
        kX = [self.gbuf(30464, NT), self.gbuf(31616, NT)]
        khat, kkhat = self.gbuf(32768, NT)
        PX = [self.gbuf(33920, NT), self.gbuf(35072, NT)]
        Spp, kSpp = self.gbuf(36224, 512, True)
        khtok, kkhtok = self.gbuf(36736, 4608)
        vt32, kvt32 = self.gbuf(0, 4608)
        T4, kT4 = self.gbuf(4608, 2304, True)
        T5, kT5 = self.gbuf(6912, 2304, True)
        p2keys = [Sbf[0][1], Sbf[1][1], qX[0][1], qX[1][1], kX[0][1], kX[1][1], kkhat, PX[0][1], PX[1][1], kSpp, kkhtok, kvt32, kT4, kT5]
        S.retire([kEA, kEB], p2keys)
        c3 = lambda ap: ap.rearrange("p (c e) -> p c e", e=128)
        vT2, kvT2 = self.sqb[0], ("sqb", 0)
        Sin4 = Sin.rearrange("p (g h e) -> p g h e", h=8, e=128)
        for h in range(8):
            got = []
            self.project([(w_in, 0, 16, OFF_Q + h * 128)], hf, hk, lambda bl: got.extend(bl))
            for (p, t0, t1, bkey) in got:
                S.op("act", (lambda p=p, o=T4[:, t0:t1]: lambda e: e.activation(out=o, in_=p, func=AF.Copy))(), reads=[bkey], writes=[kT4])
            got = []
            self.project([(w_in, 0, 16, OFF_G + h * 128)], hf, hk, lambda bl: got.extend(bl))
            for (p, t0, t1, bkey) in got:
                S.op("act", (lambda p=p, o=T5[:, t0:t1]: lambda e: e.activation(out=o, in_=p, func=AF.Silu))(), reads=[bkey], writes=[kT5])
            got = []
            self.project([(w_in, 0, 16, OFF_I + h * 128)], hf, hk, lambda bl: got.extend(bl))
            for (p, t0, t1, bkey) in got:
                S.op("act", (lambda p=p, o=vT2[:, t0:t1]: lambda e: e.activation(out=o, in_=p, func=AF.Copy))(), reads=[bkey], writes=[kvT2])
            self.transpose_tiles(vT2, kvT2, c3(vt32), kvt32, 32, 32)
            for d in range(2):
                got = []
                self.project([(w_in, 0, 16, (OFF_FF if d == 0 else OFF_FB) + h * 128)], hf, hk, lambda bl: got.extend(bl))
                self.decay_prep(l, d, h, got)
                b3 = lambda ap: ap[:, :].rearrange("p (c t) -> p c t", t=32)
                S.op("dve", lambda e: e.tensor_tensor_scan(out=s2[:, :], data0=self.m32[:, :], data1=s1[:, :], initial=0.0, op0=ALU.mult, op1=ALU.add),
                     reads=[K(1), "m32"], writes=[K(2)])
                if d == 0:
                    S.op("dve", lambda e: e.tensor_copy(out=self.edge[:, :], in_=b3(s2)[:, :, 31]), reads=[K(2)], writes=["edge"])
                else:
                    S.op("dve", lambda e: e.tensor_copy(out=self.edge[:, :], in_=b3(s2)[:, :, 31]), reads=[K(2)], writes=["edge"])
                    S.op("dve", lambda e: e.tensor_tensor(out=s3[:, :], in0=s1[:, :], in1=s2[:, :], op=ALU.subtract), reads=[K(1), K(2)], writes=[K(3)])
                    S.op("dve", lambda e: e.tensor_tensor(out=b3(s2), in0=b3(s3), in1=self.edge[:, :].unsqueeze(2).broadcast_to([128, NCH, 32]), op=ALU.add),
                         reads=[K(3), "edge"], writes=[K(2)])
                S.op("act", (lambda d=d: lambda e: e.activation(out=self.dcol[:, d, :], in_=self.edge[:, :], func=AF.Exp))(), reads=["edge"], writes=[("dcol", d)])
                S.op("act", lambda e: e.activation(out=s3[:, :], in_=s2[:, :], func=AF.Exp), reads=[K(2)], writes=[K(3)])
                S.op("dve", (lambda d=d: lambda e: e.tensor_tensor(out=qX[d][0], in0=T4, in1=s3[:, :], op=ALU.mult))(), reads=[kT4, K(3)], writes=[qX[d][1]])
                S.op("dve", lambda e: e.tensor_tensor(out=b3(s3), in0=b3(s2), in1=self.edge[:, :].unsqueeze(2).broadcast_to([128, NCH, 32]), op=ALU.subtract),
                     reads=[K(2), "edge"], writes=[K(3)])
                S.op("act", lambda e: e.activation(out=s3[:, :], in_=s3[:, :], func=AF.Exp, scale=-1.0), reads=[K(3)], writes=[K(3)])
                S.op("dve", lambda e: e.tensor_tensor(out=khat, in0=s0[:, :], in1=s3[:, :], op=ALU.mult), reads=[K(0), K(3)], writes=[kkhat])
                self.transpose_tiles(khat, kkhat, c3(khtok), kkhtok, 32, 32)
                midi = 15 if d == 0 else 16
                S.op("dve", (lambda midi=midi: lambda e: e.tensor_copy(out=self.edge[:, :], in_=b3(s2)[:, :, midi]))(), reads=[K(2), ("dcol", d), kkhat], writes=["edge"])
                S.op("dve", lambda e: e.tensor_tensor(out=b3(s3), in0=b3(s2), in1=self.edge[:, :].unsqueeze(2).broadcast_to([128, NCH, 32]), op=ALU.subtract),
                     reads=[K(2), "edge"], writes=[K(3)])
                S.op("act", lambda e: e.activation(out=s2[:, :], in_=s3[:, :], func=AF.Exp), reads=[K(3)], writes=[K(2)])
                S.op("dve", lambda e: e.tensor_tensor(out=khat, in0=T4, in1=s2[:, :], op=ALU.mult), reads=[kT4, K(2)], writes=[kkhat])
                S.op("act", lambda e: e.activation(out=s2[:, :], in_=s3[:, :], func=AF.Exp, scale=-1.0), reads=[K(3)], writes=[K(2)])
                S.op("dve", (lambda d=d: lambda e: e.tensor_tensor(out=kX[d][0], in0=s0[:, :], in1=s2[:, :], op=ALU.mult))(), reads=[K(0), K(2)], writes=[kX[d][1]])
                P3 = PX[d][0].rearrange("p (c t) -> p c t", t=32)
                S.op("pool", (lambda d=d: lambda e: e.memset(PX[d][0][0:32, :], 0.0))(), writes=[PX[d][1]])
                msk = self.mkb[0:32, d * 32:(d + 1) * 32].bitcast(mybir.dt.uint16)
                for c0 in range(0, NCH, 16):
                    n = min(16, NCH - c0)
                    bk = self.banks(1)[0]
                    for j in range(n):
                        c = c0 + j
                        S.op("pe", (lambda j=j, c=c, d=d, bk=bk: lambda e: e.matmul(self.ps[bk][0:32, j * 32:(j + 1) * 32], lhsT=kX[d][0][:, c * 32:(c + 1) * 32], rhs=khat[:, c * 32:(c + 1) * 32], start=True, stop=True))(),
                             reads=[kX[d][1], kkhat], writes=[("ps", bk)], signal=(j == n - 1))
                    for j in range(n):
                        c = c0 + j
                        S.op("dve", (lambda c=c, j=j, bk=bk, P3=P3, msk=msk: lambda e: e.copy_predicated(out=P3[0:32, c, :], mask=msk, data=self.ps[bk][0:32, j * 32:(j + 1) * 32]))(),
                             reads=[("ps", bk), "mkb"], writes=[PX[d][1]])
                Sb3 = c3(Sbf[d][0])
                kh3, vt3 = c3(khtok), c3(vt32)
                for pc, (cs, ce) in enumerate([(0, 32), (32, 36)]):
                    order = list(range(cs, ce)) if d == 0 else list(range(ce - 1, cs - 1, -1))
                    state = Sin4[:, d * 2 + pc, h, :]
                    skey = kSin
                    bank_of = {}
                    for n_, c in enumerate(order):
                        if n_ % 4 == 0 and n_ + 1 < len(order):
                            bk = self.banks(1)[0]
                            grp = order[n_:n_ + 4]
                            for j, cc in enumerate(grp):
                                bank_of[cc] = (bk, j)
                                S.op("pe", (lambda j=j, cc=cc, bk=bk, kh3=kh3, vt3=vt3: lambda e: e.matmul(self.ps[bk][:, j * 128:(j + 1) * 128], lhsT=kh3[0:32, cc, :], rhs=vt3[0:32, cc, :], start=True, stop=True))(),
                                     reads=[kkhtok, kvt32], writes=[("ps", bk)], signal=(j == len(grp) - 1))
                        S.op("act", (lambda c=c, state=state, Sb3=Sb3: lambda e: e.activation(out=Sb3[:, c, :], in_=state, func=AF.Copy))(), reads=[skey], writes=[Sbf[d][1]])
                        if n_ + 1 < len(order):
                            bk, j = bank_of[c]
                            nxt = Spp[:, (n_ % 2) * 128:(n_ % 2 + 1) * 128]
                            S.op("dve", (lambda c=c, d=d, state=state, nxt=nxt, bk=bk, j=j: lambda e: e.scalar_tensor_tensor(
                                out=nxt, in0=state, scalar=self.dcol[:, d, c:c + 1], in1=self.ps[bk][:, j * 128:(j + 1) * 128], op0=ALU.mult, op1=ALU.add))(),
                                reads=[skey, ("dcol", d), ("ps", bk)], writes=[kSpp])
                            state, skey = nxt, kSpp
            bks = self.banks(3)
            vt3 = c3(vt32)
            for c in range(NCH):
                bk = bks[c // 12]
                col = (c % 12) * 32
                ops = []
                for d in range(2):
                    ops.append((vt3[0:32, c, :], PX[d][0][0:32, c * 32:(c + 1) * 32], [kvt32, PX[d][1]]))
                    ops.append((c3(Sbf[d][0])[:, c, :], qX[d][0][:, c * 32:(c + 1) * 32], [Sbf[d][1], qX[d][1]]))
                for n_, (lt, rr, rkeys) in enumerate(ops):
                    S.op("pe", (lambda lt=lt, rr=rr, bk=bk, col=col, n_=n_: lambda e: e.matmul(self.ps[bk][:, col:col + 32], lhsT=lt, rhs=rr, start=(n_ == 0), stop=(n_ == 3)))(),
                         reads=rkeys, writes=[("ps", bk)], signal=(n_ == 3 and (c % 12 == 11)))
            for ti, (t0, t1) in enumerate(TBS):
                S.op("act", (lambda ti=ti, t0=t0, t1=t1, bks=bks: lambda e: e.activation(out=s0[:, t0:t1], in_=self.ps[bks[ti]][:, 0:t1 - t0], func=AF.Copy))(),
                     reads=[("ps", bks[ti])], writes=[K(0)])
            self.colstats(lambda fc: s0[:, :], lambda fc: [K(0)], 1, 128, self.rstd, "rstd")
            S.op("dve", (lambda h=h: lambda e: e.scalar_tensor_tensor(out=s1[:, :], in0=s0[:, :], scalar=self.vecT[:, V_HG + l * 8 + h:V_HG + l * 8 + h + 1], in1=self.rstd[:, :], op0=ALU.mult, op1=ALU.mult))(),
                 reads=[K(0), "vecT", "rstd"], writes=[K(1)])
            S.op("pool", (lambda h=h: lambda e: e.tensor_tensor(out=yb[:, h, :], in0=s1[:, :], in1=T5, op=ALU.mult))(), reads=[K(1), kT5], writes=[("yb", h)])
        return p2keys + [kSin]

    def mixer(self, l):
        S = self.S
        G = self.G
        w_in = self.w_in[l]
        self.prologue(l, 3, 4, 2)
        ya = G[:, 0:9216].rearrange("p (c t) -> p c t", t=NT)
        yb = G[:, 9216:18432].rearrange("p (c t) -> p c t", t=NT)
        v32 = G[:, 18432:36864].bitcast(F32).rearrange("p (c t) -> p c t", t=NT)
        vn = G[:, 36864:46080].rearrange("p (c t) -> p c t", t=NT)
        wsT = G[:, 46080:47104].rearrange("p (g t) -> p g t", t=128)
        bsb = G[:, 47104:49152].bitcast(F32)
        vtok = G[:, 18432:27648].rearrange("p (n d) -> p n d", d=1024)
        mm = G[:, 18432:36864].rearrange("p (c t) -> p c t", t=NT)
        k8 = lambda n: [(n, i) for i in range(8)]
        xkeys = [("x", fc) for fc in range(FC)]
        hk = lambda k: [("h", k)]
        hf = lambda k: self.h[:, k, :]
        if self.stages.get("hgrn", True):
            hkeys = self.hgrn(l, xkeys, yb)
            S.retire(hkeys, k8("ya") + k8("v") + k8("vn") + ["wsT", "bsb"])
        else:
            S.retire(xkeys, k8("ya") + k8("yb") + k8("v") + k8("vn") + ["wsT", "bsb"])
            for i in range(8):
                S.op("pool", (lambda i=i: lambda e: e.memset(yb[:, i, :], 0.0))(), writes=[("yb", i)])
        for fc in range(8):
            def ev_u(bl, fc=fc):
                for (p, t0, t1, bkey) in bl:
                    S.op("act", (lambda p=p, o=ya[:, fc, t0:t1]: lambda e: e.activation(out=o, in_=p, func=AF.Gelu))(), reads=[bkey], writes=[("ya", fc)])
            self.project([(w_in, 0, 16, OFF_U + fc * 128)], hf, hk, ev_u)
        for fc in range(8):
            def ev_v(bl, fc=fc):
                for (p, t0, t1, bkey) in bl:
                    S.op("act", (lambda p=p, o=v32[:, fc, t0:t1]: lambda e: e.activation(out=o, in_=p, func=AF.Gelu))(), reads=[bkey], writes=[("v", fc)])
            self.project([(w_in, 0, 16, OFF_V + fc * 128)], hf, hk, ev_v)
        bks = self.banks(3)
        for fc in range(8):
            sq = self.sqb[fc % 2]
            sk = ("sqb", fc % 2)
            S.op("act", (lambda fc=fc, sq=sq: lambda e: e.activation(out=sq[:, :], in_=v32[:, fc, :], func=AF.Copy))(), reads=[("v", fc)], writes=[sk])
            for ti, (t0, t1) in enumerate(TBS):
                S.op("pe", (lambda o=self.ps[bks[ti]][:, 0:t1 - t0], rr=sq[:, t0:t1], st=(fc == 0), sp=(fc == 7):
                            lambda e: e.matmul(o, lhsT=self.onesb[:, :], rhs=rr, start=st, stop=sp))(),
                     reads=[sk, "onesb"], writes=[("ps", bks[ti])], signal=(ti == 2))
        mean = self.scr[0]
        for ti, (t0, t1) in enumerate(TBS):
            S.op("act", (lambda o=mean[:, t0:t1], i=self.ps[bks[ti]][:, 0:t1 - t0]: lambda e: e.activation(out=o, in_=i, func=AF.Identity, scale=1.0 / 1024, bias=0.0))(),
                 reads=[("ps", bks[ti])], writes=[("scr", 0)])
        for fc in range(8):
            S.op("dve", (lambda fc=fc: lambda e: e.tensor_tensor(out=v32[:, fc, :], in0=v32[:, fc, :], in1=mean[:, :], op=ALU.subtract))(),
                 reads=[("v", fc), ("scr", 0)], writes=[("v", fc)])
        self.colstats(lambda fc: v32[:, fc, :], lambda fc: [("v", fc)], 8, 1024, self.rstd, "rstd")
        for fc in range(8):
            S.op("dve", (lambda fc=fc: lambda e: e.scalar_tensor_tensor(out=vn[:, fc, :], in0=v32[:, fc, :], scalar=self.vecT[:, V_CG + l * 8 + fc:V_CG + l * 8 + fc + 1], in1=self.rstd[:, :], op0=ALU.mult, op1=ALU.mult))(),
                 reads=[("v", fc), "vecT", "rstd"], writes=[("vn", fc)])
        vtk = [("vtok", tt) for tt in range(9)]
        S.retire(k8("v"), vtk)
        for tt in range(9):
            bk = self.banks(1)[0]
            psb = self.ps[bk][:, :].bitcast(BF16)
            for g in range(8):
                S.op("pe", (lambda g=g, tt=tt, psb=psb: lambda e: e.transpose(out=psb[:, g * 128:(g + 1) * 128], in_=vn[:, g, tt * 128:(tt + 1) * 128], identity=self.identb[:, :]))(),
                     reads=[("vn", g), "identb"], writes=[("ps", bk)], signal=(g == 7))
            if tt % 2 == 0:
                S.op("dve", (lambda tt=tt, psb=psb: lambda e: e.tensor_copy(out=vtok[:, tt, :], in_=psb[:, 0:1024]))(), reads=[("ps", bk)], writes=[("vtok", tt)])
            else:
                S.op("act", (lambda tt=tt, psb=psb: lambda e: e.activation(out=vtok[:, tt, :], in_=psb[:, 0:1024], func=AF.Copy))(), reads=[("ps", bk)], writes=[("vtok", tt)])
        wstg = self.scr[1][:, 0:1024].rearrange("p (g s) -> p g s", s=128)
        S.dma("sp", lambda e: e.dma_start(out=wstg, in_=self.w_sp[l].rearrange("g t s -> t g s")), writes=[("scr", 1)])
        S.dma("sp", lambda e: e.dma_start(out=bsb, in_=self.b_sp[l].partition_broadcast(128)), writes=["bsb"])
        b2 = self.banks(2)
        for g in range(8):
            bk = b2[g // 4]
            S.op("pe", (lambda g=g, bk=bk: lambda e: e.transpose(out=self.ps[bk][:, (g % 4) * 128:(g % 4 + 1) * 128], in_=wstg[:, g, :], identity=self.cst[:, 0:128]))(),
                 reads=[("scr", 1), "cst"], writes=[("ps", bk)], signal=(g % 4 == 3))
        for q in range(2):
            S.op("dve", (lambda q=q: lambda e: e.tensor_copy(out=wsT[:, q * 4:(q + 1) * 4, :], in_=self.ps[b2[q]][:, :].rearrange("p (g t) -> p g t", t=128)))(),
                 reads=[("ps", b2[q])], writes=["wsT"])
        for g in range(8):
            bks = self.banks(3)
            for tt in range(9):
                S.op("pe", (lambda g=g, tt=tt, bk=bks[tt // 3]: lambda e: e.matmul(self.ps[bk][:, (tt % 3) * 128:(tt % 3 + 1) * 128], lhsT=vtok[:, tt, g * 128:(g + 1) * 128], rhs=wsT[:, g, :], start=True, stop=True))(),
                     reads=[("vtok", tt), "wsT"], writes=[("ps", bks[tt // 3])], signal=(tt % 3 == 2))
            sv = self.scr[2]
            for ti, (t0, t1) in enumerate(TBS):
                S.op("dve", (lambda g=g, bk=bks[ti], t0=t0, t1=t1: lambda e: e.tensor_tensor(
                    out=sv[:, t0:t1].rearrange("p (n t) -> p n t", t=128), in0=self.ps[bk][:, 0:384].rearrange("p (n t) -> p n t", t=128),
                    in1=bsb[:, g * 128:(g + 1) * 128].unsqueeze(1).broadcast_to([128, 3, 128]), op=ALU.add))(),
                    reads=[("ps", bks[ti]), "bsb"], writes=[("scr", 2)])
            S.op("pool", (lambda g=g: lambda e: e.tensor_tensor(out=ya[:, g, :], in0=ya[:, g, :], in1=sv[:, :], op=ALU.mult))(),
                 reads=[("scr", 2), ("ya", g)], writes=[("ya", g)])
        mk = [("mm", j) for j in range(FC)]
        S.retire(vtk + k8("vn") + ["wsT", "bsb"], mk)
        yaf, yak = (lambda k: ya[:, k, :]), (lambda k: [("ya", k)])
        ybf, ybk = (lambda k: yb[:, k, :]), (lambda k: [("yb", k)])
        for jb in range(FC):
            def ev_sig(dst, dk):
                def ev(bl):
                    for (p, t0, t1, bkey) in bl:
                        S.op("act", (lambda p=p, o=dst[:, t0:t1]: lambda e: e.activation(out=o, in_=p, func=AF.Sigmoid))(), reads=[bkey], writes=[dk])
                return ev

            def ev_mul(dst, dk, src, skk):
                def ev(bl):
                    for (p, t0, t1, bkey) in bl:
                        S.op("dve", (lambda p=p, o=dst[:, t0:t1], i=src[:, t0:t1]: lambda e: e.tensor_tensor(out=o, in0=i, in1=p, op=ALU.mult))(), reads=[bkey, skk], writes=[dk])
                return ev
            self.project([(w_in, 0, 16, OFF_GA + jb * 128)], hf, hk, ev_sig(self.scr[2], ("scr", 2)))
            self.project([(self.w_ua[l], 0, 8, jb * 128)], yaf, yak, ev_mul(self.scr[3], ("scr", 3), self.scr[2], ("scr", 2)))
            self.project([(w_in, 0, 16, OFF_GB + jb * 128)], hf, hk, ev_sig(self.scr[0], ("scr", 0)))
            self.project([(self.w_ub[l], 0, 8, jb * 128)], ybf, ybk, ev_mul(self.scr[1], ("scr", 1), self.scr[0], ("scr", 0)))
            S.op("pool", (lambda jb=jb: lambda e: e.tensor_tensor(out=mm[:, jb, :], in0=self.scr[3][:, :], in1=self.scr[1][:, :], op=ALU.add))(),
                 reads=[("scr", 3), ("scr", 1)], writes=[("mm", jb)])
        ykeys = [("y", fc) for fc in range(FC)]
        S.retire([("h", fc) for fc in range(FC)], ykeys)
        for jo in range(FC):
            def ev_y(bl, jo=jo):
                for n, (p, t0, t1, bkey) in enumerate(bl):
                    o = self.h[:, jo, t0:t1]
                    if n % 2 == 0:
                        S.op("act", (lambda p=p, o=o: lambda e: e.activation(out=o, in_=p, func=AF.Copy))(), reads=[bkey], writes=[("y", jo)])
                    else:
                        S.op("dve", (lambda p=p, o=o: lambda e: e.tensor_copy(out=o, in_=p))(), reads=[bkey], writes=[("y", jo)])
            self.project([(self.w_out[l], 0, 16, jo * 128)], lambda k: mm[:, k, :], lambda k: [("mm", k)], ev_y)
        S.retire(mk + k8("ya") + k8("yb"), xkeys)
        self.epilogue(l, 5, 3, 1.0)
        S.retire(ykeys, [("h", fc) for fc in range(FC)])


def host_prep(inputs):
    f = lambda a: np.ascontiguousarray(np.asarray(a, dtype=np.float32))
    x, c, ctx, c_ctx = f(inputs["x"]), f(inputs["c"]), f(inputs["ctx"]), f(inputs["c_ctx"])
    consts = np.zeros((128, 192), np.float32)
    consts[:, 0:128] = np.eye(128, dtype=np.float32)
    s = np.arange(32)
    consts[0:32, 128:160] = (s[:, None] <= s[None, :]).astype(np.float32)
    consts[0:32, 160:192] = (s[:, None] >= s[None, :]).astype(np.float32)
    shared = {
        "consts": consts,
        "w_mod": f(inputs["w_mod"]),
        "ffn1_w_gu": f(inputs["ffn1_w_gu"]), "ffn1_w_down": f(inputs["ffn1_w_down"]),
        "ffn2_w_gu": f(inputs["ffn2_w_gu"]), "ffn2_w_down": f(inputs["ffn2_w_down"]),
        "w_in": f(inputs["w_in"]), "w_spatial": f(inputs["w_spatial"]),
        "b_spatial": f(inputs["b_spatial"]).reshape(DEPTH, 1, 1024),
        "w_up_a": f(inputs["w_up_a"]), "w_up_b": f(inputs["w_up_b"]), "w_out": f(inputs["w_out"]),
    }
    in_maps = []
    for r in range(NCORES):
        b, hf = r // 2, r % 2
        vec = np.zeros((V_ROWS, 128), np.float32)
        vec[V_BMOD:V_BMOD + 288] = f(inputs["b_mod"]).reshape(288, 128)
        vec[V_NG:V_NG + 192] = f(inputs["norm_g"]).reshape(192, 128)
        vec[V_CG:V_CG + 16] = f(inputs["chunk_norm_g"]).reshape(16, 128)
        vec[V_HG:V_HG + 16] = f(inputs["hgrn_norm_g"]).reshape(16, 128)
        vec[V_LB:V_LB + 32] = f(inputs["lb_logits"]).reshape(32, 128)
        vec[V_C:V_C + 16] = c[b].reshape(16, 128)
        vec[V_C + 16:V_C + 32] = c_ctx.reshape(16, 128)
        flags = np.zeros((128, 2), np.float32)
        flags[:, 0] = 1.0 if hf == 0 else 0.0
        flags[:, 1] = 0.0 if hf == 0 else 1.0
        m = dict(shared)
        m["x_lat"] = np.ascontiguousarray(x[b, hf * NL:(hf + 1) * NL])
        m["x_ctx"] = np.ascontiguousarray(ctx[b, hf * NCX:(hf + 1) * NCX])
        m["vecs"] = vec
        m["flags"] = flags
        in_maps.append(m)
    return in_maps


def run(inputs, stages=None):
    bld = Builder(stages or {})
    nc = bld.build()
    in_maps = host_prep(inputs)
    res = run_bass_kernel_spmd(nc, in_maps, core_ids=list(range(NCORES)))
    out = np.zeros((4, 2048, D), np.float32)
    for r in range(NCORES):
        b, hf = r // 2, r % 2
        out[b, hf * NL:(hf + 1) * NL] = res.results[r]["out"]
    return out


def kernel(**inputs):
    return run(inputs)
```

```python
import numpy as np
from contextlib import ExitStack
import concourse.bass as bass
import concourse.mybir as mybir
from concourse.bass_utils import run_bass_kernel_spmd

F32 = mybir.dt.float32
BF16 = mybir.dt.bfloat16
AF = mybir.ActivationFunctionType
ALU = mybir.AluOpType
AX = mybir.AxisListType

NCORES = 8
D = 2048
FC = 16
NL = 1024
NCX = 128
NT = NL + NCX
DFF = 5632
NJ = DFF // 128
DEPTH = 2
EPS = 1e-6
TBS = [(0, 384), (384, 768), (768, 1152)]
RNG = [(0, NL), (NL, NT)]
NCH = NT // 32
OFF_U, OFF_V, OFF_Q, OFF_FF, OFF_FB, OFF_I, OFF_G, OFF_GA, OFF_GB = 0, 1024, 2048, 3072, 4096, 5120, 6144, 7168, 9216
V_BMOD, V_NG, V_CG, V_HG, V_LB, V_C, V_ROWS = 0, 288, 480, 496, 512, 544, 640


class Sched:
    ENG = ("pe", "act", "dve", "pool", "sp")

    def __init__(self, nc, sems, dma_sems):
        self.nc = nc
        self.sem = dict(zip(self.ENG, sems))
        self.cnt = {e: 0 for e in self.ENG}
        self.prog = {e: [] for e in self.ENG}
        self.waited = {e: {} for e in self.ENG}
        self.dma_sems = list(dma_sems)
        self.dma_val = [0] * len(self.dma_sems)
        self.dma_rr = 0
        self.last_w = {}
        self.readers = {}
        self.n_ins = 0

    def _need(self, eng, tok, waits):
        if tok is None:
            return
        if tok[0] == "e":
            if tok[1] == eng and eng == "pe":
                return
            key = ("e", tok[1])
        else:
            key = ("d", tok[1])
        val = tok[2]
        if self.waited[eng].get(key, 0) >= val:
            return
        waits[key] = max(waits.get(key, 0), val)

    def _emit_waits(self, eng, waits):
        for key, val in waits.items():
            self.waited[eng][key] = val
            s = self.sem[key[1]] if key[0] == "e" else self.dma_sems[key[1]]
            self.prog[eng].append(("w", s, val))

    def _deps(self, eng, reads, writes, waits):
        for b in reads:
            self._need(eng, self.last_w.get(b), waits)
        for b in writes:
            self._need(eng, self.last_w.get(b), waits)
            for t in self.readers.get(b, ()):
                self._need(eng, t, waits)

    def _record(self, tok, reads, writes):
        for b in reads:
            self.readers.setdefault(b, []).append(tok)
        for b in writes:
            self.last_w[b] = tok
            self.readers[b] = []
        self.n_ins += 1

    def op(self, eng, fn, reads=(), writes=(), signal=True):
        waits = {}
        self._deps(eng, reads, writes, waits)
        self._emit_waits(eng, waits)
        if signal:
            self.cnt[eng] += 1
            tok = ("e", eng, self.cnt[eng])
            self.prog[eng].append(("i", fn, self.sem[eng], 1))
        else:
            tok = ("e", eng, self.cnt[eng] + 1)
            self.prog[eng].append(("i", fn, None, 0))
        self._record(tok, reads, writes)
        return tok

    def dma(self, eng, fn, reads=(), writes=()):
        idx = self.dma_rr
        self.dma_rr = (self.dma_rr + 1) % len(self.dma_sems)
        waits = {}
        if self.dma_val[idx] > 0:
            self._need(eng, ("d", idx, self.dma_val[idx]), waits)
        self._deps(eng, reads, writes, waits)
        self._emit_waits(eng, waits)
        self.dma_val[idx] += 16
        tok = ("d", idx, self.dma_val[idx])
        self.prog[eng].append(("i", fn, self.dma_sems[idx], 16))
        self._record(tok, reads, writes)
        return tok

    def retire(self, old_keys, new_keys):
        toks = []
        for k in old_keys:
            if self.last_w.get(k) is not None:
                toks.append(self.last_w[k])
            toks.extend(self.readers.get(k, ()))
            self.last_w.pop(k, None)
            self.readers.pop(k, None)
        best = {}
        for t in toks:
            key = (t[0], t[1])
            if key not in best or best[key][2] < t[2]:
                best[key] = t
        toks = list(best.values())
        for k in new_keys:
            self.last_w[k] = None
            self.readers[k] = list(toks) + self.readers.get(k, [])

    def final_wait(self, eng):
        waits = {}
        for e in self.ENG:
            if e != eng and self.cnt[e] > 0:
                self._need(eng, ("e", e, self.cnt[e]), waits)
        for i, v in enumerate(self.dma_val):
            if v > 0:
                self._need(eng, ("d", i, v), waits)
        self._emit_waits(eng, waits)

    def flush(self, block):
        engobj = {"pe": "tensor", "act": "scalar", "dve": "vector", "pool": "gpsimd", "sp": "sync"}

        def run(e):
            def body(engine):
                for item in self.prog[e]:
                    if item[0] == "w":
                        engine.wait_ge(item[1], item[2])
                    else:
                        ins = item[1](engine)
                        if item[2] is not None:
                            ins.then_inc(item[2], item[3])
            return body

        for e in self.ENG:
            getattr(block, engobj[e])(run(e))


class Builder:
    def __init__(self, stages, plan=None):
        self.stages = stages
        self.nc = bass.Bass("TRN2", target_bir_lowering=False)
        self.plan = plan
        self.record = []
        self.req_idx = 0
        self.issued = 0
        self.loaded = {}

    def build(self):
        nc = self.nc
        dt = nc.dram_tensor
        self.x_lat = dt("x_lat", [NL, D], F32, kind="ExternalInput").ap()
        self.x_ctx = dt("x_ctx", [NCX, D], F32, kind="ExternalInput").ap()
        self.vecs = dt("vecs", [V_ROWS, 128], F32, kind="ExternalInput").ap()
        self.consts = dt("consts", [128, 192], F32, kind="ExternalInput").ap()
        self.flags = dt("flags", [128, 2], F32, kind="ExternalInput").ap()
        self.w_mod = dt("w_mod", [DEPTH, D, 9 * D], F32, kind="ExternalInput").ap()
        self.w_gu = [dt(f"ffn{i}_w_gu", [DEPTH, D, 2 * DFF], F32, kind="ExternalInput").ap() for i in (1, 2)]
        self.w_dn = [dt(f"ffn{i}_w_down", [DEPTH, DFF, D], F32, kind="ExternalInput").ap() for i in (1, 2)]
        self.w_in = dt("w_in", [DEPTH, D, 11264], F32, kind="ExternalInput").ap()
        self.w_sp = dt("w_spatial", [DEPTH, 8, 128, 128], F32, kind="ExternalInput").ap()
        self.b_sp = dt("b_spatial", [DEPTH, 1, 1024], F32, kind="ExternalInput").ap()
        self.w_ua = dt("w_up_a", [DEPTH, 1024, D], F32, kind="ExternalInput").ap()
        self.w_ub = dt("w_up_b", [DEPTH, 1024, D], F32, kind="ExternalInput").ap()
        self.w_out = dt("w_out", [DEPTH, D, D], F32, kind="ExternalInput").ap()
        self.out = dt("out", [NL, D], F32, kind="ExternalOutput").ap()
        self.wmap = {"w_mod": self.w_mod, "w_gu0": self.w_gu[0], "w_gu1": self.w_gu[1], "w_dn0": self.w_dn[0], "w_dn1": self.w_dn[1],
                     "w_in": self.w_in, "w_ua": self.w_ua, "w_ub": self.w_ub, "w_out": self.w_out}
        self.xs = dt("xs", [FC, 128, NT], F32).ap()
        self.cc_in = [dt(f"cc_in{g}", [128, 1040], F32).ap() for g in range(4)]
        self.cc_out = [dt(f"cc_out{g}", [256, 1040], F32).ap() for g in range(4)]

        with ExitStack() as st:
            E = st.enter_context
            sb = lambda n, s, d: E(nc.sbuf_tensor(n, s, d))
            self.G = sb("G", [128, 50688], BF16)
            self.H = sb("H", [128, FC * NT], BF16)
            self.NS, self.NB = 4, 7
            self.wst = [sb(f"wst{i}", [128, 1024], F32) for i in range(self.NS)]
            self.wbf = [sb(f"wbf{i}", [128, 1024], BF16) for i in range(self.NB)]
            self.scr = [sb(f"scr{i}", [128, NT], F32) for i in range(4)]
            self.sqb = [sb(f"sqb{i}", [128, NT], BF16) for i in range(2)]
            self.rstd = sb("rstd", [128, NT], F32)
            self.cst = sb("cst", [128, 192], F32)
            self.identb = sb("identb", [128, 128], BF16)
            self.onesb = sb("onesb", [128, 128], BF16)
            self.mkb = sb("mkb", [128, 64], BF16)
            self.m32 = sb("m32", [128, NT], BF16)
            self.mpc = sb("mpc", [128, NT], BF16)
            self.vecT = sb("vecT", [128, V_ROWS], F32)
            self.modT = sb("modT", [128, 144, 2], F32)
            self.cols = sb("cols", [128, 8, FC], F32)
            self.lbc = sb("lbc", [128, 64], F32)
            self.scb = sb("scb", [128, 16, 2], BF16)
            self.flg = sb("flg", [128, 2], F32)
            self.dloc = sb("dloc", [128, 32], F32)
            self.edge = sb("edge", [128, 36], F32)
            self.dcol = sb("dcol", [128, 2, 36], F32)
            self.ps = [E(nc.psum_tensor(f"ps{i}", [128, 512], F32)) for i in range(8)]
            sems = [E(nc.semaphore(f"e{i}")) for i in range(5)]
            dsems = [E(nc.semaphore(f"d{i}")) for i in range(24)]
            self.ccsem = E(nc.semaphore("cc"))
            self.ccval = 0
            block = E(nc.Block())
            self.S = Sched(nc, sems, dsems)
            self.ucount = 0
            self.bank = 0
            self.X = self.G[:, 0:2 * FC * NT].bitcast(F32).rearrange("p (c t) -> p c t", t=NT)
            self.gT = self.G[:, 0:NJ * NT].rearrange("p (c t) -> p c t", t=NT)
            self.h = self.H[:, :].rearrange("p (c t) -> p c t", t=NT)
            self.program()
            self.S.final_wait("sp")
            self.S.flush(block)
        return nc

    def banks(self, n):
        r = [(self.bank + i) % 8 for i in range(n)]
        self.bank = (self.bank + n) % 8
        return r

    def get_unit(self, req):
        if self.plan is None:
            self.record.append(req)
            return self.load_unit(*req)
        i = self.req_idx
        assert self.plan[i] == req, (i, self.plan[i], req)
        P = self.NB - 2
        while self.issued <= min(len(self.plan) - 1, i + P):
            self.loaded[self.issued] = self.load_unit(*self.plan[self.issued])
            self.issued += 1
        self.req_idx += 1
        return self.loaded.pop(i)

    def load_unit(self, wref, k0, kc, c0, ncol=128):
        S = self.S
        wap = self.wmap[wref[0]][wref[1]]
        u = self.ucount
        self.ucount += 1
        si, bi = u % self.NS, u % self.NB
        stg = self.wst[si][:, 0:kc * ncol].rearrange("p (k c) -> p k c", c=ncol)
        wb = self.wbf[bi][:, 0:kc * ncol].rearrange("p (k c) -> p k c", c=ncol)
        src = wap[k0 * 128:(k0 + kc) * 128, c0:c0 + ncol].rearrange("(k p) c -> p k c", p=128)
        S.dma("sp", lambda e: e.dma_start(out=stg, in_=src), writes=[("ws", si)])
        flat_s = self.wst[si][:, 0:kc * ncol]
        flat_b = self.wbf[bi][:, 0:kc * ncol]
        ce = ("act", "dve", "pool", "dve", "act")[u % 5]
        if ce == "act":
            S.op("act", lambda e: e.activation(out=flat_b, in_=flat_s, func=AF.Copy), reads=[("ws", si)], writes=[("wb", bi)])
        else:
            S.op(ce, lambda e: e.tensor_copy(out=flat_b, in_=flat_s), reads=[("ws", si)], writes=[("wb", bi)])
        return wb, ("wb", bi)

    def project(self, units, rhs_fn, rhs_keys, evac, tbs=TBS):
        S = self.S
        bks = self.banks(len(tbs))
        ktot = sum(u[2] for u in units)
        kg = 0
        units = [(wap, k0 + q, min(8, kc - q), c0) for (wap, k0, kc, c0) in units for q in range(0, kc, 8)]
        for (wap, k0, kc, c0) in units:
            wb, wkey = self.get_unit((wap, k0, kc, c0))
            for k in range(kc):
                r = rhs_fn(kg)
                for ti, (t0, t1) in enumerate(tbs):
                    last = (k == kc - 1 and ti == len(tbs) - 1)
                    o = self.ps[bks[ti]][:, 0:t1 - t0]
                    S.op("pe", (lambda o=o, l=wb[:, k, :], rr=r[:, t0:t1], st=(kg == 0), sp=(kg == ktot - 1):
                                lambda e: e.matmul(o, lhsT=l, rhs=rr, start=st, stop=sp))(),
                         reads=[wkey] + list(rhs_keys(kg)), writes=[("ps", bks[ti])], signal=last)
                kg += 1
        evac([(self.ps[bks[ti]][:, 0:t1 - t0], t0, t1, ("ps", bks[ti])) for ti, (t0, t1) in enumerate(tbs)])

    def colstats(self, src_fn, src_keys, nfc, n, out, outkey, eps=EPS):
        S = self.S
        bks = self.banks(3)
        for fc in range(nfc):
            sq = self.sqb[fc % 2]
            sk = ("sqb", fc % 2)
            S.op("act", (lambda s=src_fn(fc), sq=sq: lambda e: e.activation(out=sq[:, :], in_=s, func=AF.Square))(),
                 reads=list(src_keys(fc)), writes=[sk])
            for ti, (t0, t1) in enumerate(TBS):
                S.op("pe", (lambda o=self.ps[bks[ti]][:, 0:t1 - t0], rr=sq[:, t0:t1], st=(fc == 0), sp=(fc == nfc - 1):
                            lambda e: e.matmul(o, lhsT=self.onesb[:, :], rhs=rr, start=st, stop=sp))(),
                     reads=[sk, "onesb"], writes=[("ps", bks[ti])], signal=(ti == 2))
        for ti, (t0, t1) in enumerate(TBS):
            S.op("act", (lambda o=out[:, t0:t1], i=self.ps[bks[ti]][:, 0:t1 - t0]:
                         lambda e: e.activation(out=o, in_=i, func=AF.Sqrt, scale=1.0 / n, bias=eps))(),
                 reads=[("ps", bks[ti])], writes=[outkey])
        S.op("dve", lambda e: e.reciprocal(out=out[:, :], in_=out[:, :]), reads=[outkey], writes=[outkey])

    def mod_cols(self, l, kA, kB, gi, dst, mode):
        S = self.S
        g = self.vecT[:, V_NG + l * 96 + gi * 16: V_NG + l * 96 + gi * 16 + 16]
        for j in range(2):
            m = self.modT[:, kA * 16:(kA + 1) * 16, j]
            o = self.cols[:, dst + j, :]
            if mode == "pre":
                S.op("dve", (lambda o=o, m=m: lambda e: e.scalar_tensor_tensor(out=o, in0=m, scalar=1.0, in1=g, op0=ALU.add, op1=ALU.mult))(),
                     reads=["modT", "vecT"], writes=["cols"])
            else:
                S.op("dve", (lambda o=o, m=m: lambda e: e.scalar_tensor_tensor(out=o, in0=m, scalar=float(kB), in1=g, op0=ALU.mult, op1=ALU.mult))(),
                     reads=["modT", "vecT"], writes=["cols"])

    def prologue(self, l, k_shift, k_scale, gi):
        S = self.S
        self.colstats(lambda fc: self.X[:, fc, :], lambda fc: [("x", fc)], FC, D, self.rstd, "rstd")
        self.mod_cols(l, k_scale, None, gi, 0, "pre")
        for fc in range(FC):
            tmp = self.scr[fc % 2]
            tk = ("scr", fc % 2)
            for j, (t0, t1) in enumerate(RNG):
                S.op("dve", (lambda o=tmp[:, t0:t1], i=self.X[:, fc, t0:t1], a=self.cols[:, j, fc:fc + 1], r=self.rstd[:, t0:t1]:
                             lambda e: e.scalar_tensor_tensor(out=o, in0=i, scalar=a, in1=r, op0=ALU.mult, op1=ALU.mult))(),
                     reads=[("x", fc), "cols", "rstd"], writes=[tk])
                S.op("act", (lambda o=self.h[:, fc, t0:t1], i=tmp[:, t0:t1], b=self.modT[:, k_shift * 16 + fc, j:j + 1]:
                             lambda e: e.activation(out=o, in_=i, func=AF.Identity, bias=b, scale=1.0))(),
                     reads=[tk, "modT"], writes=[("h", fc)])

    def epilogue(self, l, k_gate, gi, weight):
        S = self.S
        self.colstats(lambda fc: self.h[:, fc, :], lambda fc: [("y", fc)], FC, D, self.rstd, "rstd")
        self.mod_cols(l, k_gate, weight, gi, 2, "post")
        for fc in range(FC):
            S.dma("sp", (lambda fc=fc: lambda e: e.dma_start(out=self.X[:, fc, :], in_=self.xs[fc]))(),
                  reads=[("xs", fc)], writes=[("x", fc)])
        for fc in range(FC):
            tmp = self.scr[fc % 2]
            tk = ("scr", fc % 2)
            for j, (t0, t1) in enumerate(RNG):
                S.op("dve", (lambda o=tmp[:, t0:t1], i=self.h[:, fc, t0:t1], a=self.cols[:, 2 + j, fc:fc + 1], r=self.rstd[:, t0:t1]:
                             lambda e: e.scalar_tensor_tensor(out=o, in0=i, scalar=a, in1=r, op0=ALU.mult, op1=ALU.mult))(),
                     reads=[("y", fc), "cols", "rstd"], writes=[tk])
            S.op("pool", (lambda o=self.X[:, fc, :], t=tmp[:, :]: lambda e: e.tensor_tensor(out=o, in0=o, in1=t, op=ALU.add))(),
                 reads=[tk, ("x", fc)], writes=[("x", fc)])
            S.dma("sp", (lambda fc=fc: lambda e: e.dma_start(out=self.xs[fc], in_=self.X[:, fc, :]))(),
                  reads=[("x", fc)], writes=[("xs", fc)])

    def setup(self):
        S = self.S
        S.dma("sp", lambda e: e.dma_start(out=self.cst[:, :], in_=self.consts), writes=["cst"])
        S.dma("sp", lambda e: e.dma_start(out=self.flg[:, :], in_=self.flags), writes=["flg"])
        S.op("dve", lambda e: e.tensor_copy(out=self.identb[:, :], in_=self.cst[:, 0:128]), reads=["cst"], writes=["identb"])
        S.op("dve", lambda e: e.memset(self.onesb[:, :], 1.0), writes=["onesb"])
        S.op("dve", lambda e: e.tensor_copy(out=self.mkb[0:32, :], in_=self.cst[0:32, 128:192]), reads=["cst"], writes=["mkb"])
        for m, step in ((self.m32, 32), (self.mpc, 1024)):
            key = "m32" if step == 32 else "mpc"
            S.op("pool", (lambda m=m: lambda e: e.memset(m[:, :], 1.0))(), writes=[key])
            if step == 32:
                S.op("pool", lambda e: e.memset(self.m32[:, :].rearrange("p (c t) -> p c t", t=32)[:, :, 0:1], 0.0), reads=[key], writes=[key])
            else:
                S.op("pool", lambda e: e.memset(self.mpc[:, 0:1], 0.0), reads=[key], writes=[key])
                S.op("pool", lambda e: e.memset(self.mpc[:, NL:NL + 1], 0.0), reads=[key], writes=[key])
        stg = self.wst[0]
        for i in range(V_ROWS // 128):
            S.dma("sp", (lambda i=i: lambda e: e.dma_start(out=stg[:, i * 128:(i + 1) * 128], in_=self.vecs[i * 128:(i + 1) * 128, :]))(),
                  writes=[("ws", 0)])
        b = self.banks(2)
        for i in range(V_ROWS // 128):
            bk = b[i // 4]
            S.op("pe", (lambda i=i, bk=bk: lambda e: e.transpose(out=self.ps[bk][:, (i % 4) * 128:(i % 4 + 1) * 128], in_=stg[:, i * 128:(i + 1) * 128], identity=self.cst[:, 0:128]))(),
                 reads=[("ws", 0), "cst"], writes=[("ps", bk)])
        S.op("dve", lambda e: e.tensor_copy(out=self.vecT[:, 0:512], in_=self.ps[b[0]][:, 0:512]), reads=[("ps", b[0])], writes=["vecT"])
        S.op("dve", lambda e: e.tensor_copy(out=self.vecT[:, 512:640], in_=self.ps[b[1]][:, 0:128]), reads=[("ps", b[1])], writes=["vecT"])
        S.op("act", lambda e: e.activation(out=self.scb[:, :, :], in_=self.vecT[:, V_C:V_C + 32].rearrange("p (j k) -> p k j", j=2), func=AF.Silu),
             reads=["vecT"], writes=["scb"])
        S.op("dve", lambda e: e.memset(self.lbc[:, 0:16], 0.0), writes=["lbc"])
        S.op("dve", lambda e: e.tensor_tensor(out=self.lbc[:, 16:32], in0=self.vecT[:, V_LB + 16:V_LB + 32], in1=self.vecT[:, V_LB:V_LB + 16], op=ALU.subtract),
             reads=["vecT", "lbc"], writes=["lbc"])
        S.op("act", lambda e: e.activation(out=self.lbc[:, 16:32], in_=self.lbc[:, 16:32], func=AF.Sigmoid), reads=["lbc"], writes=["lbc"])
        S.op("dve", lambda e: e.tensor_scalar(out=self.lbc[:, 32:64], in0=self.lbc[:, 0:32], scalar1=-1.0, scalar2=1.0, op0=ALU.mult, op1=ALU.add),
             reads=["lbc"], writes=["lbc"])

    def load_x(self):
        S = self.S
        Hf = self.H[:, 0:16384].bitcast(F32).rearrange("p (s c) -> p s c", c=2048)
        for tt in range(NT // 128):
            stg = Hf[:, tt % 4, :]
            sk = ("hx", tt % 4)
            src = self.x_lat[tt * 128:(tt + 1) * 128, :] if tt < 8 else self.x_ctx
            S.dma("sp", (lambda stg=stg, src=src: lambda e: e.dma_start(out=stg[:, :], in_=src))(), writes=[sk])
            for q in range(4):
                bk = self.banks(1)[0]
                for i in range(4):
                    fc = q * 4 + i
                    S.op("pe", (lambda bk=bk, i=i, fc=fc, stg=stg: lambda e: e.transpose(out=self.ps[bk][:, i * 128:(i + 1) * 128], in_=stg[:, fc * 128:(fc + 1) * 128], identity=self.cst[:, 0:128]))(),
                         reads=[sk, "cst"], writes=[("ps", bk)], signal=(i == 3))
                eng = "dve" if q % 2 == 0 else "act"
                o = self.X[:, q * 4:(q + 1) * 4, tt * 128:(tt + 1) * 128]
                i_ = self.ps[bk][:, :].rearrange("p (c t) -> p c t", t=128)
                if eng == "dve":
                    S.op("dve", (lambda o=o, i_=i_: lambda e: e.tensor_copy(out=o, in_=i_))(), reads=[("ps", bk)], writes=[("x", q * 4 + i) for i in range(4)])
                else:
                    S.op("act", (lambda o=o, i_=i_: lambda e: e.activation(out=o, in_=i_, func=AF.Copy))(), reads=[("ps", bk)], writes=[("x", q * 4 + i) for i in range(4)])
        for fc in range(FC):
            S.dma("sp", (lambda fc=fc: lambda e: e.dma_start(out=self.xs[fc], in_=self.X[:, fc, :]))(), reads=[("x", fc)], writes=[("xs", fc)])
        S.retire([("hx", i) for i in range(4)], [("h", fc) for fc in range(FC)])

    def store_x(self):
        S = self.S
        Hf = self.H[:, 0:16384].bitcast(F32).rearrange("p (s c) -> p s c", c=2048)
        S.retire([("h", fc) for fc in range(FC)], [("hx", i) for i in range(4)])
        for tt in range(NL // 128):
            stg = Hf[:, tt % 4, :]
            sk = ("hx", tt % 4)
            for q in range(4):
                bk = self.banks(1)[0]
                for i in range(4):
                    fc = q * 4 + i
                    S.op("pe", (lambda bk=bk, i=i, fc=fc, tt=tt: lambda e: e.transpose(out=self.ps[bk][:, i * 128:(i + 1) * 128], in_=self.X[:, fc, tt * 128:(tt + 1) * 128], identity=self.cst[:, 0:128]))(),
                         reads=[("x", fc), "cst"], writes=[("ps", bk)], signal=(i == 3))
                o = stg[:, q * 512:(q + 1) * 512]
                if q % 2 == 0:
                    S.op("dve", (lambda o=o, bk=bk: lambda e: e.tensor_copy(out=o, in_=self.ps[bk][:, :]))(), reads=[("ps", bk)], writes=[sk])
                else:
                    S.op("act", (lambda o=o, bk=bk: lambda e: e.activation(out=o, in_=self.ps[bk][:, :], func=AF.Copy))(), reads=[("ps", bk)], writes=[sk])
            S.dma("sp", (lambda tt=tt, stg=stg: lambda e: e.dma_start(out=self.out[tt * 128:(tt + 1) * 128, :], in_=stg[:, :]))(), reads=[sk], writes=[("out", tt)])

    def compute_mod(self, l):
        S = self.S
        bk = self.banks(1)[0]
        for jb in range(144):
            for k0 in (0, 8):
                wb, wkey = self.get_unit((("w_mod", l), k0, 8, jb * 128))
                for k in range(k0, k0 + 8):
                    S.op("pe", (lambda k=k, k0=k0, wb=wb, jb=jb: lambda e: e.matmul(self.ps[bk][:, jb * 2:jb * 2 + 2], lhsT=wb[:, k - k0, :], rhs=self.scb[:, k, :], start=(k == 0), stop=(k == 15)))(),
                         reads=[wkey, "scb"], writes=[("ps", bk)], signal=(k == k0 + 7))
        bm = self.vecT[:, V_BMOD + l * 144: V_BMOD + (l + 1) * 144]
        S.op("dve", lambda e: e.tensor_tensor(out=self.modT[:, :, :], in0=self.ps[bk][:, 0:288].rearrange("p (b j) -> p b j", j=2),
                                              in1=bm.unsqueeze(2).broadcast_to([128, 144, 2]), op=ALU.add),
             reads=[("ps", bk), "vecT"], writes=["modT"])

    def ffn(self, l, which, k0, gi0):
        S = self.S
        wgu = ("w_gu%d" % which, l)
        wdn = ("w_dn%d" % which, l)
        self.prologue(l, k0, k0 + 1, gi0)
        S.retire([("x", fc) for fc in range(FC)], [("g", j) for j in range(NJ)])
        hk = lambda k: [("h", k)]
        hf = lambda k: self.h[:, k, :]
        for j in range(NJ):
            s = self.scr[2 + j % 2]
            skey = ("scr", 2 + j % 2)

            def ev_a(bl, s=s, skey=skey):
                for (p, t0, t1, bkey) in bl:
                    S.op("act", (lambda p=p, o=s[:, t0:t1]: lambda e: e.activation(out=o, in_=p, func=AF.Silu))(), reads=[bkey], writes=[skey])

            def ev_b(bl, s=s, skey=skey, j=j):
                for (p, t0, t1, bkey) in bl:
                    S.op("dve", (lambda p=p, o=self.gT[:, j, t0:t1], i=s[:, t0:t1]: lambda e: e.tensor_tensor(out=o, in0=i, in1=p, op=ALU.mult))(),
                         reads=[bkey, skey], writes=[("g", j)])
            self.project([(wgu, 0, 16, j * 128)], hf, hk, ev_a)
            self.project([(wgu, 0, 16, DFF + j * 128)], hf, hk, ev_b)
        S.retire([("h", fc) for fc in range(FC)], [("y", fc) for fc in range(FC)])
        gk = lambda k: [("g", k)]
        gf = lambda k: self.gT[:, k, :]
        for jo in range(FC):
            def ev_y(bl, jo=jo):
                for n, (p, t0, t1, bkey) in enumerate(bl):
                    o = self.h[:, jo, t0:t1]
                    if n % 2 == 0:
                        S.op("act", (lambda p=p, o=o: lambda e: e.activation(out=o, in_=p, func=AF.Copy))(), reads=[bkey], writes=[("y", jo)])
                    else:
                        S.op("dve", (lambda p=p, o=o: lambda e: e.tensor_copy(out=o, in_=p))(), reads=[bkey], writes=[("y", jo)])
            self.project([(wdn, kq * 11, 11, jo * 128) for kq in range(4)], gf, gk, ev_y)
        S.retire([("g", j) for j in range(NJ)], [("x", fc) for fc in range(FC)])
        self.epilogue(l, k0 + 2, gi0 + 1, 0.5)
        S.retire([("y", fc) for fc in range(FC)], [("h", fc) for fc in range(FC)])

    def program(self):
        st = self.stages
        self.setup()
        self.load_x()
        for l in range(DEPTH):
            if l >= st.get("layers", DEPTH):
                break
            self.compute_mod(l)
            self.ffn(l, 0, 0, 0)
            if st.get("mixer", True):
                self.mixer(l)
            if st.get("ffn2", True):
                self.ffn(l, 1, 6, 4)
        self.store_x()


    def gbuf(self, off, n, f32=False):
        ap = self.G[:, off:off + n]
        return (ap.bitcast(F32) if f32 else ap), ("G", off)

    def decay_prep(self, l, d, h, zbanks):
        S = self.S
        idx = l * 16 + d * 8 + h
        s0, s1 = self.scr[0], self.scr[1]
        for (p, t0, t1, bkey) in zbanks:
            S.op("act", (lambda p=p, o=s0[:, t0:t1]: lambda e: e.activation(out=o, in_=p, func=AF.Sigmoid))(), reads=[bkey], writes=[("scr", 0)])
        S.op("dve", lambda e: e.tensor_scalar(out=s0[:, :], in0=s0[:, :], scalar1=self.lbc[:, 32 + idx:33 + idx], scalar2=self.lbc[:, idx:idx + 1], op0=ALU.mult, op1=ALU.add),
             reads=[("scr", 0), "lbc"], writes=[("scr", 0)])
        S.op("dve", lambda e: e.tensor_scalar_max(out=s0[:, :], in0=s0[:, :], scalar1=1e-30), reads=[("scr", 0)], writes=[("scr", 0)])
        S.op("act", lambda e: e.activation(out=s1[:, :], in_=s0[:, :], func=AF.Ln), reads=[("scr", 0)], writes=[("scr", 1)])
        S.op("dve", lambda e: e.tensor_scalar(out=s0[:, :], in0=s0[:, :], scalar1=-1.0, scalar2=1.0, op0=ALU.mult, op1=ALU.add),
             reads=[("scr", 0)], writes=[("scr", 0)])

    def transpose_tiles(self, src, skey, dst, dkey, ncols, npart_out):
        S = self.S
        n = NT // ncols
        per = 8
        for b0 in range(0, n, per):
            nb = min(per, n - b0)
            bk = self.banks(1)[0]
            psb = self.ps[bk][:, :].bitcast(BF16)
            for j in range(nb):
                c = b0 + j
                S.op("pe", (lambda j=j, c=c, psb=psb: lambda e: e.transpose(out=psb[0:npart_out, j * 128:(j + 1) * 128], in_=src[:, c * ncols:(c + 1) * ncols], identity=self.identb[:, :]))(),
                     reads=[skey, "identb"], writes=[("ps", bk)], signal=(j == nb - 1))
            o = dst[0:npart_out, b0:b0 + nb, :]
            i_ = psb[0:npart_out, 0:nb * 128].rearrange("p (n d) -> p n d", d=128)
            if (b0 // per) % 2 == 0:
                S.op("dve", (lambda o=o, i_=i_: lambda e: e.tensor_copy(out=o, in_=i_))(), reads=[("ps", bk)], writes=[dkey])
            else:
                S.op("act", (lambda o=o, i_=i_: lambda e: e.activation(out=o, in_=i_, func=AF.Copy))(), reads=[("ps", bk)], writes=[dkey])

    def hgrn(self, l, xkeys, yb):
        S = self.S
        w_in = ("w_in", l)
        hk = lambda k: [("h", k)]
        hf = lambda k: self.h[:, k, :]
        s0, s1, s2, s3 = self.scr
        K = lambda i: ("scr", i)
        ybk = [("yb", i) for i in range(8)]
        Eloc, kE = self.gbuf(18432, 8192, True)
        Ev = Eloc.rearrange("p (s e) -> p s e", e=128)
        kg = [self.gbuf(26624, NT), self.gbuf(27776, NT)]
        vT, kvT = self.gbuf(28928, NT)
        ktok = [self.gbuf(30080, NT), self.gbuf(31232, NT)]
        vtok, kvtok = self.gbuf(32384, NT)
        p1keys = [kE, kg[0][1], kg[1][1], kvT, ktok[0][1], ktok[1][1], kvtok]
        S.retire(xkeys, p1keys + ybk)
        lastc = [NL - 1, NT - 1]
        for h in range(8):
            for d in range(2):
                got = []
                self.project([(w_in, 0, 16, (OFF_FF if d == 0 else OFF_FB) + h * 128)], hf, hk, lambda bl: got.extend(bl))
                self.decay_prep(l, d, h, got)
                S.op("dve", lambda e: e.tensor_tensor_scan(out=s2[:, :], data0=self.mpc[:, :], data1=s1[:, :], initial=0.0, op0=ALU.mult, op1=ALU.add),
                     reads=[K(1), "mpc"], writes=[K(2)])
                for pc, (t0, t1) in enumerate(RNG):
                    slot = (d * 2 + pc) * 8 + h
                    lc = lastc[pc]
                    S.op("act", (lambda slot=slot, lc=lc: lambda e: e.activation(out=self.dloc[:, slot:slot + 1], in_=s2[:, lc:lc + 1], func=AF.Exp))(),
                         reads=[K(2)], writes=["dloc"])
                if d == 0:
                    for pc, (t0, t1) in enumerate(RNG):
                        lc = lastc[pc]
                        S.op("act", (lambda t0=t0, t1=t1, lc=lc: lambda e: e.activation(out=s3[:, t0:t1], in_=s2[:, t0:t1], func=AF.Exp, scale=-1.0, bias=s2[:, lc:lc + 1]))(),
                             reads=[K(2)], writes=[K(3)])
                else:
                    S.op("dve", lambda e: e.tensor_tensor(out=s3[:, :], in0=s2[:, :], in1=s1[:, :], op=ALU.subtract), reads=[K(2), K(1)], writes=[K(3)])
                    S.op("act", lambda e: e.activation(out=s3[:, :], in_=s3[:, :], func=AF.Exp), reads=[K(3)], writes=[K(3)])
                S.op("dve", (lambda d=d: lambda e: e.tensor_tensor(out=kg[d][0], in0=s0[:, :], in1=s3[:, :], op=ALU.mult))(), reads=[K(0), K(3)], writes=[kg[d][1]])
            got = []
            self.project([(w_in, 0, 16, OFF_I + h * 128)], hf, hk, lambda bl: got.extend(bl))
            for (p, t0, t1, bkey) in got:
                S.op("act", (lambda p=p, o=vT[:, t0:t1]: lambda e: e.activation(out=o, in_=p, func=AF.Copy))(), reads=[bkey], writes=[kvT])
            v3 = lambda ap: ap.rearrange("p (n d) -> p n d", d=128)
            self.transpose_tiles(vT, kvT, v3(vtok), kvtok, 128, 128)
            for d in range(2):
                self.transpose_tiles(kg[d][0], kg[d][1], v3(ktok[d][0]), ktok[d][1], 128, 128)
                bk = self.banks(1)[0]
                kt, vt = v3(ktok[d][0]), v3(vtok)
                for tt in range(9):
                    col = 0 if tt < 8 else 128
                    S.op("pe", (lambda tt=tt, col=col, kt=kt, vt=vt, bk=bk: lambda e: e.matmul(self.ps[bk][:, col:col + 128], lhsT=kt[:, tt, :], rhs=vt[:, tt, :], start=(tt == 0 or tt == 8), stop=(tt == 7 or tt == 8)))(),
                         reads=[ktok[d][1], kvtok], writes=[("ps", bk)], signal=(tt >= 7))
                for pc in range(2):
                    slot = (d * 2 + pc) * 8 + h
                    S.op("dve", (lambda slot=slot, pc=pc, bk=bk: lambda e: e.tensor_copy(out=Ev[:, slot, :], in_=self.ps[bk][:, pc * 128:(pc + 1) * 128]))(),
                         reads=[("ps", bk)], writes=[kE])
        for g in range(4):
            S.dma("sp", (lambda g=g: lambda e: e.dma_start(out=self.cc_in[g][:, 0:1024], in_=Eloc[:, g * 1024:(g + 1) * 1024]))(), reads=[kE], writes=[("cc_in", g)])
            S.dma("sp", (lambda g=g: lambda e: e.dma_start(out=self.cc_in[g][:, 1024:1032], in_=self.dloc[:, g * 8:(g + 1) * 8]))(), reads=["dloc"], writes=[("cc_in", g)])
            S.op("pool", (lambda g=g: lambda e: e.collective_compute("AllGather", ALU.bypass, replica_groups=[[0, 1], [2, 3], [4, 5], [6, 7]], ins=[self.cc_in[g]], outs=[self.cc_out[g]]))(),
                 reads=[("cc_in", g)], writes=[("cc_out", g)])
        EA, kEA = self.gbuf(0, 8320, True)
        EB, kEB = self.gbuf(33536, 8320, True)
        Sin, kSin = self.gbuf(41856, 8192, True)
        S.retire(p1keys, [kEA, kEB, kSin])
        for g in range(4):
            S.dma("sp", (lambda g=g: lambda e: e.dma_start(out=EA[:, g * 1040:(g + 1) * 1040], in_=self.cc_out[g][0:128, :]))(), reads=[("cc_out", g)], writes=[kEA])
            S.dma("sp", (lambda g=g: lambda e: e.dma_start(out=EB[:, g * 1040:(g + 1) * 1040], in_=self.cc_out[g][128:256, :]))(), reads=[("cc_out", g)], writes=[kEB])
        Eg = lambda X, g: X[:, g * 1040:g * 1040 + 1024].rearrange("p (h e) -> p h e", e=128)
        Dg = lambda X, g: X[:, g * 1040 + 1024:g * 1040 + 1032].unsqueeze(2).broadcast_to([128, 8, 128])
        Sg = lambda g: Sin[:, g * 1024:(g + 1) * 1024].rearrange("p (h e) -> p h e", e=128)
        t0v = s0[:, 0:1024].rearrange("p (h e) -> p h e", e=128)
        t1v = s1[:, 0:1024].rearrange("p (h e) -> p h e", e=128)
        fA, fB = self.flg[:, 0:1], self.flg[:, 1:2]
        rk = [kEA, kEB, "flg"]

        def sin_dir(P, Q, kP, kQ, fP, fQ, g_lat, g_ctx):
            S.op("dve", lambda e: e.tensor_tensor(out=t0v, in0=Eg(P, g_ctx), in1=Dg(Q, g_ctx), op=ALU.mult), reads=rk, writes=[K(0)])
            S.op("dve", lambda e: e.tensor_tensor(out=t0v, in0=t0v, in1=Eg(Q, g_ctx), op=ALU.add), reads=rk + [K(0)], writes=[K(0)])
            S.op("dve", lambda e: e.tensor_tensor(out=t1v, in0=t0v, in1=Dg(P, g_lat), op=ALU.mult), reads=rk + [K(0)], writes=[K(1)])
            S.op("dve", lambda e: e.tensor_tensor(out=t1v, in0=t1v, in1=Eg(P, g_lat), op=ALU.add), reads=rk + [K(1)], writes=[K(1)])
            S.op("dve", lambda e: e.tensor_scalar(out=Sg(g_lat), in0=t0v, scalar1=fP, scalar2=None, op0=ALU.mult), reads=rk + [K(0)], writes=[kSin])
            S.op("dve", lambda e: e.scalar_tensor_tensor(out=Sg(g_lat), in0=t1v, scalar=fQ, in1=Sg(g_lat), op0=ALU.mult, op1=ALU.add), reads=rk + [K(1), kSin], writes=[kSin])
            S.op("dve", lambda e: e.tensor_scalar(out=Sg(g_ctx), in0=Eg(P, g_ctx), scalar1=fQ, scalar2=None, op0=ALU.mult), reads=rk, writes=[kSin])
        sin_dir(EA, EB, kEA, kEB, fA, fB, 0, 1)
        sin_dir(EB, EA, kEB, kEA, fB, fA, 2, 3)
        Sbf = [self.gbuf(18432, 4608), self.gbuf(23296, 4608)]
        qX = [self.gbuf(28160, NT), self.gbuf(29312, NT)]
        kX = [self.gbuf(30464, NT), self.gbuf(31616, NT)]
        khat, kkhat = self.gbuf(32768, NT)
        PX = [self.gbuf(33920, NT), self.gbuf(35072, NT)]
        Spp, kSpp = self.gbuf(36224, 512, True)
        khtok, kkhtok = self.gbuf(36736, 4608)
        vt32, kvt32 = self.gbuf(0, 4608)
        T4, kT4 = self.gbuf(4608, 2304, True)
        T5, kT5 = self.gbuf(6912, 2304, True)
        p2keys = [Sbf[0][1], Sbf[1][1], qX[0][1], qX[1][1], kX[0][1], kX[1][1], kkhat, PX[0][1], PX[1][1], kSpp, kkhtok, kvt32, kT4, kT5]
        S.retire([kEA, kEB], p2keys)
        c3 = lambda ap: ap.rearrange("p (c e) -> p c e", e=128)
        vT2, kvT2 = self.sqb[0], ("sqb", 0)
        Sin4 = Sin.rearrange("p (g h e) -> p g h e", h=8, e=128)
        for h in range(8):
            got = []
            self.project([(w_in, 0, 16, OFF_Q + h * 128)], hf, hk, lambda bl: got.extend(bl))
            for (p, t0, t1, bkey) in got:
                S.op("act", (lambda p=p, o=T4[:, t0:t1]: lambda e: e.activation(out=o, in_=p, func=AF.Copy))(), reads=[bkey], writes=[kT4])
            got = []
            self.project([(w_in, 0, 16, OFF_G + h * 128)], hf, hk, lambda bl: got.extend(bl))
            for (p, t0, t1, bkey) in got:
                S.op("act", (lambda p=p, o=T5[:, t0:t1]: lambda e: e.activation(out=o, in_=p, func=AF.Silu))(), reads=[bkey], writes=[kT5])
            got = []
            self.project([(w_in, 0, 16, OFF_I + h * 128)], hf, hk, lambda bl: got.extend(bl))
            for (p, t0, t1, bkey) in got:
                S.op("act", (lambda p=p, o=vT2[:, t0:t1]: lambda e: e.activation(out=o, in_=p, func=AF.Copy))(), reads=[bkey], writes=[kvT2])
            self.transpose_tiles(vT2, kvT2, c3(vt32), kvt32, 32, 32)
            for d in range(2):
                got = []
                self.project([(w_in, 0, 16, (OFF_FF if d == 0 else OFF_FB) + h * 128)], hf, hk, lambda bl: got.extend(bl))
                self.decay_prep(l, d, h, got)
                b3 = lambda ap: ap[:, :].rearrange("p (c t) -> p c t", t=32)
                S.op("dve", lambda e: e.tensor_tensor_scan(out=s2[:, :], data0=self.m32[:, :], data1=s1[:, :], initial=0.0, op0=ALU.mult, op1=ALU.add),
                     reads=[K(1), "m32"], writes=[K(2)])
                if d == 0:
                    S.op("dve", lambda e: e.tensor_copy(out=self.edge[:, :], in_=b3(s2)[:, :, 31]), reads=[K(2)], writes=["edge"])
                else:
                    S.op("dve", lambda e: e.tensor_copy(out=self.edge[:, :], in_=b3(s2)[:, :, 31]), reads=[K(2)], writes=["edge"])
                    S.op("dve", lambda e: e.tensor_tensor(out=s3[:, :], in0=s1[:, :], in1=s2[:, :], op=ALU.subtract), reads=[K(1), K(2)], writes=[K(3)])
                    S.op("dve", lambda e: e.tensor_tensor(out=b3(s2), in0=b3(s3), in1=self.edge[:, :].unsqueeze(2).broadcast_to([128, NCH, 32]), op=ALU.add),
                         reads=[K(3), "edge"], writes=[K(2)])
                S.op("act", (lambda d=d: lambda e: e.activation(out=self.dcol[:, d, :], in_=self.edge[:, :], func=AF.Exp))(), reads=["edge"], writes=[("dcol", d)])
                S.op("act", lambda e: e.activation(out=s3[:, :], in_=s2[:, :], func=AF.Exp), reads=[K(2)], writes=[K(3)])
                S.op("dve", (lambda d=d: lambda e: e.tensor_tensor(out=qX[d][0], in0=T4, in1=s3[:, :], op=ALU.mult))(), reads=[kT4, K(3)], writes=[qX[d][1]])
                S.op("dve", lambda e: e.tensor_tensor(out=b3(s3), in0=b3(s2), in1=self.edge[:, :].unsqueeze(2).broadcast_to([128, NCH, 32]), op=ALU.subtract),
                     reads=[K(2), "edge"], writes=[K(3)])
                S.op("act", lambda e: e.activation(out=s3[:, :], in_=s3[:, :], func=AF.Exp, scale=-1.0), reads=[K(3)], writes=[K(3)])
                S.op("dve", lambda e: e.tensor_tensor(out=khat, in0=s0[:, :], in1=s3[:, :], op=ALU.mult), reads=[K(0), K(3)], writes=[kkhat])
                self.transpose_tiles(khat, kkhat, c3(khtok), kkhtok, 32, 32)
                midi = 15 if d == 0 else 16
                S.op("dve", (lambda midi=midi: lambda e: e.tensor_copy(out=self.edge[:, :], in_=b3(s2)[:, :, midi]))(), reads=[K(2), ("dcol", d), kkhat], writes=["edge"])
                S.op("dve", lambda e: e.tensor_tensor(out=b3(s3), in0=b3(s2), in1=self.edge[:, :].unsqueeze(2).broadcast_to([128, NCH, 32]), op=ALU.subtract),
                     reads=[K(2), "edge"], writes=[K(3)])
                S.op("act", lambda e: e.activation(out=s2[:, :], in_=s3[:, :], func=AF.Exp), reads=[K(3)], writes=[K(2)])
                S.op("dve", lambda e: e.tensor_tensor(out=khat, in0=T4, in1=s2[:, :], op=ALU.mult), reads=[kT4, K(2)], writes=[kkhat])
                S.op("act", lambda e: e.activation(out=s2[:, :], in_=s3[:, :], func=AF.Exp, scale=-1.0), reads=[K(3)], writes=[K(2)])
                S.op("dve", (lambda d=d: lambda e: e.tensor_tensor(out=kX[d][0], in0=s0[:, :], in1=s2[:, :], op=ALU.mult))(), reads=[K(0), K(2)], writes=[kX[d][1]])
                P3 = PX[d][0].rearrange("p (c t) -> p c t", t=32)
                S.op("pool", (lambda d=d: lambda e: e.memset(PX[d][0][0:32, :], 0.0))(), writes=[PX[d][1]])
                msk = self.mkb[0:32, d * 32:(d + 1) * 32].bitcast(mybir.dt.uint16)
                for c0 in range(0, NCH, 16):
                    n = min(16, NCH - c0)
                    bk = self.banks(1)[0]
                    for j in range(n):
                        c = c0 + j
                        S.op("pe", (lambda j=j, c=c, d=d, bk=bk: lambda e: e.matmul(self.ps[bk][0:32, j * 32:(j + 1) * 32], lhsT=kX[d][0][:, c * 32:(c + 1) * 32], rhs=khat[:, c * 32:(c + 1) * 32], start=True, stop=True))(),
                             reads=[kX[d][1], kkhat], writes=[("ps", bk)], signal=(j == n - 1))
                    for j in range(n):
                        c = c0 + j
                        S.op("dve", (lambda c=c, j=j, bk=bk, P3=P3, msk=msk: lambda e: e.copy_predicated(out=P3[0:32, c, :], mask=msk, data=self.ps[bk][0:32, j * 32:(j + 1) * 32]))(),
                             reads=[("ps", bk), "mkb"], writes=[PX[d][1]])
                Sb3 = c3(Sbf[d][0])
                kh3, vt3 = c3(khtok), c3(vt32)
                for pc, (cs, ce) in enumerate([(0, 32), (32, 36)]):
                    order = list(range(cs, ce)) if d == 0 else list(range(ce - 1, cs - 1, -1))
                    state = Sin4[:, d * 2 + pc, h, :]
                    skey = kSin
                    bank_of = {}
                    for n_, c in enumerate(order):
                        if n_ % 4 == 0 and n_ + 1 < len(order):
                            bk = self.banks(1)[0]
                            grp = order[n_:n_ + 4]
                            for j, cc in enumerate(grp):
                                bank_of[cc] = (bk, j)
                                S.op("pe", (lambda j=j, cc=cc, bk=bk, kh3=kh3, vt3=vt3: lambda e: e.matmul(self.ps[bk][:, j * 128:(j + 1) * 128], lhsT=kh3[0:32, cc, :], rhs=vt3[0:32, cc, :], start=True, stop=True))(),
                                     reads=[kkhtok, kvt32], writes=[("ps", bk)], signal=(j == len(grp) - 1))
                        S.op("act", (lambda c=c, state=state, Sb3=Sb3: lambda e: e.activation(out=Sb3[:, c, :], in_=state, func=AF.Copy))(), reads=[skey], writes=[Sbf[d][1]])
                        if n_ + 1 < len(order):
                            bk, j = bank_of[c]
                            nxt = Spp[:, (n_ % 2) * 128:(n_ % 2 + 1) * 128]
                            S.op("dve", (lambda c=c, d=d, state=state, nxt=nxt, bk=bk, j=j: lambda e: e.scalar_tensor_tensor(
                                out=nxt, in0=state, scalar=self.dcol[:, d, c:c + 1], in1=self.ps[bk][:, j * 128:(j + 1) * 128], op0=ALU.mult, op1=ALU.add))(),
                                reads=[skey, ("dcol", d), ("ps", bk)], writes=[kSpp])
                            state, skey = nxt, kSpp
            bks = self.banks(3)
            vt3 = c3(vt32)
            for c in range(NCH):
                bk = bks[c // 12]
                col = (c % 12) * 32
                ops = []
                for d in range(2):
                    ops.append((vt3[0:32, c, :], PX[d][0][0:32, c * 32:(c + 1) * 32], [kvt32, PX[d][1]]))
                    ops.append((c3(Sbf[d][0])[:, c, :], qX[d][0][:, c * 32:(c + 1) * 32], [Sbf[d][1], qX[d][1]]))
                for n_, (lt, rr, rkeys) in enumerate(ops):
                    S.op("pe", (lambda lt=lt, rr=rr, bk=bk, col=col, n_=n_: lambda e: e.matmul(self.ps[bk][:, col:col + 32], lhsT=lt, rhs=rr, start=(n_ == 0), stop=(n_ == 3)))(),
                         reads=rkeys, writes=[("ps", bk)], signal=(n_ == 3 and (c % 12 == 11)))
            for ti, (t0, t1) in enumerate(TBS):
                S.op("act", (lambda ti=ti, t0=t0, t1=t1, bks=bks: lambda e: e.activation(out=s0[:, t0:t1], in_=self.ps[bks[ti]][:, 0:t1 - t0], func=AF.Copy))(),
                     reads=[("ps", bks[ti])], writes=[K(0)])
            self.colstats(lambda fc: s0[:, :], lambda fc: [K(0)], 1, 128, self.rstd, "rstd")
            S.op("dve", (lambda h=h: lambda e: e.scalar_tensor_tensor(out=s1[:, :], in0=s0[:, :], scalar=self.vecT[:, V_HG + l * 8 + h:V_HG + l * 8 + h + 1], in1=self.rstd[:, :], op0=ALU.mult, op1=ALU.mult))(),
                 reads=[K(0), "vecT", "rstd"], writes=[K(1)])
            S.op("pool", (lambda h=h: lambda e: e.tensor_tensor(out=yb[:, h, :], in0=s1[:, :], in1=T5, op=ALU.mult))(), reads=[K(1), kT5], writes=[("yb", h)])
        return p2keys + [kSin]

    def mixer(self, l):
        S = self.S
        G = self.G
        w_in = ("w_in", l)
        self.prologue(l, 3, 4, 2)
        ya = G[:, 0:9216].rearrange("p (c t) -> p c t", t=NT)
        yb = G[:, 9216:18432].rearrange("p (c t) -> p c t", t=NT)
        v32 = G[:, 18432:36864].bitcast(F32).rearrange("p (c t) -> p c t", t=NT)
        vn = G[:, 36864:46080].rearrange("p (c t) -> p c t", t=NT)
        wsT = G[:, 46080:47104].rearrange("p (g t) -> p g t", t=128)
        bsb = G[:, 47104:49152].bitcast(F32)
        vtok = G[:, 18432:27648].rearrange("p (n d) -> p n d", d=1024)
        mm = G[:, 18432:36864].rearrange("p (c t) -> p c t", t=NT)
        k8 = lambda n: [(n, i) for i in range(8)]
        xkeys = [("x", fc) for fc in range(FC)]
        hk = lambda k: [("h", k)]
        hf = lambda k: self.h[:, k, :]
        if self.stages.get("hgrn", True):
            hkeys = self.hgrn(l, xkeys, yb)
            S.retire(hkeys, k8("ya") + k8("v") + k8("vn") + ["wsT", "bsb"])
        else:
            S.retire(xkeys, k8("ya") + k8("yb") + k8("v") + k8("vn") + ["wsT", "bsb"])
            for i in range(8):
                S.op("pool", (lambda i=i: lambda e: e.memset(yb[:, i, :], 0.0))(), writes=[("yb", i)])
        for fc in range(8):
            def ev_u(bl, fc=fc):
                for (p, t0, t1, bkey) in bl:
                    S.op("act", (lambda p=p, o=ya[:, fc, t0:t1]: lambda e: e.activation(out=o, in_=p, func=AF.Gelu))(), reads=[bkey], writes=[("ya", fc)])
            self.project([(w_in, 0, 16, OFF_U + fc * 128)], hf, hk, ev_u)
        for fc in range(8):
            def ev_v(bl, fc=fc):
                for (p, t0, t1, bkey) in bl:
                    S.op("act", (lambda p=p, o=v32[:, fc, t0:t1]: lambda e: e.activation(out=o, in_=p, func=AF.Gelu))(), reads=[bkey], writes=[("v", fc)])
            self.project([(w_in, 0, 16, OFF_V + fc * 128)], hf, hk, ev_v)
        bks = self.banks(3)
        for fc in range(8):
            sq = self.sqb[fc % 2]
            sk = ("sqb", fc % 2)
            S.op("act", (lambda fc=fc, sq=sq: lambda e: e.activation(out=sq[:, :], in_=v32[:, fc, :], func=AF.Copy))(), reads=[("v", fc)], writes=[sk])
            for ti, (t0, t1) in enumerate(TBS):
                S.op("pe", (lambda o=self.ps[bks[ti]][:, 0:t1 - t0], rr=sq[:, t0:t1], st=(fc == 0), sp=(fc == 7):
                            lambda e: e.matmul(o, lhsT=self.onesb[:, :], rhs=rr, start=st, stop=sp))(),
                     reads=[sk, "onesb"], writes=[("ps", bks[ti])], signal=(ti == 2))
        mean = self.scr[0]
        for ti, (t0, t1) in enumerate(TBS):
            S.op("act", (lambda o=mean[:, t0:t1], i=self.ps[bks[ti]][:, 0:t1 - t0]: lambda e: e.activation(out=o, in_=i, func=AF.Identity, scale=1.0 / 1024, bias=0.0))(),
                 reads=[("ps", bks[ti])], writes=[("scr", 0)])
        for fc in range(8):
            S.op("dve", (lambda fc=fc: lambda e: e.tensor_tensor(out=v32[:, fc, :], in0=v32[:, fc, :], in1=mean[:, :], op=ALU.subtract))(),
                 reads=[("v", fc), ("scr", 0)], writes=[("v", fc)])
        self.colstats(lambda fc: v32[:, fc, :], lambda fc: [("v", fc)], 8, 1024, self.rstd, "rstd")
        for fc in range(8):
            S.op("dve", (lambda fc=fc: lambda e: e.scalar_tensor_tensor(out=vn[:, fc, :], in0=v32[:, fc, :], scalar=self.vecT[:, V_CG + l * 8 + fc:V_CG + l * 8 + fc + 1], in1=self.rstd[:, :], op0=ALU.mult, op1=ALU.mult))(),
                 reads=[("v", fc), "vecT", "rstd"], writes=[("vn", fc)])
        vtk = [("vtok", tt) for tt in range(9)]
        S.retire(k8("v"), vtk)
        for tt in range(9):
            bk = self.banks(1)[0]
            psb = self.ps[bk][:, :].bitcast(BF16)
            for g in range(8):
                S.op("pe", (lambda g=g, tt=tt, psb=psb: lambda e: e.transpose(out=psb[:, g * 128:(g + 1) * 128], in_=vn[:, g, tt * 128:(tt + 1) * 128], identity=self.identb[:, :]))(),
                     reads=[("vn", g), "identb"], writes=[("ps", bk)], signal=(g == 7))
            if tt % 2 == 0:
                S.op("dve", (lambda tt=tt, psb=psb: lambda e: e.tensor_copy(out=vtok[:, tt, :], in_=psb[:, 0:1024]))(), reads=[("ps", bk)], writes=[("vtok", tt)])
            else:
                S.op("act", (lambda tt=tt, psb=psb: lambda e: e.activation(out=vtok[:, tt, :], in_=psb[:, 0:1024], func=AF.Copy))(), reads=[("ps", bk)], writes=[("vtok", tt)])
        wstg = self.scr[1][:, 0:1024].rearrange("p (g s) -> p g s", s=128)
        S.dma("sp", lambda e: e.dma_start(out=wstg, in_=self.w_sp[l].rearrange("g t s -> t g s")), writes=[("scr", 1)])
        S.dma("sp", lambda e: e.dma_start(out=bsb, in_=self.b_sp[l].partition_broadcast(128)), writes=["bsb"])
        b2 = self.banks(2)
        for g in range(8):
            bk = b2[g // 4]
            S.op("pe", (lambda g=g, bk=bk: lambda e: e.transpose(out=self.ps[bk][:, (g % 4) * 128:(g % 4 + 1) * 128], in_=wstg[:, g, :], identity=self.cst[:, 0:128]))(),
                 reads=[("scr", 1), "cst"], writes=[("ps", bk)], signal=(g % 4 == 3))
        for q in range(2):
            S.op("dve", (lambda q=q: lambda e: e.tensor_copy(out=wsT[:, q * 4:(q + 1) * 4, :], in_=self.ps[b2[q]][:, :].rearrange("p (g t) -> p g t", t=128)))(),
                 reads=[("ps", b2[q])], writes=["wsT"])
        for g in range(8):
            bks = self.banks(3)
            for tt in range(9):
                S.op("pe", (lambda g=g, tt=tt, bk=bks[tt // 3]: lambda e: e.matmul(self.ps[bk][:, (tt % 3) * 128:(tt % 3 + 1) * 128], lhsT=vtok[:, tt, g * 128:(g + 1) * 128], rhs=wsT[:, g, :], start=True, stop=True))(),
                     reads=[("vtok", tt), "wsT"], writes=[("ps", bks[tt // 3])], signal=(tt % 3 == 2))
            sv = self.scr[2]
            for ti, (t0, t1) in enumerate(TBS):
                S.op("dve", (lambda g=g, bk=bks[ti], t0=t0, t1=t1: lambda e: e.tensor_tensor(
                    out=sv[:, t0:t1].rearrange("p (n t) -> p n t", t=128), in0=self.ps[bk][:, 0:384].rearrange("p (n t) -> p n t", t=128),
                    in1=bsb[:, g * 128:(g + 1) * 128].unsqueeze(1).broadcast_to([128, 3, 128]), op=ALU.add))(),
                    reads=[("ps", bks[ti]), "bsb"], writes=[("scr", 2)])
            S.op("pool", (lambda g=g: lambda e: e.tensor_tensor(out=ya[:, g, :], in0=ya[:, g, :], in1=sv[:, :], op=ALU.mult))(),
                 reads=[("scr", 2), ("ya", g)], writes=[("ya", g)])
        mk = [("mm", j) for j in range(FC)]
        S.retire(vtk + k8("vn") + ["wsT", "bsb"], mk)
        yaf, yak = (lambda k: ya[:, k, :]), (lambda k: [("ya", k)])
        ybf, ybk = (lambda k: yb[:, k, :]), (lambda k: [("yb", k)])
        for jb in range(FC):
            def ev_sig(dst, dk):
                def ev(bl):
                    for (p, t0, t1, bkey) in bl:
                        S.op("act", (lambda p=p, o=dst[:, t0:t1]: lambda e: e.activation(out=o, in_=p, func=AF.Sigmoid))(), reads=[bkey], writes=[dk])
                return ev

            def ev_mul(dst, dk, src, skk):
                def ev(bl):
                    for (p, t0, t1, bkey) in bl:
                        S.op("dve", (lambda p=p, o=dst[:, t0:t1], i=src[:, t0:t1]: lambda e: e.tensor_tensor(out=o, in0=i, in1=p, op=ALU.mult))(), reads=[bkey, skk], writes=[dk])
                return ev
            self.project([(w_in, 0, 16, OFF_GA + jb * 128)], hf, hk, ev_sig(self.scr[2], ("scr", 2)))
            self.project([(("w_ua", l), 0, 8, jb * 128)], yaf, yak, ev_mul(self.scr[3], ("scr", 3), self.scr[2], ("scr", 2)))
            self.project([(w_in, 0, 16, OFF_GB + jb * 128)], hf, hk, ev_sig(self.scr[0], ("scr", 0)))
            self.project([(("w_ub", l), 0, 8, jb * 128)], ybf, ybk, ev_mul(self.scr[1], ("scr", 1), self.scr[0], ("scr", 0)))
            S.op("pool", (lambda jb=jb: lambda e: e.tensor_tensor(out=mm[:, jb, :], in0=self.scr[3][:, :], in1=self.scr[1][:, :], op=ALU.add))(),
                 reads=[("scr", 3), ("scr", 1)], writes=[("mm", jb)])
        ykeys = [("y", fc) for fc in range(FC)]
        S.retire([("h", fc) for fc in range(FC)], ykeys)
        for jo in range(FC):
            def ev_y(bl, jo=jo):
                for n, (p, t0, t1, bkey) in enumerate(bl):
                    o = self.h[:, jo, t0:t1]
                    if n % 2 == 0:
                        S.op("act", (lambda p=p, o=o: lambda e: e.activation(out=o, in_=p, func=AF.Copy))(), reads=[bkey], writes=[("y", jo)])
                    else:
                        S.op("dve", (lambda p=p, o=o: lambda e: e.tensor_copy(out=o, in_=p))(), reads=[bkey], writes=[("y", jo)])
            self.project([(("w_out", l), 0, 16, jo * 128)], lambda k: mm[:, k, :], lambda k: [("mm", k)], ev_y)
        S.retire(mk + k8("ya") + k8("yb"), xkeys)
        self.epilogue(l, 5, 3, 1.0)
        S.retire(ykeys, [("h", fc) for fc in range(FC)])


def host_prep(inputs):
    f = lambda a: np.ascontiguousarray(np.asarray(a, dtype=np.float32))
    x, c, ctx, c_ctx = f(inputs["x"]), f(inputs["c"]), f(inputs["ctx"]), f(inputs["c_ctx"])
    consts = np.zeros((128, 192), np.float32)
    consts[:, 0:128] = np.eye(128, dtype=np.float32)
    s = np.arange(32)
    consts[0:32, 128:160] = (s[:, None] <= s[None, :]).astype(np.float32)
    consts[0:32, 160:192] = (s[:, None] >= s[None, :]).astype(np.float32)
    shared = {
        "consts": consts,
        "w_mod": f(inputs["w_mod"]),
        "ffn1_w_gu": f(inputs["ffn1_w_gu"]), "ffn1_w_down": f(inputs["ffn1_w_down"]),
        "ffn2_w_gu": f(inputs["ffn2_w_gu"]), "ffn2_w_down": f(inputs["ffn2_w_down"]),
        "w_in": f(inputs["w_in"]), "w_spatial": f(inputs["w_spatial"]),
        "b_spatial": f(inputs["b_spatial"]).reshape(DEPTH, 1, 1024),
        "w_up_a": f(inputs["w_up_a"]), "w_up_b": f(inputs["w_up_b"]), "w_out": f(inputs["w_out"]),
    }
    in_maps = []
    for r in range(NCORES):
        b, hf = r // 2, r % 2
        vec = np.zeros((V_ROWS, 128), np.float32)
        vec[V_BMOD:V_BMOD + 288] = f(inputs["b_mod"]).reshape(288, 128)
        vec[V_NG:V_NG + 192] = f(inputs["norm_g"]).reshape(192, 128)
        vec[V_CG:V_CG + 16] = f(inputs["chunk_norm_g"]).reshape(16, 128)
        vec[V_HG:V_HG + 16] = f(inputs["hgrn_norm_g"]).reshape(16, 128)
        vec[V_LB:V_LB + 32] = f(inputs["lb_logits"]).reshape(32, 128)
        vec[V_C:V_C + 16] = c[b].reshape(16, 128)
        vec[V_C + 16:V_C + 32] = c_ctx.reshape(16, 128)
        flags = np.zeros((128, 2), np.float32)
        flags[:, 0] = 1.0 if hf == 0 else 0.0
        flags[:, 1] = 0.0 if hf == 0 else 1.0
        m = dict(shared)
        m["x_lat"] = np.ascontiguousarray(x[b, hf * NL:(hf + 1) * NL])
        m["x_ctx"] = np.ascontiguousarray(ctx[b, hf * NCX:(hf + 1) * NCX])
        m["vecs"] = vec
        m["flags"] = flags
        in_maps.append(m)
    return in_maps


def run(inputs, stages=None):
    dry = Builder(stages or {})
    dry.build()
    bld = Builder(stages or {}, plan=list(dry.record))
    nc = bld.build()
    assert bld.req_idx == len(bld.plan) == bld.issued
    in_maps = host_prep(inputs)
    res = run_bass_kernel_spmd(nc, in_maps, core_ids=list(range(NCORES)))
    out = np.zeros((4, 2048, D), np.float32)
    for r in range(NCORES):
        b, hf = r // 2, r % 2
        out[b, hf * NL:(hf + 1) * NL] = res.results[r]["out"]
    return out


def kernel(**inputs):
    return run(inputs)
```
